# Optimizing a Trainium2 kernel written in Bass

```python
import jax, jax.numpy as jnp
from jax import lax
import numpy as np

D_MODEL = 1024
BATCH = 8
SEQ = 2048
DEPTH = 2

HEAD_DIM = 64
NSA_HEADS = 8
NSA_KV_GROUPS = 2
NSA_REP = NSA_HEADS // NSA_KV_GROUPS
CMP_LEN = 32
CMP_STRIDE = 16
CMP_HIDDEN = 128
SEL_BLOCK = 64
SEL_TOPK = 16
WINDOW = 512
MOBA_HEADS = 8
MOBA_BLOCK = 256
MOBA_TOPK = 3
Q_BLOCK = 128
D_FF = 2816
ROPE_THETA = 10000.0
EPS = 1e-6
NEG = -1e30
TINY = 1e-30
FORCE_BONUS = 1e4
NSA_WIDTH = NSA_HEADS * HEAD_DIM
NSA_KV_WIDTH = NSA_KV_GROUPS * HEAD_DIM
MOBA_WIDTH = MOBA_HEADS * HEAD_DIM
SPLIT_SIZES = (NSA_WIDTH, 3 * NSA_HEADS, NSA_KV_WIDTH, NSA_KV_WIDTH, NSA_KV_WIDTH, NSA_KV_WIDTH, NSA_KV_WIDTH, NSA_KV_WIDTH, MOBA_WIDTH, MOBA_WIDTH, MOBA_WIDTH, D_MODEL, D_MODEL)
IN_COLS = NSA_WIDTH + 3 * NSA_HEADS + 6 * NSA_KV_WIDTH + 3 * MOBA_WIDTH + 2 * D_MODEL

kernel_name = "hybrid_nsa_moba_macaron"


def _rms(x, g):
    xf = x.astype(jnp.float32)
    y = xf * lax.rsqrt(jnp.mean(xf * xf, axis=-1, keepdims=True) + EPS)
    return (y * g.astype(jnp.float32)).astype(x.dtype)


def _swiglu(h, wg, wu, wd):
    return (jax.nn.silu(h @ wg) * (h @ wu)) @ wd


def _rope_tables(pos):
    inv = ROPE_THETA ** (-jnp.arange(0, HEAD_DIM, 2, dtype=jnp.float32) / HEAD_DIM)
    ang = pos.astype(jnp.float32)[:, None] * inv[None, :]
    return jnp.cos(ang), jnp.sin(ang)


def _rope(x, cos, sin):
    x1, x2 = jnp.split(x.astype(jnp.float32), 2, axis=-1)
    return jnp.concatenate([x1 * cos - x2 * sin, x2 * cos + x1 * sin], axis=-1).astype(x.dtype)


def _masked_softmax(s, mask, axis):
    s = jnp.where(mask, s.astype(jnp.float32), NEG)
    m = jnp.max(s, axis=axis, keepdims=True)
    p = jnp.exp(s - m) * mask
    return p / jnp.maximum(jnp.sum(p, axis=axis, keepdims=True), TINY)


def _heads(z, n):
    b, s, _ = z.shape
    return z.reshape(b, s, n, HEAD_DIM).transpose(0, 2, 1, 3)


def _split_cols(z):
    idx = []
    acc = 0
    for w in SPLIT_SIZES[:-1]:
        acc += w
        idx.append(acc)
    return jnp.split(z, idx, axis=-1)


def _chunk_queries(a):
    b = a.shape[0]
    s = a.shape[-2]
    nq = s // Q_BLOCK
    mid = a.shape[1:-2]
    a = a.reshape((b,) + mid + (nq, Q_BLOCK, a.shape[-1]))
    nd = a.ndim
    perm = (0, nd - 3) + tuple(range(1, nd - 3)) + (nd - 2, nd - 1)
    return a.transpose(perm).reshape((b * nq,) + mid + (Q_BLOCK, a.shape[-1]))


def _unchunk_queries(a, b):
    nq = a.shape[0] // b
    mid = a.shape[1:-2]
    a = a.reshape((b, nq) + mid + a.shape[-2:])
    nd = a.ndim
    perm = (0,) + tuple(range(2, nd - 2)) + (1, nd - 2, nd - 1)
    return a.transpose(perm).reshape((b,) + mid + (nq * Q_BLOCK, a.shape[-1]))


def _compress(k, pos_emb, w1, w2):
    b, g, s, d = k.shape
    ch = k.reshape(b, g, s // CMP_STRIDE, CMP_STRIDE, d)
    blocks = jnp.concatenate([ch[:, :, :-1], ch[:, :, 1:]], axis=3) + pos_emb
    flat = blocks.reshape(b, g, blocks.shape[2], CMP_LEN * d)
    return jax.nn.gelu(flat @ w1) @ w2


def _nsa(q, g, kc, vc, ks, vs, kw, vw, ck_pos, ck_w1, ck_w2, cv_pos, cv_w1, cv_w2):
    B, S, _ = q.shape
    G, R, D = NSA_KV_GROUPS, NSA_REP, HEAD_DIM
    scale = HEAD_DIM ** -0.5
    t = jnp.arange(S)
    cos, sin = _rope_tables(t)
    qh = _rope(_heads(q, NSA_HEADS), cos, sin).reshape(B, G, R, S, D)

    kcb = _compress(_heads(kc, G), ck_pos, ck_w1, ck_w2)
    vcb = _compress(_heads(vc, G), cv_pos, cv_w1, cv_w2)
    nc = kcb.shape[2]
    end_pos = jnp.arange(nc) * CMP_STRIDE + (CMP_LEN - 1)
    cc, sc = _rope_tables(end_pos)
    kcb = _rope(kcb, cc, sc)
    s_cmp = jnp.einsum('bgrtd,bgcd->bgrtc', qh, kcb) * scale
    p_cmp = _masked_softmax(s_cmp, end_pos[None, :] <= t[:, None], -1)
    o_cmp = jnp.einsum('bgrtc,bgcd->bgrtd', p_cmp.astype(vcb.dtype), vcb)

    ns = S // SEL_BLOCK
    ci = jnp.arange(nc)[:, None] * CMP_STRIDE
    sj = jnp.arange(ns)[None, :] * SEL_BLOCK
    overlap = ((ci < sj + SEL_BLOCK) & (ci + CMP_LEN > sj)).astype(jnp.float32)
    imp = jnp.einsum('bgrtc,cj->bgtj', p_cmp, overlap)
    tblk = (t // SEL_BLOCK)[:, None]
    jj = jnp.arange(ns)[None, :]
    forced = (jj == 0) | (jj == tblk) | (jj == tblk - 1)
    imp = jnp.where(jj <= tblk, imp + jnp.where(forced, FORCE_BONUS, 0.0), NEG)
    k_sel = min(SEL_TOPK, ns)
    _, sel_idx = lax.top_k(imp, k_sel)

    ksb = _rope(_heads(ks, G), cos, sin).reshape(B, G, ns, SEL_BLOCK, D)
    vsb = _heads(vs, G).reshape(B, G, ns, SEL_BLOCK, D)
    nq = S // Q_BLOCK
    q_ch = _chunk_queries(qh)
    idx_ch = _chunk_queries(sel_idx)
    b_ids = jnp.repeat(jnp.arange(B), nq)
    n_ids = jnp.tile(jnp.arange(nq), B)

    def sel_body(args):
        qc, ic, bi, ni = args
        kg = jax.vmap(lambda kk, ii: kk[ii])(ksb[bi], ic)
        vg = jax.vmap(lambda vv, ii: vv[ii])(vsb[bi], ic)
        s = jnp.einsum('grqd,gqnkd->grqnk', qc, kg) * scale
        tq = ni * Q_BLOCK + jnp.arange(Q_BLOCK)
        kpos = ic[..., None] * SEL_BLOCK + jnp.arange(SEL_BLOCK)
        mask = (kpos <= tq[None, :, None, None])[:, None]
        p = _masked_softmax(s, mask, (-2, -1))
        return jnp.einsum('grqnk,gqnkd->grqd', p.astype(vg.dtype), vg)

    o_sel = _unchunk_queries(lax.map(sel_body, (q_ch, idx_ch, b_ids, n_ids)), B)

    kwh = _rope(_heads(kw, G), cos, sin)
    vwh = _heads(vw, G)
    pad = ((0, 0), (0, 0), (WINDOW, 0), (0, 0))
    band = jnp.arange(nq)[:, None] * Q_BLOCK + jnp.arange(WINDOW + Q_BLOCK)[None, :]
    kband = jnp.take(jnp.pad(kwh, pad), band, axis=2)
    vband = jnp.take(jnp.pad(vwh, pad), band, axis=2)
    qb = qh.reshape(B, G, R, nq, Q_BLOCK, D)
    s_win = jnp.einsum('bgrnqd,bgnkd->bgrnqk', qb, kband) * scale
    kpos = (band - WINDOW)[:, None, :]
    tq = (jnp.arange(nq)[:, None] * Q_BLOCK + jnp.arange(Q_BLOCK)[None, :])[:, :, None]
    m_win = (kpos >= 0) & (kpos <= tq) & (tq - kpos < WINDOW)
    p_win = _masked_softmax(s_win, m_win, -1)
    o_win = jnp.einsum('bgrnqk,bgnkd->bgrnqd', p_win.astype(vband.dtype), vband).reshape(B, G, R, S, D)

    gate = jax.nn.sigmoid(g).reshape(B, S, G, R, 3).transpose(0, 2, 3, 1, 4)
    o = gate[..., 0:1] * o_cmp + gate[..., 1:2] * o_sel + gate[..., 2:3] * o_win
    return o.transpose(0, 3, 1, 2, 4).reshape(B, S, NSA_WIDTH)


def _moba(q, k, v):
    B, S, _ = q.shape
    H, D = MOBA_HEADS, HEAD_DIM
    scale = HEAD_DIM ** -0.5
    t = jnp.arange(S)
    cos, sin = _rope_tables(t)
    qh = _rope(_heads(q, H), cos, sin)
    kh = _rope(_heads(k, H), cos, sin)
    vh = _heads(v, H)
    nb = -(-S // MOBA_BLOCK)
    sp = nb * MOBA_BLOCK
    kp = jnp.pad(kh, ((0, 0), (0, 0), (0, sp - S), (0, 0)))
    vp = jnp.pad(vh, ((0, 0), (0, 0), (0, sp - S), (0, 0)))
    kblk = kp.reshape(B, H, nb, MOBA_BLOCK, D)
    vblk = vp.reshape(B, H, nb, MOBA_BLOCK, D)
    n_top = min(MOBA_TOPK, nb - 1)
    nq = S // Q_BLOCK
    if n_top > 0:
        kmean = jnp.mean(kblk.astype(jnp.float32), axis=3).astype(kh.dtype)
        gsc = jnp.einsum('bhtd,bhjd->bhtj', qh, kmean)
        past = jnp.arange(nb)[None, :] < (t // MOBA_BLOCK)[:, None]
        gsc = jnp.where(past, gsc.astype(jnp.float32), NEG)
        _, gidx = lax.top_k(gsc, n_top)
        idx_ch = _chunk_queries(gidx)
    else:
        idx_ch = jnp.zeros((B * nq, H, Q_BLOCK, 1), jnp.int32)
    q_ch = _chunk_queries(qh)
    b_ids = jnp.repeat(jnp.arange(B), nq)
    n_ids = jnp.tile(jnp.arange(nq), B)

    def body(args):
        qc, ic, bi, ni = args
        tq = ni * Q_BLOCK + jnp.arange(Q_BLOCK)
        c = (ni * Q_BLOCK) // MOBA_BLOCK
        kown = lax.dynamic_slice_in_dim(kp[bi], c * MOBA_BLOCK, MOBA_BLOCK, axis=1)
        vown = lax.dynamic_slice_in_dim(vp[bi], c * MOBA_BLOCK, MOBA_BLOCK, axis=1)
        s_own = jnp.einsum('hqd,hkd->hqk', qc, kown) * scale
        own_pos = c * MOBA_BLOCK + jnp.arange(MOBA_BLOCK)
        m_own = jnp.broadcast_to((own_pos[None, :] <= tq[:, None])[None], s_own.shape)
        if n_top > 0:
            kg = jax.vmap(lambda kk, ii: kk[ii])(kblk[bi], ic)
            vg = jax.vmap(lambda vv, ii: vv[ii])(vblk[bi], ic)
            s_sel = jnp.einsum('hqd,hqnkd->hqnk', qc, kg) * scale
            m_sel = jnp.broadcast_to((ic < c)[..., None], s_sel.shape)
            s_all = jnp.concatenate([s_sel.reshape(H, Q_BLOCK, -1), s_own], axis=-1)
            m_all = jnp.concatenate([m_sel.reshape(H, Q_BLOCK, -1), m_own], axis=-1)
            p = _masked_softmax(s_all, m_all, -1)
            p_sel = p[..., :n_top * MOBA_BLOCK].reshape(H, Q_BLOCK, n_top, MOBA_BLOCK)
            p_own = p[..., n_top * MOBA_BLOCK:]
            return (jnp.einsum('hqnk,hqnkd->hqd', p_sel.astype(vg.dtype), vg)
                    + jnp.einsum('hqk,hkd->hqd', p_own.astype(vown.dtype), vown))
        p_own = _masked_softmax(s_own, m_own, -1)
        return jnp.einsum('hqk,hkd->hqd', p_own.astype(vown.dtype), vown)

    o = _unchunk_queries(lax.map(body, (q_ch, idx_ch, b_ids, n_ids)), B)
    return o.transpose(0, 2, 1, 3).reshape(B, S, MOBA_WIDTH)


def setup_inputs(seed: int = 0) -> dict:
    key = jax.random.key(seed)
    ks = jax.random.split(key, 24)

    def w(k, shape, fan_in):
        return jax.random.normal(k, shape, jnp.float32) * (fan_in ** -0.5)

    def gain(k, shape):
        return 1.0 + 0.01 * jax.random.normal(k, shape, jnp.float32)

    L, Dm, F = DEPTH, D_MODEL, D_FF
    return {
        "x": jax.random.normal(ks[0], (BATCH, SEQ, Dm), jnp.float32),
        "ffn1_norm": gain(ks[1], (L, Dm)),
        "ffn1_wg": w(ks[2], (L, Dm, F), Dm),
        "ffn1_wu": w(ks[3], (L, Dm, F), Dm),
        "ffn1_wd": w(ks[4], (L, F, Dm), F),
        "mix_norm": gain(ks[5], (L, Dm)),
        "w_in": w(ks[6], (L, Dm, IN_COLS), Dm),
        "cmpk_pos": 0.1 * jax.random.normal(ks[7], (L, CMP_LEN, HEAD_DIM), jnp.float32),
        "cmpk_w1": w(ks[8], (L, CMP_LEN * HEAD_DIM, CMP_HIDDEN), CMP_LEN * HEAD_DIM),
        "cmpk_w2": w(ks[9], (L, CMP_HIDDEN, HEAD_DIM), CMP_HIDDEN),
        "cmpv_pos": 0.1 * jax.random.normal(ks[10], (L, CMP_LEN, HEAD_DIM), jnp.float32),
        "cmpv_w1": w(ks[11], (L, CMP_LEN * HEAD_DIM, CMP_HIDDEN), CMP_LEN * HEAD_DIM),
        "cmpv_w2": w(ks[12], (L, CMP_HIDDEN, HEAD_DIM), CMP_HIDDEN),
        "w_branch_nsa": w(ks[13], (L, NSA_WIDTH, Dm), NSA_WIDTH),
        "w_branch_moba": w(ks[14], (L, MOBA_WIDTH, Dm), MOBA_WIDTH),
        "w_out": w(ks[15], (L, Dm, Dm), Dm),
        "ffn2_norm": gain(ks[16], (L, Dm)),
        "ffn2_wg": w(ks[17], (L, Dm, F), Dm),
        "ffn2_wu": w(ks[18], (L, Dm, F), Dm),
        "ffn2_wd": w(ks[19], (L, F, Dm), F),
        "final_norm": gain(ks[20], (Dm,)),
    }


def reference(x, ffn1_norm, ffn1_wg, ffn1_wu, ffn1_wd, mix_norm, w_in, cmpk_pos, cmpk_w1, cmpk_w2, cmpv_pos, cmpv_w1, cmpv_w2, w_branch_nsa, w_branch_moba, w_out, ffn2_norm, ffn2_wg, ffn2_wu, ffn2_wd, final_norm):
    for l in range(DEPTH):
        x = x + 0.5 * _swiglu(_rms(x, ffn1_norm[l]), ffn1_wg[l], ffn1_wu[l], ffn1_wd[l])
        h = _rms(x, mix_norm[l])
        (q_a, g_a, kc, vc, ks_, vs_, kw, vw, q_b, k_b, v_b, gate_a, gate_b) = _split_cols(h @ w_in[l])
        y_a = _nsa(q_a, g_a, kc, vc, ks_, vs_, kw, vw, cmpk_pos[l], cmpk_w1[l], cmpk_w2[l], cmpv_pos[l], cmpv_w1[l], cmpv_w2[l]) @ w_branch_nsa[l]
        y_b = _moba(q_b, k_b, v_b) @ w_branch_moba[l]
        merged = jax.nn.sigmoid(gate_a) * y_a + jax.nn.sigmoid(gate_b) * y_b
        x = x + merged @ w_out[l]
        x = x + 0.5 * _swiglu(_rms(x, ffn2_norm[l]), ffn2_wg[l], ffn2_wu[l], ffn2_wd[l])
    return _rms(x, final_norm)
```

```python
import numpy as np
from contextlib import ExitStack
import concourse.bass as bass
import concourse.mybir as mybir
from concourse.bass_utils import run_bass_kernel_spmd

F32 = mybir.dt.float32
BF16 = mybir.dt.bfloat16
AF = mybir.ActivationFunctionType
ALU = mybir.AluOpType
AX = mybir.AxisListType

S_LEN = 2048
DM = 1024
NT = 16
DFF = 2816
NFC = 22
IN_COLS = 4888
NEG_BIG = -30000.0
NEG = -1e30
EPS = 1e-6


class Buf:
    __slots__ = ("name", "lw", "rd", "excl")

    def __init__(self, name="", excl=False):
        self.name = name
        self.lw = None
        self.rd = []
        self.excl = excl


def PBuf():
    return Buf("psum", True)


class Track:
    def __init__(self, sem, step, name):
        self.sem = sem
        self.step = step
        self.count = 0
        self.name = name


class Eng:
    def __init__(self, obj, track, name, inorder=False):
        self.obj = obj
        self.track = track
        self.name = name
        self.seen = {}
        self.inorder = inorder


class Sched:
    def __init__(self, nc, stack, n_dma_tracks=16):
        self.nc = nc

        def mk(obj, name, inorder=False):
            sem = stack.enter_context(nc.semaphore("s_" + name))
            return Eng(obj, Track(sem, 1, name), name, inorder)

        self.pe = mk(nc.tensor, "pe", True)
        self.act = mk(nc.scalar, "act")
        self.dve = mk(nc.vector, "dve")
        self.pool = mk(nc.gpsimd, "pool")
        self.sp = mk(nc.sync, "sp")
        self.engs = [self.pe, self.act, self.dve, self.pool, self.sp]
        self.dma_pools = {}
        for e_ in (self.sp, self.pool):
            self.dma_pools[e_.name] = [
                Track(stack.enter_context(nc.semaphore("s_dma_%s%d" % (e_.name, i))), 16, "dma_%s%d" % (e_.name, i))
                for i in range(n_dma_tracks // 2)
            ]
        self.dma_tracks = self.dma_pools["sp"] + self.dma_pools["pool"]
        self.dma_i = {"sp": 0, "pool": 0}
        self.n_inst = 0
        self.n_wait = 0
        self.budget = None

    def _skip(self):
        if self.budget is None:
            return False
        if self.budget <= 0:
            return True
        self.budget -= 1
        return False

    def _wait(self, eng, track, val):
        if eng.inorder and track is eng.track:
            return
        if eng.seen.get(track, 0) >= val:
            return
        eng.obj.wait_ge(track.sem, val)
        eng.seen[track] = val
        self.n_wait += 1

    @staticmethod
    def _split(reads, writes):
        ex = [b for b in reads if b.excl]
        if ex:
            reads = [b for b in reads if not b.excl]
            writes = list(writes) + ex
        return reads, writes

    def _deps(self, eng, reads, writes):
        for b in reads:
            if b.lw is not None:
                self._wait(eng, *b.lw)
        for b in writes:
            if b.lw is not None:
                self._wait(eng, *b.lw)
            for r in b.rd:
                self._wait(eng, *r)

    def _commit(self, track, val, reads, writes):
        for b in reads:
            b.rd.append((track, val))
        for b in writes:
            b.lw = (track, val)
            b.rd = []

    def op(self, eng, fn, reads=(), writes=()):
        if self._skip():
            return None
        reads, writes = self._split(reads, writes)
        self._deps(eng, reads, writes)
        inst = fn(eng.obj)
        t = eng.track
        t.count += 1
        inst.then_inc(t.sem, 1)
        self._commit(t, t.count, reads, writes)
        self.n_inst += 1
        return inst

    def mm(self, out_ap, pairs, reads=(), writes=()):
        if self._skip():
            return None
        eng = self.pe
        reads, writes = self._split(reads, writes)
        self._deps(eng, reads, writes)
        n = len(pairs)
        inst = None
        for i, (l, r) in enumerate(pairs):
            inst = eng.obj.matmul(out_ap, l, r, start=(i == 0), stop=(i == n - 1))
            self.n_inst += 1
        t = eng.track
        t.count += 1
        inst.then_inc(t.sem, 1)
        self._commit(t, t.count, reads, writes)

    def mm1(self, out_ap, l, r, start, stop, reads=(), writes=(), skip=False):
        if self._skip():
            return None
        eng = self.pe
        reads, writes = self._split(reads, writes)
        self._deps(eng, reads, writes)
        if skip:
            inst = eng.obj.matmul(out_ap, l, r, start=start, stop=stop, skip_group_check=True)
        else:
            inst = eng.obj.matmul(out_ap, l, r, start=start, stop=stop)
        t = eng.track
        t.count += 1
        inst.then_inc(t.sem, 1)
        self._commit(t, t.count, reads, writes)
        self.n_inst += 1

    def dma(self, eng, out, in_, reads=(), writes=(), **kw):
        if self._skip():
            return None
        pool_ = self.dma_pools[eng.name]
        tr = pool_[self.dma_i[eng.name] % len(pool_)]
        self.dma_i[eng.name] += 1
        if tr.count:
            self._wait(eng, tr, tr.count * 16)
        self._deps(eng, reads, writes)
        inst = eng.obj.dma_start(out=out, in_=in_, **kw)
        tr.count += 1
        inst.then_inc(tr.sem, 16)
        self._commit(tr, tr.count * 16, reads, writes)
        self.n_inst += 1
        return inst

    def pe_fence(self):
        t = self.pe.track
        if t.count and self.pe.seen.get(t, 0) < t.count:
            self.pe.obj.wait_ge(t.sem, t.count)
            self.pe.seen[t] = t.count
            self.n_wait += 1

    def barrier(self):
        for e in self.engs:
            for o in self.engs:
                if o.track.count:
                    self._wait(e, o.track, o.track.count)
            for tr in self.dma_tracks:
                if tr.count:
                    self._wait(e, tr, tr.count * 16)

    def finish(self, eng=None):
        eng = eng or self.sp
        for o in self.engs:
            if o.track.count and o is not eng:
                self._wait(eng, o.track, o.track.count)
        for tr in self.dma_tracks:
            if tr.count:
                self._wait(eng, tr, tr.count * 16)


def bcast(ap, pos, n):
    l = [list(d) for d in ap.ap]
    l.insert(pos, [0, n])
    return bass.AP(ap.tensor, ap.offset, l)


def host_consts():
    c = {}
    inv = (np.float32(10000.0) ** (-np.arange(0, 64, 2, dtype=np.float32) / np.float32(64))).astype(np.float32)
    pos = np.arange(S_LEN, dtype=np.float32)
    ang = (pos[:, None] * inv[None, :]).astype(np.float32)
    cs = np.stack([np.cos(ang), np.sin(ang)], axis=1).astype(np.float32)
    csf = np.stack([np.concatenate([np.cos(ang), np.cos(ang)], axis=1),
                    np.concatenate([-np.sin(ang), np.sin(ang)], axis=1)], axis=1).astype(np.float32)
    c["rope_cs"] = np.ascontiguousarray(csf.reshape(NT, 128, 2, 64).transpose(1, 0, 2, 3))
    endp = (np.arange(127) * 16 + 31).astype(np.float32)
    angc = (endp[:, None] * inv[None, :]).astype(np.float32)
    rc = np.zeros((128, 2, 64), np.float32)
    rc[:127, 0] = np.concatenate([np.cos(angc), np.cos(angc)], axis=1)
    rc[:127, 1] = np.concatenate([-np.sin(angc), np.sin(angc)], axis=1)
    c["rope_c"] = rc
    t = np.arange(S_LEN)
    cm = np.where((np.arange(128)[None, :] * 16 + 31 <= t[:, None]) & (np.arange(128)[None, :] < 127), 0.0, NEG_BIG)
    c["cmask"] = np.ascontiguousarray(cm.astype(np.float32).reshape(NT, 128, 128).transpose(1, 0, 2))
    tb = (t // 64)[:, None]
    jj = np.arange(32)[None, :]
    forced = (jj == 0) | (jj == tb) | (jj == tb - 1)
    sb_ = np.where(jj <= tb, np.where(forced, 1e4, 0.0), NEG).astype(np.float32)
    c["selbias"] = np.ascontiguousarray(sb_.reshape(NT, 128, 32).transpose(1, 0, 2))
    E = np.zeros((32, 16, 128), np.float32)
    for kc in range(16):
        for k in range(128):
            E[2 * kc + k // 64, kc, k] = 1.0
    c["e_nsa"] = E
    p = np.arange(128)[:, None, None]
    d = np.arange(4)[None, :, None]
    j = np.arange(512)[None, None, :]
    c["caus"] = np.where(p + 128 * d <= j, 0.0, NEG_BIG).astype(np.float32)
    e = (np.arange(8) - 3)[None, :, None]
    dl = j - p + 128 * e
    c["wmask"] = np.where((dl >= 0) & (dl < 512), 0.0, NEG_BIG).astype(np.float32)
    blk = np.arange(8)[None, :]
    c["pastbias"] = np.ascontiguousarray(
        np.where(blk < (t // 256)[:, None], 0.0, NEG).astype(np.float32).reshape(NT, 128, 8).transpose(1, 0, 2))
    c["ownsel"] = np.ascontiguousarray(
        (blk == (t // 256)[:, None]).astype(np.float32).reshape(NT, 128, 8).transpose(1, 0, 2))
    ef = np.zeros((32, S_LEN), np.float32)
    ef[np.arange(S_LEN) // 64, np.arange(S_LEN)] = 1.0
    c["e_full"] = ef
    e8 = np.zeros((4, 32, S_LEN), np.float32)
    for j_ in range(4):
        e8[j_, 8 * j_ + np.arange(S_LEN) // 256, np.arange(S_LEN)] = 1.0
    c["e8"] = e8
    E2 = np.zeros((64, 64, 128), np.float32)
    for i in range(64):
        E2[i, i, :] = 1.0
    c["e_moba"] = E2
    return c


CONST_SHAPES = {
    "rope_cs": (128, 16, 2, 64), "rope_c": (128, 2, 64), "cmask": (128, 16, 128), "selbias": (128, 16, 32),
    "caus": (128, 4, 512), "wmask": (128, 8, 512), "pastbias": (128, 16, 8),
    "ownsel": (128, 16, 8), "e_full": (32, 2048), "e8": (4, 32, 2048),
}

WEIGHT_SHAPES = {
    "ffn1_norm": (2, 1024), "ffn1_wg": (2, 1024, 2816), "ffn1_wu": (2, 1024, 2816), "ffn1_wd": (2, 2816, 1024),
    "mix_norm": (2, 1024), "w_in": (2, 1024, 4888),
    "cmpk_pos": (2, 32, 64), "cmpk_w1": (2, 2048, 128), "cmpk_w2": (2, 128, 64),
    "cmpv_pos": (2, 32, 64), "cmpv_w1": (2, 2048, 128), "cmpv_w2": (2, 128, 64),
    "w_branch_nsa": (2, 512, 1024), "w_branch_moba": (2, 512, 1024), "w_out": (2, 1024, 1024),
    "ffn2_norm": (2, 1024), "ffn2_wg": (2, 1024, 2816), "ffn2_wu": (2, 1024, 2816), "ffn2_wd": (2, 2816, 1024),
    "final_norm": (1, 1024),
}


def build_program(n_layers=2, stages=("ffn1", "mix", "ffn2"), dbg=()):
    nc = bass.Bass("TRN2", target_bir_lowering=False)
    D = {}
    for name, shp in WEIGHT_SHAPES.items():
        D[name] = nc.dram_tensor(name, list(shp), F32, kind="ExternalInput").ap()
    for name, shp in CONST_SHAPES.items():
        D[name] = nc.dram_tensor(name, list(shp), F32, kind="ExternalInput").ap()
    x_d = nc.dram_tensor("x", [S_LEN, DM], F32, kind="ExternalInput").ap()
    out_d = nc.dram_tensor("out", [S_LEN, DM], F32, kind="ExternalOutput").ap()
    DBG = {}

    with ExitStack() as top:
        S = Sched(nc, top)
        cnt = [0]

        def sbuf(st, shape, dt, name=None):
            cnt[0] += 1
            return st.enter_context(nc.sbuf_tensor("%s_%d" % (name or "t", cnt[0]), list(shape), dt))

        def psum(st, shape, dt, name=None):
            cnt[0] += 1
            full = 512 if dt == F32 else 1024
            t = st.enter_context(nc.psum_tensor("%s_%d" % (name or "p", cnt[0]), [128, full], dt))
            shape = list(shape)
            n = 1
            for d_ in shape[1:]:
                n *= d_
            v = t[0:shape[0], 0:n]
            if len(shape) == 3:
                v = v.rearrange("p (a b) -> p a b", a=shape[1])
            elif len(shape) == 4:
                v = v.rearrange("p (a b c) -> p a b c", a=shape[1], b=shape[2])
            return v

        x_sb = sbuf(top, [128, NT, DM], F32, "x")
        xB = [Buf("x%d" % t) for t in range(NT)]
        ident = sbuf(top, [128, 128], BF16, "ident")
        identf = sbuf(top, [128, 128], F32, "identf")
        identB = Buf("ident")
        rope = sbuf(top, [128, NT, 2, 64], F32, "rope")
        ropeB = Buf("rope")

        for t in range(NT):
            S.dma(S.sp if t % 2 == 0 else S.pool, x_sb[:, t, :], x_d[t * 128:(t + 1) * 128, :], writes=[xB[t]])
        S.dma(S.sp, rope[:], D["rope_cs"], writes=[ropeB])
        S.op(S.pool, lambda e: e.memset(identf[:], 1.0), writes=[identB])
        S.op(S.pool, lambda e: e.affine_select(identf[:], identf[:], [[-1, 128]], ALU.is_equal, 0.0,
                                               base=0, channel_multiplier=1), reads=[identB], writes=[identB])
        S.op(S.dve, lambda e: e.tensor_copy(ident[:], identf[:]), reads=[identB], writes=[identB])

        def dbg_out(name, ap, reads):
            if name not in dbg:
                return
            shp = list(ap.shape)
            dt_ = ap.dtype
            d = nc.dram_tensor("dbg_" + name, shp, dt_, kind="ExternalOutput").ap()
            DBG[name] = d
            S.dma(S.sp, d, ap, reads=reads)

        def norm_T(gain_row, hT, hTB):
            with ExitStack() as st:
                gbc = sbuf(st, [128, DM], F32, "gbc")
                ss = sbuf(st, [128, NT], F32, "ss")
                sd = sbuf(st, [128, NT], F32, "sd")
                rstd = sbuf(st, [128, NT], F32, "rstd")
                junk = sbuf(st, [128, DM], BF16, "junk")
                hn = [sbuf(st, [128, DM], BF16, "hn") for _ in range(2)]
                tp = [psum(st, [128, 8, 128], BF16, "tp") for _ in range(2)]
                gB, ss0B, junkB = Buf(), Buf(), Buf()
                ssB = [Buf() for _ in range(NT)]
                hnB = [Buf(), Buf()]
                tpB = [PBuf(), PBuf()]
                S.dma(S.sp, gbc[:], gain_row.partition_broadcast(128), writes=[gB])
                S.op(S.dve, lambda e: e.memset(ss[:], 0.0), writes=ssB)

                def stats(t):
                    S.op(S.act, lambda e: e.activation(junk[:], x_sb[:, t, :], AF.Square, accum_out=ss[:, t:t + 1]),
                         reads=[xB[t]], writes=[junkB, ssB[t]])
                    S.op(S.act, lambda e: e.activation(sd[:, t:t + 1], ss[:, t:t + 1], AF.Sqrt, bias=EPS_AP[:, 0:1], scale=1.0 / DM),
                         reads=[ssB[t], epsB], writes=[ssB[t]])
                    S.op(S.dve, lambda e: e.reciprocal(rstd[:, t:t + 1], sd[:, t:t + 1]), reads=[ssB[t]], writes=[ssB[t]])
                stats(0)
                stats(1)
                for t in range(NT):
                    i = t % 2
                    if t + 2 < NT:
                        stats(t + 2)
                    S.op(S.dve, lambda e: e.scalar_tensor_tensor(hn[i][:], x_sb[:, t, :], rstd[:, t:t + 1], gbc[:],
                                                                 ALU.mult, ALU.mult),
                         reads=[xB[t], ssB[t], gB], writes=[hnB[i]])
                    for c in range(8):
                        S.op(S.pe, lambda e: e.transpose(tp[i][:, c, :], hn[i][:, c * 128:(c + 1) * 128], ident[:]),
                             reads=[hnB[i], identB], writes=[tpB[i]])
                    S.op(S.act, lambda e: e.activation(hT[:, :, t * 128:(t + 1) * 128], tp[i][:], AF.Copy),
                         reads=[tpB[i]], writes=[hTB[t]])
                S.barrier()

        def make_tile_norm(st, gain_row, shared=None):
            gbc = sbuf(st, [128, DM], F32, "gbc")
            hn = [sbuf(st, [128, DM], BF16, "hn") for _ in range(2)]
            hTt = [sbuf(st, [128, 8, 128], BF16, "hTt") for _ in range(2)]
            tph = psum(st, [128, 8, 128], BF16, "tph")
            gB, tphB = Buf(), PBuf()
            hnB = [Buf(), Buf()]
            hTtB = [Buf(), Buf()]
            S.dma(S.sp, gbc[:], gain_row.partition_broadcast(128), writes=[gB])
            have = shared is not None and shared.get("done")
            if shared is None:
                shared = {}
            if not have:
                if "rstd" not in shared:
                    shared["rstd"] = sbuf(st, [128, NT], F32, "rstd")
                    shared["B"] = [Buf() for _ in range(NT)]
                ss = sbuf(st, [128, NT], F32, "ss")
                sd = sbuf(st, [128, NT], F32, "sd")
                junk = sbuf(st, [128, DM], BF16, "junk")
                junkB = Buf()
                S.op(S.dve, lambda e: e.memset(ss[:], 0.0), writes=shared["B"])
            rstd = shared["rstd"]
            ssB = shared["B"]
            started = set()

            def stats(t):
                if have or t in started or t >= NT:
                    return
                started.add(t)
                S.op(S.act, lambda e: e.activation(junk[:], x_sb[:, t, :], AF.Square, accum_out=ss[:, t:t + 1]),
                     reads=[xB[t]], writes=[junkB, ssB[t]])
                S.op(S.act, lambda e: e.activation(sd[:, t:t + 1], ss[:, t:t + 1], AF.Sqrt, bias=EPS_AP[:, 0:1], scale=1.0 / DM),
                     reads=[ssB[t], epsB], writes=[ssB[t]])
                S.op(S.dve, lambda e: e.reciprocal(rstd[:, t:t + 1], sd[:, t:t + 1]), reads=[ssB[t]], writes=[ssB[t]])
            stats(0)
            stats(1)
            shared["done"] = True

            def tile(t):
                i = t % 2
                stats(t)
                stats(t + 1)
                stats(t + 2)
                S.op(S.dve, lambda e: e.scalar_tensor_tensor(hn[i][:], x_sb[:, t, :], rstd[:, t:t + 1], gbc[:],
                                                             ALU.mult, ALU.mult),
                     reads=[xB[t], ssB[t], gB], writes=[hnB[i]])
                for c in range(8):
                    S.op(S.pe, lambda e: e.transpose(tph[:, c, :], hn[i][:, c * 128:(c + 1) * 128], ident[:]),
                         reads=[hnB[i], identB], writes=[tphB])
                S.op(S.act, lambda e: e.activation(hTt[i][:], tph[:], AF.Copy), reads=[tphB], writes=[hTtB[i]])
                return hTt[i], hTtB[i]
            return tile

        def ffn(l, gain, wg, wu, wd, passes=((0, 11), (11, 22))):
            with ExitStack() as st:
                hT = sbuf(st, [128, 8, S_LEN], BF16, "hT")
                hTB = [Buf() for _ in range(NT)]
                wg_v = wg[l].rearrange("(c p) f -> p c f", p=128)
                wu_v = wu[l].rearrange("(c p) f -> p c f", p=128)
                wd_v = wd[l].rearrange("(c p) d -> p c d", p=128)
                wgs = [sbuf(st, [128, 8, 256], BF16, "wg") for _ in range(2)]
                wus = [sbuf(st, [128, 8, 256], BF16, "wu") for _ in range(2)]
                wgB = [Buf(), Buf()]
                wuB = [Buf(), Buf()]
                sg = [sbuf(st, [128, 512], F32, "sg") for _ in range(2)]
                sgB = [Buf(), Buf()]
                pg = [psum(st, [128, 512], F32, "pg") for _ in range(2)]
                pu = [psum(st, [128, 512], F32, "pu") for _ in range(2)]
                pd = [psum(st, [128, 512], F32, "pd") for _ in range(2)]
                pgB = [PBuf(), PBuf()]
                puB = [PBuf(), PBuf()]
                pdB = [PBuf(), PBuf()]
                npmax = max(b - a for a, b in passes)
                actT = sbuf(st, [128, npmax, S_LEN], BF16, "actT")
                wds = sbuf(st, [128, npmax, DM], BF16, "wd")
                actB = [[Buf() for _ in range(4)] for _ in range(npmax)]
                wdB = [Buf() for _ in range(npmax)]
                cnt_ = {"gi": 0, "ei": 0, "di": 0}

                def load_group(f0, fi, gsz):
                    b = cnt_["gi"] % 2
                    cnt_["gi"] += 1
                    c0 = (f0 + fi) * 128
                    S.dma(S.pool, wgs[b][:, :, 0:gsz * 128], wg_v[:, :, c0:c0 + gsz * 128], writes=[wgB[b]])
                    S.dma(S.pool, wus[b][:, :, 0:gsz * 128], wu_v[:, :, c0:c0 + gsz * 128], writes=[wuB[b]])
                    return b

                def gate_up(b, fidx, fo, tb):
                    k = cnt_["ei"] % 2
                    cnt_["ei"] += 1
                    hr = [hTB[4 * tb + q] for q in range(4)]
                    S.mm(pg[k][:], [(wgs[b][:, c, fo * 128:(fo + 1) * 128], hT[:, c, tb * 512:(tb + 1) * 512])
                                    for c in range(8)], reads=[wgB[b]] + hr, writes=[pgB[k]])
                    S.mm(pu[k][:], [(wus[b][:, c, fo * 128:(fo + 1) * 128], hT[:, c, tb * 512:(tb + 1) * 512])
                                    for c in range(8)], reads=[wuB[b]] + hr, writes=[puB[k]])
                    S.op(S.act, lambda e: e.activation(sg[k][:], pg[k][:], AF.Silu), reads=[pgB[k]], writes=[sgB[k]])
                    S.op(S.dve, lambda e: e.tensor_tensor(actT[:, fidx, tb * 512:(tb + 1) * 512], sg[k][:], pu[k][:], ALU.mult),
                         reads=[sgB[k], puB[k]], writes=[actB[fidx][tb]])

                f0_, f1_ = passes[0]
                g0sz = min(2, f1_ - f0_)
                b0 = load_group(f0_, 0, g0sz)
                gbc = sbuf(st, [128, DM], F32, "gbc")
                ss = sbuf(st, [128, NT], F32, "ss")
                sd = sbuf(st, [128, NT], F32, "sd")
                rstd = sbuf(st, [128, NT], F32, "rstd")
                junk = sbuf(st, [128, DM], BF16, "junk")
                hn = [sbuf(st, [128, DM], BF16, "hn") for _ in range(2)]
                tp = [psum(st, [128, 8, 128], BF16, "tp") for _ in range(2)]
                gB, junkB = Buf(), Buf()
                ssB = [Buf() for _ in range(NT)]
                hnB = [Buf(), Buf()]
                tpB = [PBuf(), PBuf()]
                S.dma(S.sp, gbc[:], gain[l:l + 1, :].partition_broadcast(128), writes=[gB])
                S.op(S.dve, lambda e: e.memset(ss[:], 0.0), writes=ssB)

                def stats(t):
                    S.op(S.act, lambda e: e.activation(junk[:], x_sb[:, t, :], AF.Square, accum_out=ss[:, t:t + 1]),
                         reads=[xB[t]], writes=[junkB, ssB[t]])
                    S.op(S.act, lambda e: e.activation(sd[:, t:t + 1], ss[:, t:t + 1], AF.Sqrt, bias=EPS_AP[:, 0:1], scale=1.0 / DM),
                         reads=[ssB[t], epsB], writes=[ssB[t]])
                    S.op(S.dve, lambda e: e.reciprocal(rstd[:, t:t + 1], sd[:, t:t + 1]), reads=[ssB[t]], writes=[ssB[t]])
                stats(0)
                stats(1)
                for t in range(NT):
                    i = t % 2
                    if t + 2 < NT:
                        stats(t + 2)
                    S.op(S.dve, lambda e: e.scalar_tensor_tensor(hn[i][:], x_sb[:, t, :], rstd[:, t:t + 1], gbc[:],
                                                                 ALU.mult, ALU.mult),
                         reads=[xB[t], ssB[t], gB], writes=[hnB[i]])
                    for c in range(8):
                        S.op(S.pe, lambda e: e.transpose(tp[i][:, c, :], hn[i][:, c * 128:(c + 1) * 128], ident[:]),
                             reads=[hnB[i], identB], writes=[tpB[i]])
                    S.op(S.act, lambda e: e.activation(hT[:, :, t * 128:(t + 1) * 128], tp[i][:], AF.Copy),
                         reads=[tpB[i]], writes=[hTB[t]])
                    if t % 4 == 3 and t >= 7:
                        for fo in range(g0sz):
                            gate_up(b0, fo, fo, t // 4 - 1)
                for fo in range(g0sz):
                    gate_up(b0, fo, fo, 3)

                first = True
                for (f0, f1) in passes:
                    nf = f1 - f0
                    wd_issued = False
                    fi = 0
                    while fi < nf:
                        gsz = min(2, nf - fi)
                        if first:
                            b = b0
                        else:
                            b = load_group(f0, fi, gsz)
                        if not wd_issued:
                            wd_issued = True
                            for fj in range(nf):
                                S.dma(S.pool, wds[:, fj, :], wd_v[:, f0 + fj, :], writes=[wdB[fj]])
                        if not first:
                            for fo in range(gsz):
                                for tb in range(4):
                                    gate_up(b, fi + fo, fo, tb)
                        first = False
                        fi += gsz
                    for t in range(NT):
                        for hf in range(2):
                            k = cnt_["di"] % 2
                            cnt_["di"] += 1
                            S.mm(pd[k][:], [(actT[:, fi, t * 128:(t + 1) * 128], wds[:, fi, hf * 512:(hf + 1) * 512])
                                            for fi in range(nf)],
                                 reads=[actB[fi][t // 4] for fi in range(nf)] + wdB[0:nf], writes=[pdB[k]])
                            S.op(S.dve, lambda e: e.scalar_tensor_tensor(
                                x_sb[:, t, hf * 512:(hf + 1) * 512], pd[k][:], 0.5,
                                x_sb[:, t, hf * 512:(hf + 1) * 512], ALU.mult, ALU.add),
                                reads=[pdB[k], xB[t]], writes=[xB[t]])
                S.barrier()

        EPS_AP = sbuf(top, [128, 1], F32, "eps")
        epsB = Buf("eps")
        S.op(S.dve, lambda e: e.memset(EPS_AP[:], EPS), writes=[epsB])

        from_mixer = {}
        for l in range(n_layers):
            if "ffn1" in stages:
                ffn(l, D["ffn1_norm"], D["ffn1_wg"], D["ffn1_wu"], D["ffn1_wd"])
            if "mix" in stages:
                mixer(nc, S, D, l, x_sb, xB, ident, identf, identB, rope, ropeB, make_tile_norm, sbuf, psum, dbg_out, EPS_AP, epsB)
            if "ffn2" in stages:
                ffn(l, D["ffn2_norm"], D["ffn2_wg"], D["ffn2_wu"], D["ffn2_wd"])

        S.budget = None
        with ExitStack() as st:
            gbc = sbuf(st, [128, DM], F32, "gbcf")
            ss = sbuf(st, [128, NT], F32, "ssf")
            sd = sbuf(st, [128, NT], F32, "sdf")
            rstd = sbuf(st, [128, NT], F32, "rstdf")
            junk = sbuf(st, [128, DM], BF16, "junkf")
            yo = [sbuf(st, [128, DM], F32, "yo") for _ in range(2)]
            gB, ssB, junkB = Buf(), Buf(), Buf()
            yB = [Buf(), Buf()]
            S.dma(S.sp, gbc[:], D["final_norm"][0:1, :].partition_broadcast(128), writes=[gB])
            S.op(S.dve, lambda e: e.memset(ss[:], 0.0), writes=[ssB])
            for t in range(NT):
                S.op(S.act, lambda e: e.activation(junk[:], x_sb[:, t, :], AF.Square, accum_out=ss[:, t:t + 1]),
                     reads=[xB[t]], writes=[junkB, ssB])
            S.op(S.act, lambda e: e.activation(sd[:], ss[:], AF.Sqrt, bias=EPS_AP[:, 0:1], scale=1.0 / DM),
                 reads=[ssB, epsB], writes=[ssB])
            S.op(S.dve, lambda e: e.reciprocal(rstd[:], sd[:]), reads=[ssB], writes=[ssB])
            for t in range(NT):
                i = t % 2
                S.op(S.dve, lambda e: e.scalar_tensor_tensor(yo[i][:], x_sb[:, t, :], rstd[:, t:t + 1], gbc[:],
                                                             ALU.mult, ALU.mult),
                     reads=[xB[t], ssB, gB], writes=[yB[i]])
                S.dma(S.sp, out_d[t * 128:(t + 1) * 128, :], yo[i][:], reads=[yB[i]])
            S.finish()
        print("program: n_inst=%d n_wait=%d" % (S.n_inst, S.n_wait))
    return nc, DBG


def rope_apply(S, st_tmp, src, H, cs, dst, reads, writes, dup=1):
    k = st_tmp["i"][0] % 2
    st_tmp["i"][0] += 1
    t1, t2 = st_tmp["t"][k]
    tB = st_tmp["B"][k]
    np_ = src.shape[0]
    a1 = t1[0:np_, 0:H, :]
    a2 = t2[0:np_, 0:H, :]
    S.op(S.dve, lambda e: e.tensor_tensor(a1, src, bcast(cs[0], 1, H), ALU.mult), reads=reads, writes=[tB[0]])
    S.op(S.dve, lambda e: e.tensor_tensor(a2[:, :, 0:32], src[:, :, 32:64], bcast(cs[1][:, 0:32], 1, H), ALU.mult),
         reads=reads, writes=[tB[1]])
    S.op(S.dve, lambda e: e.tensor_tensor(a2[:, :, 32:64], src[:, :, 0:32], bcast(cs[1][:, 32:64], 1, H), ALU.mult),
         reads=reads, writes=[tB[1]])
    if dup > 1:
        b1, b2 = bcast(a1, 2, dup), bcast(a2, 2, dup)
    else:
        b1, b2 = a1, a2
    S.op(S.pool, lambda e: e.tensor_tensor(dst, b1, b2, ALU.add), reads=[tB[0], tB[1]], writes=writes)


STOP = [None]


def run_pipeline(items, la=2, filler=None, per_item=0, drain=True, epi_delay=2):
    n = len(items)
    pending = []
    for i in range(n + la):
        if i < n:
            items[i][0]()
        j = i - la
        if j >= 0:
            items[j][1]()
            while pending and pending[0][0] <= j:
                pending.pop(0)[1]()
            items[j][2]()
            if items[j][3] is not None:
                items[j][3]()
                pending.append((j + epi_delay, items[j][4]))
            if filler is not None:
                for _ in range(per_item):
                    if next(filler, "done") == "done":
                        filler = None
                        break
    for _, fn in pending:
        fn()
    if filler is not None and drain:
        for _ in filler:
            pass


def mixer(nc, S, D, l, x_sb, xB, ident, identf, identB, rope, ropeB, make_tile_norm, sbuf, psum, dbg_out, EPS_AP, epsB):
    win_v = D["w_in"][l].rearrange("(c p) n -> p c n", p=128)
    with ExitStack() as ms:
        nsaT = sbuf(ms, [128, 4, S_LEN], BF16, "nsaT")
        nsaTB = [Buf() for _ in range(NT)]
        mobaTB = [Buf() for _ in range(NT)]
        nstats = {"rstd": sbuf(ms, [128, NT], F32, "rstd_mix"), "B": [Buf() for _ in range(NT)]}

        def make_rt(st_):
            return {"t": [(sbuf(st_, [128, 8, 64], F32, "rt1"), sbuf(st_, [128, 8, 64], F32, "rt2")) for _ in range(2)],
                    "B": [(Buf(), Buf()) for _ in range(2)], "i": [0]}

        with ExitStack() as ns:
            qaT = sbuf(ns, [128, 8, S_LEN], BF16, "qaT")
            kskwT = sbuf(ns, [128, 4, S_LEN], BF16, "kskwT")
            vsw = sbuf(ns, [128, NT, 4, 65], BF16, "vsw")
            ga = sbuf(ns, [128, NT, 24], F32, "ga")
            kcbT = sbuf(ns, [128, 2, 128], BF16, "kcbT")
            vcb = sbuf(ns, [128, 2, 64], BF16, "vcb")
            kcbTB, vcbB = Buf(), Buf()
            selRB = [[Buf() for _ in range(NT)] for _ in range(2)]
            eB_ = Buf()
            S.op(S.pool, lambda e: e.memset(qaT[64:96, :, :], 0.0), writes=[b_ for g_ in selRB for b_ in g_])
            S.op(S.pool, lambda e: e.memset(kskwT[64:96, 2:4, :], 0.0), writes=[eB_])
            S.op(S.pool, lambda e: e.memset(kcbT[:], 0.0), writes=[kcbTB])
            for g_ in range(2):
                S.dma(S.pool, kskwT[64:96, g_, :], D["e_full"], writes=[eB_])
            ks_scope = ns.enter_context(ExitStack())
            kcvcT = sbuf(ks_scope, [128, 2, S_LEN], BF16, "kcvcT")
            qaB = [Buf() for _ in range(NT)]
            kkB = [Buf() for _ in range(NT)]
            kcB = [Buf() for _ in range(NT)]
            vswB = [Buf() for _ in range(NT)]
            gaB = [Buf() for _ in range(NT)]
            S.op(S.pool, lambda e: e.memset(vsw[:], 1.0), writes=vswB)
            with ExitStack() as st:
                rt = make_rt(st)
                tnorm = make_tile_norm(st, D["mix_norm"][l:l + 1, :], nstats)
                wn = sbuf(st, [128, 8, 1304], BF16, "wn")
                wnB = [Buf() for _ in range(3)]
                segs = [(0, 0, 512, 0), (512, 536, 664, 1), (640, 664, 792, 1), (768, 792, 920, 1), (896, 1048, 1176, 1),
                        (1024, 920, 1048, 2), (1152, 1176, 1304, 2), (1280, 512, 536, 2)]
                for (dst, a, b, wb) in segs:
                    S.dma(S.pool, wn[:, :, dst:dst + (b - a)], win_v[:, :, a:b], reads=[], writes=[wnB[wb]])
                pA = [psum(st, [128, 512], F32, "pA") for _ in range(2)]
                pBk = [psum(st, [128, 512], F32, "pB")] * 2
                pC = psum(st, [128, 512], F32, "pC")
                tq = psum(st, [128, 8, 128], BF16, "tq")
                tkk = psum(st, [128, 6, 128], BF16, "tkk")
                tk = tkk[:, 0:4, :]
                tkv = tkk[:, 4:6, :]
                pAB = [PBuf(), PBuf()]
                pBB = [PBuf()] * 2
                pCB, tqB, tkB = PBuf(), PBuf(), PBuf()
                tkvB = tkB
                qr = [sbuf(st, [128, 8, 128], BF16, "qr") for _ in range(2)]
                kr = [sbuf(st, [128, 4, 128], BF16, "kr") for _ in range(2)]
                kv = [sbuf(st, [128, 256], BF16, "kv") for _ in range(2)]
                qrB = [Buf(), Buf()]
                krB = [Buf(), Buf()]
                kvB = [Buf(), Buf()]
                for i_ in range(2):
                    S.op(S.pool, lambda e: e.memset(qr[i_][:], 0.0), writes=[qrB[i_]])
                    S.op(S.pool, lambda e: e.memset(kr[i_][:], 0.0), writes=[krB[i_]])
                hts = {}

                def s_norm(t):
                    hts[t] = tnorm(t)

                def s_pa(t):
                    hTt, hTtB = hts[t]
                    S.mm(pA[t % 2][:], [(hTt[:, c, :], wn[:, c, 0:512]) for c in range(8)], reads=[hTtB, wnB[0]], writes=[pAB[t % 2]])

                def s_pbc(t):
                    hTt, hTtB = hts[t]
                    S.mm(pBk[0][:], [(hTt[:, c, :], wn[:, c, 512:1024]) for c in range(8)], reads=[hTtB, wnB[1]], writes=[pBB[0]])
                    S.mm(pC[:, 0:280], [(hTt[:, c, :], wn[:, c, 1024:1304]) for c in range(8)], reads=[hTtB, wnB[2]], writes=[pCB])

                def s_post(t):
                    i = t % 2
                    tsl = slice(t * 128, (t + 1) * 128)
                    cs = (rope[:, t, 0, :], rope[:, t, 1, :])
                    rope_apply(S, rt, pA[i][:].rearrange("p (h d) -> p h d", h=8), 8, cs, qr[i][:, :, 0:64],
                               reads=[pAB[i], ropeB], writes=[qrB[i]])
                    S.op(S.act, lambda e: e.activation(kv[i][:], pBk[0][:, 0:256], AF.Copy), reads=[pBB[0]], writes=[kvB[i]])
                    rope_apply(S, rt, pBk[0][:, 256:512].rearrange("p (h d) -> p h d", h=4), 4, cs,
                               kr[i][:, :, 0:64], reads=[pBB[0], ropeB], writes=[krB[i]])
                    S.op(S.act, lambda e: e.activation(vsw[:, t, :, 0:64], pC[:, 0:256].rearrange("p (h d) -> p h d", h=4), AF.Copy),
                         reads=[pCB], writes=[vswB[t]])
                    S.op(S.act, lambda e: e.activation(ga[:, t, :], pC[:, 256:280], AF.Sigmoid), reads=[pCB], writes=[gaB[t]])
                    for c in range(8):
                        S.op(S.pe, lambda e: e.transpose(tq[:, c, :], qr[i][:, c, :], ident[:]),
                             reads=[qrB[i], identB], writes=[tqB])
                    S.op(S.act, lambda e: e.activation(qaT[0:64, :, tsl], tq[0:64, :, :], AF.Copy), reads=[tqB], writes=[qaB[t]])
                    for c in range(2):
                        S.op(S.pe, lambda e: e.transpose(tkv[:, c, :], kv[i][:, c * 128:(c + 1) * 128], ident[:]),
                             reads=[kvB[i], identB], writes=[tkvB])
                    S.op(S.act, lambda e: e.activation(kcvcT[:, :, tsl], tkv, AF.Copy), reads=[tkvB], writes=[kcB[t]])
                    for c in range(4):
                        S.op(S.pe, lambda e: e.transpose(tk[:, c, :], kr[i][:, c, :], ident[:]),
                             reads=[krB[i], identB], writes=[tkB])
                    S.op(S.act, lambda e: e.activation(kskwT[0:64, :, tsl], tk[0:64, :, :], AF.Copy), reads=[tkB], writes=[kkB[t]])

                print('SBUF remaining in NSA proj:', nc.sbuf_bytes_remaining)
                s_norm(0)
                s_pa(0)
                s_pbc(0)
                s_norm(1)
                for t in range(NT):
                    if t + 2 < NT:
                        s_norm(t + 2)
                    if t + 1 < NT:
                        s_pa(t + 1)
                    s_post(t)
                    if t + 1 < NT:
                        s_pbc(t + 1)
                S.barrier()
            if STOP[0] == "nsaproj":
                return
            dbg_out("qaT", qaT[:], qaB)
            dbg_out("kskwT", kskwT[:], kkB)
            dbg_out("ga", ga[:], gaB)

            with ExitStack() as st:
                rt = make_rt(st)
                w1 = [sbuf(st, [128, 32, 128], BF16, "w1") for _ in range(2)]
                posT = [sbuf(st, [128, 32], BF16, "posT") for _ in range(2)]
                posj = [sbuf(st, [32, 64], BF16, "posj") for _ in range(2)]
                w2 = [sbuf(st, [128, 64], BF16, "w2") for _ in range(2)]
                ropec = sbuf(st, [128, 2, 64], F32, "ropec")
                wB = Buf()
                for kvi, nm in enumerate(("cmpk", "cmpv")):
                    w1v = D[nm + "_w1"][l].rearrange("(j d) h -> d j h", d=64)
                    for hlf in range(2):
                        S.dma(S.pool, w1[kvi][hlf * 64:(hlf + 1) * 64, :, :], w1v, writes=[wB])
                    S.dma(S.pool, posj[kvi][:], D[nm + "_pos"][l], writes=[wB])
                    S.dma(S.pool, w2[kvi][:], D[nm + "_w2"][l], writes=[wB])
                S.dma(S.sp, ropec[:], D["rope_c"], writes=[wB])
                ph = [psum(st, [128, 128], F32, "ph") for _ in range(2)]
                pb = psum(st, [128, 8], F32, "pb")
                pk = psum(st, [128, 64], F32, "pk")
                tpk = psum(st, [128, 128], BF16, "tpk")
                phB = [PBuf(), PBuf()]
                pbB, pkB, tpkB = PBuf(), PBuf(), PBuf()
                bias = sbuf(st, [128, 2], F32, "bias")
                biasB = Buf()
                hb = sbuf(st, [128, 128], F32, "hb")
                h2 = sbuf(st, [128, 128], F32, "h2")
                uu = sbuf(st, [128, 128], F32, "uu")
                gl = sbuf(st, [128, 128], BF16, "gl")
                kcr = sbuf(st, [128, 2, 64], BF16, "kcr")
                hbB, h2B, uuB, glB, kcrB = Buf(), Buf(), Buf(), Buf(), Buf()
                posTB = Buf()
                for kvi in range(2):
                    S.op(S.pe, lambda e: e.transpose(tpk[0:64, 0:32], posj[kvi][:], ident[0:32, 0:32]), reads=[wB, identB], writes=[tpkB])
                    S.op(S.act, lambda e: e.activation(posT[kvi][0:64, :], tpk[0:64, 0:32], AF.Copy), reads=[tpkB], writes=[posTB])
                for kvi in range(2):
                    S.mm(pb[:, kvi:kvi + 1], [(w1[kvi][0:64, j, :], posT[kvi][0:64, j:j + 1]) for j in range(32)],
                         reads=[wB, posTB], writes=[pbB])
                S.op(S.act, lambda e: e.activation(bias[:], pb[:, 0:2], AF.Copy), reads=[pbB], writes=[biasB])
                it = 0
                for kvi in range(2):
                    for g in range(2):
                        k = it % 2
                        it += 1
                        pairs = []
                        for j in range(32):
                            base = kcvcT[g * 64:(g + 1) * 64, kvi, j:j + 1]
                            rhs = bass.AP(base.tensor, base.offset, [list(base.ap[0]), [16, 127]])
                            pairs.append((w1[kvi][g * 64:(g + 1) * 64, j, :], rhs))
                        S.mm(ph[k][:, 0:127], pairs, reads=[wB] + kcB, writes=[phB[k]])
                        hv, h2v, uv = hb[:, 0:127], h2[:, 0:127], uu[:, 0:127]
                        S.op(S.act, lambda e: e.activation(hv, ph[k][:, 0:127], AF.Identity, bias=bias[:, kvi:kvi + 1]),
                             reads=[phB[k], biasB], writes=[hbB])
                        S.op(S.dve, lambda e: e.tensor_tensor(h2v, hv, hv, ALU.mult), reads=[hbB], writes=[h2B])
                        S.op(S.dve, lambda e: e.tensor_scalar(h2v, h2v, 0.044715, 1.0, ALU.mult, ALU.add), reads=[h2B], writes=[h2B])
                        S.op(S.dve, lambda e: e.tensor_tensor(uv, h2v, hv, ALU.mult), reads=[h2B, hbB], writes=[uuB])
                        S.op(S.act, lambda e: e.activation(uv, uv, AF.Exp, scale=-1.5957691216057308), reads=[uuB], writes=[uuB])
                        S.op(S.dve, lambda e: e.tensor_scalar(uv, uv, 1.0, None, ALU.add), reads=[uuB], writes=[uuB])
                        S.op(S.dve, lambda e: e.reciprocal(uv, uv), reads=[uuB], writes=[uuB])
                        S.op(S.dve, lambda e: e.tensor_tensor(gl[:, 0:127], hv, uv, ALU.mult), reads=[uuB, hbB], writes=[glB])
                        S.mm(pk[0:127, :], [(gl[:, 0:127], w2[kvi][:])], reads=[glB, wB], writes=[pkB])
                        if kvi == 0:
                            rope_apply(S, rt, pk[0:127, :].rearrange("p (h d) -> p h d", h=1), 1,
                                       (ropec[0:127, 0, :], ropec[0:127, 1, :]),
                                       kcr[0:127, :, :].rearrange("p (h u) d -> p h u d", h=1),
                                       reads=[pkB, wB], writes=[kcrB], dup=2)
                            S.op(S.pe, lambda e: e.transpose(tpk[:, 0:127], kcr[0:127, :, :].rearrange("p a d -> p (a d)"), ident[0:127, 0:127]),
                                 reads=[kcrB, identB], writes=[tpkB])
                            S.op(S.act, lambda e: e.activation(kcbT[0:64, g, 0:127], tpk[0:64, 0:127], AF.Copy), reads=[tpkB], writes=[kcbTB])
                        else:
                            S.op(S.act, lambda e: e.activation(vcb[0:127, g, :], pk[0:127, :], AF.Copy), reads=[pkB], writes=[vcbB])
                S.barrier()
            ks_scope.close()
            if STOP[0] == "compress":
                return
            dbg_out("kcbT", kcbT[:], [kcbTB])
            dbg_out("vcb", vcb[:], [vcbB])

            with ExitStack() as st:
                cmask = sbuf(st, [128, NT, 128], F32, "cmask")
                selbias = sbuf(st, [128, NT, 32], F32, "selbias")
                wmask = sbuf(st, [128, 8, 512], BF16, "wmask")
                cB = Buf()
                caus = sbuf(st, [128, 4, 512], BF16, "caus")
                causB = Buf()
                S.dma(S.pool, caus[:], D["caus"], writes=[causB])
                S.dma(S.sp, cmask[:], D["cmask"], writes=[cB])
                S.dma(S.sp, selbias[:], D["selbias"], writes=[cB])
                S.dma(S.pool, wmask[:], D["wmask"], writes=[cB])
                sc = psum(st, [128, 4, 128], F32, "sc")
                tpb = psum(st, [128, 5, 128], BF16, "tpb")
                ocotp = psum(st, [128, 512], F32, "ocotp")
                oc = ocotp[:, 0:256].rearrange("p (a b) -> p a b", a=4)
                otp = ocotp[:, 0:260].rearrange("p (a b) -> p a b", a=4)
                stp = [psum(st, [128, 512], F32, "stp") for _ in range(3)]
                oT = [psum(st, [128, 512], F32, "oT") for _ in range(2)]
                scB, tpbB, ocB = PBuf(), PBuf(), PBuf()
                otpB = ocB
                tpsB = tpbB
                stB = [PBuf(), PBuf(), PBuf()]
                oTB = [PBuf(), PBuf()]
                TL = []
                for g_ in range(2):
                    T = {}
                    T["sm"] = sbuf(st, [128, 4, 127], F32, "sm")
                    T["pb16"] = sbuf(st, [128, 4, 128], BF16, "pb16")
                    T["pT"] = sbuf(st, [128, 4, 128], BF16, "pT")
                    T["pp"] = sbuf(st, [128, 132], F32, "pp")
                    T["st8"] = sbuf(st, [128, 8], F32, "st8")
                    T["imp"] = sbuf(st, [128, 32], F32, "imp")
                    T["imp3"] = sbuf(st, [128, 32], F32, "imp3")
                    T["m8"] = sbuf(st, [128, 16], F32, "m8")
                    T["selb"] = sbuf(st, [128, 96], BF16, "selb")
                    T["etmp2"] = sbuf(st, [128, 4, 64], F32, "etmp2")
                    T["B"] = [Buf() for _ in range(9)]
                    TL.append(T)
                nacc2 = [sbuf(st, [128, 4, 8, 64], F32, "nacc") for _ in range(2)]
                nb16 = sbuf(st, [128, 512], BF16, "nb16")
                PT = [sbuf(st, [128, 512], BF16, "PT") for _ in range(3)]
                oTs = sbuf(st, [65, 512], F32, "oTs")
                ew = sbuf(st, [128, 8], F32, "ew")
                etmp = sbuf(st, [128, 4, 64], F32, "etmp")
                naccB2 = [[Buf() for _ in range(4)] for _ in range(2)]
                nb16B, oTsB, ewB, etmpB = Buf(), Buf(), Buf(), Buf()
                PTB = [Buf() for _ in range(3)]
                for T in TL:
                    S.op(S.dve, lambda e: e.memset(T["pp"][:], 0.0), writes=[T["B"][3]])
                    S.op(S.dve, lambda e: e.memset(T["pb16"][:], 0.0), writes=[T["B"][1]])
                    S.op(S.dve, lambda e: e.memset(T["selb"][:], 0.0), writes=[T["B"][7]])
                sti = [0]
                pti = [0]
                oti = [0]

                def attn_items(items, qb, h, kT_idx, v_idx, chunks, gate_col, acc_tile, accB, qT, qTB, kT, kTB, vt, vtB, mask_fn, extra_reads):
                    hs = slice(0, 96)
                    pair = h
                    ob = oti[0] % 2
                    oti[0] += 1
                    n = len(chunks)
                    for ci, (kc, j0, j1) in enumerate(chunks):
                        sb_ = sti[0] % 3
                        sti[0] += 1
                        pb_ = pti[0] % 3
                        pti[0] += 1
                        qsl = slice(qb * 512 + j0, qb * 512 + j1)

                        def qk(kc=kc, j0=j0, j1=j1, sb_=sb_, qsl=qsl):
                            pairs = [(kT[hs, kT_idx, kc * 128:(kc + 1) * 128], qT[hs, pair, qsl])] + mask_fn(kc, j0, j1)
                            S.mm(stp[sb_][:, j0:j1], pairs,
                                 reads=[kTB[kc]] + [qTB[4 * qb + q] for q in range(j0 // 128, (j1 + 127) // 128)] + extra_reads,
                                 writes=[stB[sb_]])

                        def ex(j0=j0, j1=j1, sb_=sb_, pb_=pb_):
                            S.op(S.act, lambda e: e.activation(PT[pb_][:, j0:j1], stp[sb_][:, j0:j1], AF.Exp, scale=0.125),
                                 reads=[stB[sb_]], writes=[PTB[pb_]])

                        def pv(kc=kc, j0=j0, j1=j1, pb_=pb_, ci=ci):
                            S.mm1(oT[ob][0:65, j0:j1], vt[:, kc, v_idx, 0:65], PT[pb_][:, j0:j1], start=(ci == 0), stop=(ci == n - 1),
                                  reads=[PTB[pb_], vtB[kc]], writes=[oTB[ob]])
                        epi_a = epi_b = None
                        if ci == n - 1:
                            def epi_a():
                                S.op(S.dve, lambda e: e.tensor_copy(oTs[:], oT[ob][0:65, :]), reads=[oTB[ob]], writes=[oTsB])

                            def epi_b():
                                for tl in range(4):
                                    S.op(S.pe, lambda e: e.transpose(otp[:, tl, :], oTs[:, tl * 128:(tl + 1) * 128], identf[0:65, 0:65]),
                                         reads=[oTsB, identB], writes=[otpB])
                                S.op(S.dve, lambda e: e.tensor_scalar(ew[:, 0:4], otp[:, :, 64], 1e-30, None, ALU.max), reads=[otpB], writes=[ewB])
                                S.op(S.dve, lambda e: e.reciprocal(ew[:, 0:4], ew[:, 0:4]), reads=[ewB], writes=[ewB])
                                S.op(S.dve, lambda e: e.tensor_tensor(ew[:, 0:4], ew[:, 0:4], ga[:, 4 * qb:4 * qb + 4, gate_col], ALU.mult),
                                     reads=[ewB] + gaB[4 * qb:4 * qb + 4], writes=[ewB])
                                S.op(S.dve, lambda e: e.tensor_tensor(etmp[:], otp[:, :, 0:64], bcast(ew[:, 0:4], 2, 64), ALU.mult),
                                     reads=[otpB, ewB], writes=[etmpB])
                                S.op(S.pool, lambda e: e.tensor_tensor(acc_tile[:, :, h, :], acc_tile[:, :, h, :], etmp[:], ALU.add),
                                     reads=[etmpB] + accB, writes=accB)
                        items.append((qk, ex, pv, epi_a, epi_b))

                if STOP[0] and STOP[0].startswith("budget:"):
                    S.budget = int(STOP[0].split(":")[1])
                def part_a_gen(qb):
                    nacc = nacc2[qb % 2]
                    naccB = naccB2[qb % 2]
                    S.op(S.pool, lambda e: e.memset(nacc[:], 0.0), writes=naccB)
                    yield
                    gens = [part_a_chain(qb, 0), part_a_chain(qb, 1)]
                    while gens:
                        for gn in list(gens):
                            if next(gn, "done") == "done":
                                gens.remove(gn)
                            else:
                                yield

                def part_a_chain(qb, g):
                    nacc = nacc2[qb % 2]
                    naccB = naccB2[qb % 2]
                    T = TL[g]
                    sm, pb16, pT, pp, st8, imp, imp3, m8, selb, etmp2 = (T[k_] for k_ in ("sm", "pb16", "pT", "pp", "st8", "imp", "imp3", "m8", "selb", "etmp2"))
                    smB, pb16B, pTB, ppB, st8B, impB, m8B, selbB, etmp2B = T["B"]
                    ppw = bass.AP(pp[:, 0:1].tensor, pp[:, 0:1].offset, [list(pp[:, 0:1].ap[0]), [4, 32], [1, 5]])
                    if True:
                        for tl in range(4):
                            t = 4 * qb + tl
                            tsl = slice(t * 128, (t + 1) * 128)
                            for r in range(4):
                                h = 4 * g + r
                                S.mm(sc[:, r, 0:127], [(qaT[0:96, h, tsl], kcbT[0:96, g, 0:127])],
                                     reads=[qaB[t], kcbTB], writes=[scB])
                            S.op(S.dve, lambda e: e.tensor_tensor(sm[:], sc[:, :, 0:127], bcast(cmask[:, t, 0:127], 1, 4), ALU.add),
                                 reads=[scB, cB], writes=[smB])
                            yield
                            S.op(S.act, lambda e: e.activation(sm[:], sm[:], AF.Exp, scale=0.125), reads=[smB], writes=[smB])
                            yield
                            S.op(S.dve, lambda e: e.tensor_reduce(st8[:, 0:4], sm[:], AX.X, ALU.add), reads=[smB], writes=[st8B])
                            S.op(S.dve, lambda e: e.tensor_scalar(st8[:, 0:4], st8[:, 0:4], 1e-30, None, ALU.max), reads=[st8B], writes=[st8B])
                            yield
                            S.op(S.dve, lambda e: e.reciprocal(st8[:, 0:4], st8[:, 0:4]), reads=[st8B], writes=[st8B])
                            S.op(S.dve, lambda e: e.tensor_tensor(sm[:], sm[:], bcast(st8[:, 0:4], 2, 127), ALU.mult),
                                 reads=[smB, st8B], writes=[smB])
                            yield
                            S.op(S.pool, lambda e: e.tensor_copy(pb16[:, :, 0:127], sm[:]), reads=[smB], writes=[pb16B])
                            S.op(S.dve, lambda e: e.tensor_reduce(pp[:, 1:128], sm[:].rearrange("p r c -> p c r"), AX.X, ALU.add),
                                 reads=[smB], writes=[ppB])
                            yield
                            S.op(S.dve, lambda e: e.tensor_reduce(imp[:], ppw, AX.X, ALU.add), reads=[ppB], writes=[impB])
                            S.op(S.dve, lambda e: e.tensor_tensor(imp[:], imp[:], selbias[:, t, :], ALU.add), reads=[impB, cB], writes=[impB])
                            yield
                            S.op(S.dve, lambda e: e.max(m8[:, 0:8], imp[:]), reads=[impB], writes=[m8B])
                            S.op(S.dve, lambda e: e.match_replace(imp3[:], m8[:, 0:8], imp[:], -3.0e38), reads=[impB, m8B], writes=[m8B])
                            yield
                            S.op(S.dve, lambda e: e.max(m8[:, 8:16], imp3[:]), reads=[m8B], writes=[m8B])
                            S.op(S.dve, lambda e: e.tensor_scalar(imp3[:], imp[:], m8[:, 15:16], None, ALU.is_ge), reads=[impB, m8B], writes=[m8B])
                            yield
                            S.op(S.dve, lambda e: e.tensor_scalar(selb[:, 64:96], imp3[:], -NEG_BIG, NEG_BIG, ALU.mult, ALU.add),
                                 reads=[m8B], writes=[selbB])
                            yield
                            S.op(S.pe, lambda e: e.transpose(tpb[0:96, 4, :], selb[:], ident[:]), reads=[selbB, identB], writes=[tpsB])
                            S.op(S.dve, lambda e: e.tensor_copy(qaT[64:96, 4 * g:4 * g + 4, tsl], bcast(tpb[64:96, 4, :], 1, 4)),
                                 reads=[tpsB], writes=[selRB[g][t]])
                            yield
                            for r in range(4):
                                S.op(S.pe, lambda e: e.transpose(tpb[0:127, r, :], pb16[:, r, 0:127], ident[:]),
                                     reads=[pb16B, identB], writes=[tpbB])
                            S.op(S.dve, lambda e: e.tensor_copy(pT[0:127, :, :], tpb[0:127, 0:4, :]), reads=[tpbB], writes=[pTB])
                            yield
                            for r in range(4):
                                S.mm(oc[:, r, :], [(pT[0:127, r, :], vcb[0:127, g, :])], reads=[pTB, vcbB], writes=[ocB])
                            S.op(S.dve, lambda e: e.tensor_tensor(etmp2[:], oc[:], bcast(ga[:, t, 12 * g:12 * g + 12:3], 2, 64), ALU.mult),
                                 reads=[ocB, gaB[t]], writes=[etmp2B])
                            S.op(S.pool, lambda e: e.tensor_tensor(nacc[:, tl, 4 * g:4 * g + 4, :], nacc[:, tl, 4 * g:4 * g + 4, :], etmp2[:], ALU.add),
                                 reads=[etmp2B, naccB[tl]], writes=[naccB[tl]])
                            yield

                print('SBUF remaining in NSA attn:', nc.sbuf_bytes_remaining)
                gen_a = part_a_gen(0)
                next(gen_a)
                for qb in range(4):
                    nacc = nacc2[qb % 2]
                    naccB = naccB2[qb % 2]
                    win_items = []
                    sel_items = []
                    for h in range(8):
                        g = h // 4
                        order = [1, 0, 2, 3, 4, -1, -2, -3]
                        chunks = []
                        for e_ in order:
                            kc = 4 * qb - e_
                            if kc < 0 or kc > 4 * qb + 3:
                                continue
                            if e_ >= 1:
                                j0, j1 = 0, min(512, 640 - 128 * e_)
                            else:
                                j0, j1 = -128 * e_, 512
                            chunks.append((kc, j0, j1))

                        def win_mask(kc, j0, j1, qb=qb):
                            return [(ident[:], wmask[:, 4 * qb - kc + 3, j0:j1])]
                        attn_items(win_items, qb, h, 2 + g, 2 + g, chunks, 3 * h + 2, nacc, naccB, qaT, qaB, kskwT, kkB, vsw, vswB, win_mask,
                                   [cB, identB])
                    for h in range(8):
                        g = h // 4
                        chunks = []
                        for kc in range(4 * qb + 4):
                            d = kc - 4 * qb
                            chunks.append((kc, 128 * d if d > 0 else 0, 512))

                        def sel_mask(kc, j0, j1, g=g, qb=qb, h=h):
                            m = []
                            d = kc - 4 * qb
                            if d >= 0:
                                m.append((ident[:], caus[:, d, j0:j1]))
                            return m
                        attn_items(sel_items, qb, h, g, g, chunks, 3 * h + 1, nacc, naccB, qaT, qaB, kskwT, kkB, vsw, vswB, sel_mask,
                                   selRB[g][4 * qb:4 * qb + 4] + [eB_, cB, causB, identB])
                    run_pipeline(win_items, filler=gen_a, per_item=(8 * 14) // len(win_items) + 1)
                    gen_a = part_a_gen(qb + 1) if qb < 3 else None
                    if gen_a is not None:
                        next(gen_a)
                    run_pipeline(sel_items, filler=gen_a, per_item=1, drain=False)
                    for tl in range(4):
                        t = 4 * qb + tl
                        S.op(S.pool, lambda e: e.tensor_copy(nb16[:], nacc[:, tl, :, :].rearrange("p h d -> p (h d)")),
                             reads=[naccB[tl]], writes=[nb16B])
                        for c in range(4):
                            S.op(S.pe, lambda e: e.transpose(tpb[:, c, :], nb16[:, c * 128:(c + 1) * 128], ident[:]),
                                 reads=[nb16B, identB], writes=[tpbB])
                        S.op(S.act, lambda e: e.activation(nsaT[:, :, t * 128:(t + 1) * 128], tpb[:, 0:4, :], AF.Copy),
                             reads=[tpbB], writes=[nsaTB[t]])
                S.barrier()
        dbg_out("nsaT", nsaT[:], nsaTB)
        if STOP[0] == "nsaattn":
            return

        mobaT = sbuf(ms, [128, 4, S_LEN], BF16, "mobaT")
        mw = ms.enter_context(ExitStack())
        pastb = sbuf(mw, [128, NT, 8], F32, "pastb")
        ownsel = sbuf(mw, [128, NT, 8], F32, "ownsel")
        cB2 = Buf()
        S.dma(S.sp, pastb[:], D["pastbias"], writes=[cB2])
        S.dma(S.sp, ownsel[:], D["ownsel"], writes=[cB2])
        wm_all = [sbuf(mw, [128, 8, 768], BF16, "wm") for _ in range(2)]
        wmB_all = [[Buf() for _ in range(3)] for _ in range(2)]
        for hp_ in range(2):
            for i in range(3):
                c0 = 1304 + 512 * i + 256 * hp_
                S.dma(S.pool, wm_all[hp_][:, :, i * 256:(i + 1) * 256], win_v[:, :, c0:c0 + 256], writes=[wmB_all[hp_][i]])
        for hp in range(2):
            with ExitStack() as mo:
                qbT = sbuf(mo, [128, 4, S_LEN], BF16, "qbT")
                kbT = sbuf(mo, [128, 4, S_LEN], BF16, "kbT")
                vb = sbuf(mo, [128, NT, 4, 65], BF16, "vb")
                qbB = [Buf() for _ in range(NT)]
                kbB = [Buf() for _ in range(NT)]
                vbB = [Buf() for _ in range(NT)]
                mselB = [Buf() for _ in range(NT)]
                e8B = Buf()
                S.op(S.pool, lambda e: e.memset(vb[:], 1.0), writes=vbB)
                S.op(S.pool, lambda e: e.memset(qbT[64:96, :, :], 0.0), writes=mselB)
                for j in range(4):
                    S.dma(S.pool, kbT[64:96, j, :], D["e8"][j], writes=[e8B])
                with ExitStack() as st:
                    rt = make_rt(st)
                    tnorm = make_tile_norm(st, D["mix_norm"][l:l + 1, :], nstats)
                    wm = wm_all[hp]
                    wmB = wmB_all[hp]
                    pA = [psum(st, [128, 512], F32, "pA") for _ in range(2)]
                    pC = [psum(st, [128, 256], F32, "pC") for _ in range(2)]
                    tqk = psum(st, [128, 8, 128], BF16, "tqk")
                    pAB = [PBuf(), PBuf()]
                    pCB = [PBuf(), PBuf()]
                    tqkB = PBuf()
                    qkr = [sbuf(st, [128, 8, 128], BF16, "qkr") for _ in range(2)]
                    qkrB = [Buf(), Buf()]
                    for i_ in range(2):
                        S.op(S.pool, lambda e: e.memset(qkr[i_][:], 0.0), writes=[qkrB[i_]])
                    hts = {}

                    def m_norm(t):
                        hts[t] = tnorm(t)

                    def m_pa(t):
                        hTt, hTtB = hts[t]
                        i = t % 2
                        S.mm(pA[i][:], [(hTt[:, c, :], wm[:, c, 0:512]) for c in range(8)], reads=[hTtB, wmB[0], wmB[1]], writes=[pAB[i]])

                    def m_pc(t):
                        hTt, hTtB = hts[t]
                        i = t % 2
                        S.mm(pC[i][:], [(hTt[:, c, :], wm[:, c, 512:768]) for c in range(8)], reads=[hTtB, wmB[2]], writes=[pCB[i]])

                    def m_post(t):
                        i = t % 2
                        tsl = slice(t * 128, (t + 1) * 128)
                        cs = (rope[:, t, 0, :], rope[:, t, 1, :])
                        rope_apply(S, rt, pA[i][:].rearrange("p (h d) -> p h d", h=8), 8, cs, qkr[i][:, :, 0:64],
                                   reads=[pAB[i], ropeB], writes=[qkrB[i]])
                        S.op(S.act, lambda e: e.activation(vb[:, t, :, 0:64], pC[i][:].rearrange("p (h d) -> p h d", h=4), AF.Copy),
                             reads=[pCB[i]], writes=[vbB[t]])
                        for c in range(8):
                            S.op(S.pe, lambda e: e.transpose(tqk[:, c, :], qkr[i][:, c, :], ident[:]),
                                 reads=[qkrB[i], identB], writes=[tqkB])
                        S.op(S.act, lambda e: e.activation(qbT[0:64, :, tsl], tqk[0:64, 0:4, :], AF.Copy), reads=[tqkB], writes=[qbB[t]])
                        S.op(S.act, lambda e: e.activation(kbT[0:64, :, tsl], tqk[0:64, 4:8, :], AF.Copy), reads=[tqkB], writes=[kbB[t]])

                    print('SBUF remaining in MoBA proj:', nc.sbuf_bytes_remaining)
                    m_norm(0)
                    m_pa(0)
                    m_pc(0)
                    m_norm(1)
                    for t in range(NT):
                        if t + 2 < NT:
                            m_norm(t + 2)
                        if t + 1 < NT:
                            m_pa(t + 1)
                            m_pc(t + 1)
                        m_post(t)
                    S.barrier()
                with ExitStack() as st:
                    kmf = sbuf(st, [128, 4, 8], F32, "kmf")
                    kmb = sbuf(st, [128, 4, 8], BF16, "kmb")
                    kmB = Buf()
                    S.op(S.dve, lambda e: e.tensor_reduce(kmf[0:64], kbT[0:64].rearrange("p c (b k) -> p c b k", b=8), AX.X, ALU.add),
                         reads=kbB, writes=[kmB])
                    S.op(S.dve, lambda e: e.tensor_scalar(kmb[0:64], kmf[0:64], 1.0 / 256.0, None, ALU.mult), reads=[kmB], writes=[kmB])
                    gs2 = sbuf(st, [128, 4, 8], F32, "gs2")
                    m8a = sbuf(st, [128, 4, 8], F32, "m8a")
                    thr = sbuf(st, [128, 4], F32, "thr")
                    sel = sbuf(st, [128, 4, 8], F32, "sel")
                    mb = sbuf(st, [128, 128], BF16, "mb")
                    gs2B, m8aB, thrB, selB_, mbB = Buf(), Buf(), Buf(), Buf(), Buf()
                    S.op(S.pool, lambda e: e.memset(mb[:], 0.0), writes=[mbB])

                    def gate_gen(tiles):
                        for t in tiles:
                            tsl = slice(t * 128, (t + 1) * 128)
                            for j in range(4):
                                S.mm(gp[:, j, :], [(qbT[0:64, j, tsl], kmb[0:64, j, :])], reads=[qbB[t], kmB], writes=[gpB])
                            yield
                            S.op(S.dve, lambda e: e.tensor_tensor(gs2[:], gp[:, 0:4, :], bcast(pastb[:, t, :], 1, 4), ALU.add),
                                 reads=[gpB, cB2], writes=[gs2B])
                            yield
                            for j in range(4):
                                S.op(S.dve, lambda e: e.max(m8a[:, j, :], gs2[:, j, :]), reads=[gs2B], writes=[m8aB])
                                if j % 2 == 1:
                                    yield
                            S.op(S.dve, lambda e: e.tensor_scalar(thr[:], m8a[:, :, 2], -1e29, None, ALU.max), reads=[m8aB], writes=[thrB])
                            S.op(S.dve, lambda e: e.tensor_tensor(sel[:], gs2[:], bcast(thr[:], 2, 8), ALU.is_ge), reads=[gs2B, thrB], writes=[selB_])
                            yield
                            S.op(S.dve, lambda e: e.tensor_tensor(sel[:], sel[:], bcast(ownsel[:, t, :], 1, 4), ALU.max), reads=[selB_, cB2], writes=[selB_])
                            S.op(S.dve, lambda e: e.tensor_scalar(mb[:, 64:96], sel[:].rearrange("p h b -> p (h b)"), -NEG_BIG, NEG_BIG, ALU.mult, ALU.add),
                                 reads=[selB_], writes=[mbB])
                            yield
                            S.op(S.pe, lambda e: e.transpose(tm, mb[:], ident[:]), reads=[mbB, identB], writes=[tmB])
                            S.op(S.dve, lambda e: e.tensor_copy(qbT[64:96, :, tsl], bcast(tm[64:96, :], 1, 4)), reads=[tmB], writes=[mselB[t]])
                            yield

                    caus = sbuf(st, [128, 4, 512], BF16, "caus")
                    causB = Buf()
                    S.dma(S.pool, caus[:], D["caus"], writes=[causB])
                    stp = [psum(st, [128, 512], F32, "stp") for _ in range(4)]
                    oT = [psum(st, [128, 512], F32, "oT") for _ in range(2)]
                    otpg = psum(st, [128, 512], F32, "otpg")
                    otp = otpg[:, 0:260].rearrange("p (a b) -> p a b", a=4)
                    gp = otpg[:, 448:512].rearrange("p (a b) -> p a b", a=8)
                    tpbm = psum(st, [128, 8, 128], BF16, "tpbm")
                    tpb = tpbm[:, 0:4, :]
                    tm = tpbm[:, 4, :]
                    stB = [PBuf() for _ in range(4)]
                    oTB = [PBuf(), PBuf()]
                    otpB, tpbB = PBuf(), PBuf()
                    gpB = otpB
                    tmB = tpbB
                    print('SBUF remaining in MoBA attn (before macc etc):', nc.sbuf_bytes_remaining)
                    for _ in gate_gen(range(0, 4)):
                        pass
                    macc = sbuf(st, [128, 4, 4, 64], F32, "macc")
                    maccB = [Buf() for _ in range(4)]
                    nb16 = sbuf(st, [128, 256], BF16, "nb16")
                    PT = [sbuf(st, [128, 512], BF16, "PT") for _ in range(4)]
                    oTs = sbuf(st, [65, 512], F32, "oTs")
                    ew = sbuf(st, [128, 8], F32, "ew")
                    nb16B, oTsB, ewB = Buf(), Buf(), Buf()
                    PTB = [Buf() for _ in range(4)]
                    sti = 0
                    pti = 0
                    oti = 0
                    for qb in range(4):
                        items = []
                        for j in range(4):
                            ob = oti % 2
                            oti += 1
                            nch = 4 * qb + 4
                            for kc in range(nch):
                                d = kc - 4 * qb
                                j0 = 128 * d if d > 0 else 0
                                sb_ = sti % 4
                                sti += 1
                                pb_ = pti % 4
                                pti += 1
                                qsl = slice(qb * 512 + j0, (qb + 1) * 512)

                                def qk(kc=kc, d=d, j0=j0, sb_=sb_, qsl=qsl, j=j, qb=qb):
                                    pairs = [(kbT[0:96, j, kc * 128:(kc + 1) * 128], qbT[0:96, j, qsl])]
                                    if d >= 0:
                                        pairs.append((ident[:], caus[:, d, j0:512]))
                                    S.mm(stp[sb_][:, j0:512], pairs,
                                         reads=[kbB[kc], e8B, causB, identB] + [qbB[4 * qb + q] for q in range(j0 // 128, 4)]
                                         + [mselB[4 * qb + q] for q in range(j0 // 128, 4)], writes=[stB[sb_]])

                                def ex(j0=j0, sb_=sb_, pb_=pb_):
                                    S.op(S.act, lambda e: e.activation(PT[pb_][:, j0:512], stp[sb_][:, j0:512], AF.Exp, scale=0.125),
                                         reads=[stB[sb_]], writes=[PTB[pb_]])

                                def pv(kc=kc, j0=j0, pb_=pb_, j=j, ob=ob, nch=nch):
                                    S.mm1(oT[ob][0:65, j0:512], vb[:, kc, j, 0:65], PT[pb_][:, j0:512], start=(kc == 0), stop=(kc == nch - 1),
                                          reads=[PTB[pb_], vbB[kc]], writes=[oTB[ob]])
                                epi_a = epi_b = None
                                if kc == nch - 1:
                                    def epi_a(ob=ob):
                                        S.op(S.dve, lambda e: e.tensor_copy(oTs[:], oT[ob][0:65, :]), reads=[oTB[ob]], writes=[oTsB])

                                    def epi_b(j=j):
                                        for tl in range(4):
                                            S.op(S.pe, lambda e: e.transpose(otp[:, tl, :], oTs[:, tl * 128:(tl + 1) * 128], identf[0:65, 0:65]),
                                                 reads=[oTsB, identB], writes=[otpB])
                                        S.op(S.dve, lambda e: e.tensor_scalar(ew[:, 0:4], otp[:, :, 64], 1e-30, None, ALU.max), reads=[otpB], writes=[ewB])
                                        S.op(S.dve, lambda e: e.reciprocal(ew[:, 0:4], ew[:, 0:4]), reads=[ewB], writes=[ewB])
                                        S.op(S.dve, lambda e: e.tensor_tensor(macc[:, :, j, :], otp[:, :, 0:64], bcast(ew[:, 0:4], 2, 64), ALU.mult),
                                             reads=[otpB, ewB], writes=maccB)
                                items.append((qk, ex, pv, epi_a, epi_b))
                        if qb < 3:
                            run_pipeline(items, la=3, filler=gate_gen(range(4 * qb + 4, 4 * qb + 8)), per_item=(4 * 8) // len(items) + 1)
                        else:
                            run_pipeline(items, la=3)
                        for tl in range(4):
                            t = 4 * qb + tl
                            S.op(S.pool, lambda e: e.tensor_copy(nb16[:], macc[:, tl, :, :].rearrange("p h d -> p (h d)")),
                                 reads=[maccB[tl]], writes=[nb16B])
                            for c in range(2):
                                S.op(S.pe, lambda e: e.transpose(tpb[:, c, :], nb16[:, c * 128:(c + 1) * 128], ident[:]),
                                     reads=[nb16B, identB], writes=[tpbB])
                            S.op(S.act, lambda e: e.activation(mobaT[:, 2 * hp:2 * hp + 2, t * 128:(t + 1) * 128], tpb[:, 0:2, :], AF.Copy),
                                 reads=[tpbB], writes=[mobaTB[t]])
                    S.barrier()
        mw.close()
        dbg_out("mobaT", mobaT[:], mobaTB)
        if STOP[0] == "mobaattn":
            return

        with ExitStack() as st:
            wgt = sbuf(st, [128, 8, 2048], BF16, "wgt")
            wbn = sbuf(st, [128, 4, DM], BF16, "wbn")
            wbm = sbuf(st, [128, 4, DM], BF16, "wbm")
            wo = sbuf(st, [128, 8, DM], BF16, "wo")
            wB = [Buf() for _ in range(4)]
            for i in range(4):
                S.dma(S.pool, wgt[:, :, i * 512:(i + 1) * 512], win_v[:, :, 2840 + i * 512:2840 + (i + 1) * 512], writes=[wB[0]])
            S.dma(S.pool, wbn[:], D["w_branch_nsa"][l].rearrange("(c p) n -> p c n", p=128), writes=[wB[1]])
            S.dma(S.pool, wbm[:], D["w_branch_moba"][l].rearrange("(c p) n -> p c n", p=128), writes=[wB[2]])
            for i in range(2):
                S.dma(S.pool, wo[:, :, i * 512:(i + 1) * 512], D["w_out"][l].rearrange("(c p) n -> p c n", p=128)[:, :, i * 512:(i + 1) * 512],
                      writes=[wB[3]])
            tnorm = make_tile_norm(st, D["mix_norm"][l:l + 1, :], nstats)
            sa = sbuf(st, [128, 512], F32, "sa")
            sbb = sbuf(st, [128, 512], F32, "sbb")
            ma = sbuf(st, [128, 512], F32, "ma")
            mg = sbuf(st, [128, DM], BF16, "mg")
            mgT = sbuf(st, [128, 8, 128], BF16, "mgT")
            saB, sbB, maB, mgB, mgTB = (Buf() for _ in range(5))
            tpm = psum(st, [128, 8, 128], BF16, "tpm")
            pga = psum(st, [128, 512], F32, "pga")
            pya = psum(st, [128, 512], F32, "pya")
            pgb = psum(st, [128, 512], F32, "pgb")
            pyb = psum(st, [128, 512], F32, "pyb")
            po = [psum(st, [128, 512], F32, "po") for _ in range(2)]
            tpmB, pgaB, pyaB, pgbB, pybB = (PBuf() for _ in range(5))
            poB = [PBuf(), PBuf()]
            mg2 = sbuf(st, [128, DM], BF16, "mg2")
            mgs = [mg, mg2]
            mgBs = [mgB, Buf()]
            hts = {}
            pi = [0]

            def g_norm(t):
                hts[t] = tnorm(t)

            def g_gate(t):
                tsl = slice(t * 128, (t + 1) * 128)
                hTt, hTtB = hts[t]
                mgt, mgtB = mgs[t % 2], mgBs[t % 2]
                for hf in range(2):
                    cs_ = slice(hf * 512, (hf + 1) * 512)
                    S.mm(pga[:], [(hTt[:, c, :], wgt[:, c, hf * 512:(hf + 1) * 512]) for c in range(8)], reads=[hTtB, wB[0]], writes=[pgaB])
                    S.mm(pya[:], [(nsaT[:, c, tsl], wbn[:, c, cs_]) for c in range(4)], reads=[nsaTB[t], wB[1]], writes=[pyaB])
                    S.mm(pgb[:], [(hTt[:, c, :], wgt[:, c, 1024 + hf * 512:1024 + (hf + 1) * 512]) for c in range(8)], reads=[hTtB, wB[0]], writes=[pgbB])
                    S.mm(pyb[:], [(mobaT[:, c, tsl], wbm[:, c, cs_]) for c in range(4)], reads=[mobaTB[t], wB[2]], writes=[pybB])
                    S.op(S.act, lambda e: e.activation(sa[:], pga[:], AF.Sigmoid), reads=[pgaB], writes=[saB])
                    S.op(S.act, lambda e: e.activation(sbb[:], pgb[:], AF.Sigmoid), reads=[pgbB], writes=[sbB])
                    S.op(S.dve, lambda e: e.tensor_tensor(ma[:], sa[:], pya[:], ALU.mult), reads=[saB, pyaB], writes=[maB])
                    S.op(S.dve, lambda e: e.tensor_tensor(sbb[:], sbb[:], pyb[:], ALU.mult), reads=[sbB, pybB], writes=[sbB])
                    S.op(S.dve, lambda e: e.tensor_tensor(mgt[:, cs_], ma[:], sbb[:], ALU.add), reads=[maB, sbB], writes=[mgtB])

            def g_out(t):
                mgt, mgtB = mgs[t % 2], mgBs[t % 2]
                for c in range(8):
                    S.op(S.pe, lambda e: e.transpose(tpm[:, c, :], mgt[:, c * 128:(c + 1) * 128], ident[:]), reads=[mgtB, identB], writes=[tpmB])
                S.op(S.act, lambda e: e.activation(mgT[:], tpm[:], AF.Copy), reads=[tpmB], writes=[mgTB])
                for hf in range(2):
                    k = pi[0] % 2
                    pi[0] += 1
                    cs_ = slice(hf * 512, (hf + 1) * 512)
                    S.mm(po[k][:], [(mgT[:, c, :], wo[:, c, cs_]) for c in range(8)], reads=[mgTB, wB[3]], writes=[poB[k]])
                    S.op(S.dve, lambda e: e.tensor_tensor(x_sb[:, t, cs_], x_sb[:, t, cs_], po[k][:], ALU.add),
                         reads=[poB[k], xB[t]], writes=[xB[t]])

            print('SBUF remaining in merge:', nc.sbuf_bytes_remaining)
            g_norm(0)
            g_norm(1)
            g_gate(0)
            for t in range(NT):
                if t + 2 < NT:
                    g_norm(t + 2)
                if t + 1 < NT:
                    g_gate(t + 1)
                g_out(t)
            S.barrier()


_CACHE = {}


def kernel(**inputs):
    x = np.asarray(inputs["x"], dtype=np.float32)
    B = x.shape[0]
    consts = host_consts()
    shared = {}
    for name, shp in WEIGHT_SHAPES.items():
        shared[name] = np.ascontiguousarray(np.asarray(inputs[name], dtype=np.float32).reshape(shp))
    for name in CONST_SHAPES:
        shared[name] = consts[name]
    if "nc" not in _CACHE:
        _CACHE["nc"] = build_program()[0]
    nc = _CACHE["nc"]
    in_maps = []
    for b in range(B):
        m = dict(shared)
        m["x"] = np.ascontiguousarray(x[b])
        in_maps.append(m)
    res = run_bass_kernel_spmd(nc, in_maps, core_ids=list(range(B)))
    out = np.stack([np.asarray(res.results[b]["out"], dtype=np.float32) for b in range(B)], axis=0)
    return out
```

```python
import itertools
import numpy as np
from contextlib import ExitStack
import concourse.bass as bass
import concourse.mybir as mybir
from concourse.bass_utils import run_bass_kernel_spmd

F32 = mybir.dt.float32
BF16 = mybir.dt.bfloat16
AF = mybir.ActivationFunctionType
ALU = mybir.AluOpType
AX = mybir.AxisListType

S_LEN = 2048
DM = 1024
NT = 16
DFF = 2816
NFC = 22
IN_COLS = 4888
NEG_BIG = -30000.0
NEG = -1e30
EPS = 1e-6


class Buf:
    __slots__ = ("name", "lw", "rd", "excl")

    def __init__(self, name="", excl=False):
        self.name = name
        self.lw = None
        self.rd = []
        self.excl = excl


def PBuf():
    return Buf("psum", True)


class Track:
    def __init__(self, sem, step, name):
        self.sem = sem
        self.step = step
        self.count = 0
        self.name = name


class Eng:
    def __init__(self, obj, track, name, inorder=False):
        self.obj = obj
        self.track = track
        self.name = name
        self.seen = {}
        self.inorder = inorder


class Sched:
    def __init__(self, nc, stack, n_dma_tracks=16):
        self.nc = nc

        def mk(obj, name, inorder=False):
            sem = stack.enter_context(nc.semaphore("s_" + name))
            return Eng(obj, Track(sem, 1, name), name, inorder)

        self.pe = mk(nc.tensor, "pe", True)
        self.act = mk(nc.scalar, "act")
        self.dve = mk(nc.vector, "dve")
        self.pool = mk(nc.gpsimd, "pool")
        self.sp = mk(nc.sync, "sp")
        self.engs = [self.pe, self.act, self.dve, self.pool, self.sp]
        self.dma_pools = {}
        for e_ in (self.sp, self.pool):
            self.dma_pools[e_.name] = [
                Track(stack.enter_context(nc.semaphore("s_dma_%s%d" % (e_.name, i))), 16, "dma_%s%d" % (e_.name, i))
                for i in range(n_dma_tracks // 2)
            ]
        self.dma_tracks = self.dma_pools["sp"] + self.dma_pools["pool"]
        self.dma_i = {"sp": 0, "pool": 0}
        self.n_inst = 0
        self.n_wait = 0
        self.budget = None

    def _skip(self):
        if self.budget is None:
            return False
        if self.budget <= 0:
            return True
        self.budget -= 1
        return False

    def _wait(self, eng, track, val):
        if eng.inorder and track is eng.track:
            return
        if eng.seen.get(track, 0) >= val:
            return
        eng.obj.wait_ge(track.sem, val)
        eng.seen[track] = val
        self.n_wait += 1

    @staticmethod
    def _split(reads, writes):
        ex = [b for b in reads if b.excl]
        if ex:
            reads = [b for b in reads if not b.excl]
            writes = list(writes) + ex
        return reads, writes

    def _deps(self, eng, reads, writes):
        for b in reads:
            if b.lw is not None:
                self._wait(eng, *b.lw)
        for b in writes:
            if b.lw is not None:
                self._wait(eng, *b.lw)
            for r in b.rd:
                self._wait(eng, *r)

    def _commit(self, track, val, reads, writes):
        for b in reads:
            b.rd.append((track, val))
        for b in writes:
            b.lw = (track, val)
            b.rd = []

    def op(self, eng, fn, reads=(), writes=()):
        if self._skip():
            return None
        reads, writes = self._split(reads, writes)
        self._deps(eng, reads, writes)
        inst = fn(eng.obj)
        t = eng.track
        t.count += 1
        inst.then_inc(t.sem, 1)
        self._commit(t, t.count, reads, writes)
        self.n_inst += 1
        return inst

    def mm(self, out_ap, pairs, reads=(), writes=()):
        if self._skip():
            return None
        eng = self.pe
        reads, writes = self._split(reads, writes)
        self._deps(eng, reads, writes)
        n = len(pairs)
        inst = None
        for i, (l, r) in enumerate(pairs):
            inst = eng.obj.matmul(out_ap, l, r, start=(i == 0), stop=(i == n - 1))
            self.n_inst += 1
        t = eng.track
        t.count += 1
        inst.then_inc(t.sem, 1)
        self._commit(t, t.count, reads, writes)

    def mm1(self, out_ap, l, r, start, stop, reads=(), writes=(), skip=False):
        if self._skip():
            return None
        eng = self.pe
        reads, writes = self._split(reads, writes)
        self._deps(eng, reads, writes)
        if skip:
            inst = eng.obj.matmul(out_ap, l, r, start=start, stop=stop, skip_group_check=True)
        else:
            inst = eng.obj.matmul(out_ap, l, r, start=start, stop=stop)
        t = eng.track
        t.count += 1
        inst.then_inc(t.sem, 1)
        self._commit(t, t.count, reads, writes)
        self.n_inst += 1

    def dma(self, eng, out, in_, reads=(), writes=(), **kw):
        if self._skip():
            return None
        pool_ = self.dma_pools[eng.name]
        tr = pool_[self.dma_i[eng.name] % len(pool_)]
        self.dma_i[eng.name] += 1
        if tr.count:
            self._wait(eng, tr, tr.count * 16)
        self._deps(eng, reads, writes)
        inst = eng.obj.dma_start(out=out, in_=in_, **kw)
        tr.count += 1
        inst.then_inc(tr.sem, 16)
        self._commit(tr, tr.count * 16, reads, writes)
        self.n_inst += 1
        return inst

    def pe_fence(self):
        t = self.pe.track
        if t.count and self.pe.seen.get(t, 0) < t.count:
            self.pe.obj.wait_ge(t.sem, t.count)
            self.pe.seen[t] = t.count
            self.n_wait += 1

    def barrier(self):
        for e in self.engs:
            for o in self.engs:
                if o.track.count:
                    self._wait(e, o.track, o.track.count)
            for tr in self.dma_tracks:
                if tr.count:
                    self._wait(e, tr, tr.count * 16)

    def finish(self, eng=None):
        eng = eng or self.sp
        for o in self.engs:
            if o.track.count and o is not eng:
                self._wait(eng, o.track, o.track.count)
        for tr in self.dma_tracks:
            if tr.count:
                self._wait(eng, tr, tr.count * 16)


def bcast(ap, pos, n):
    l = [list(d) for d in ap.ap]
    l.insert(pos, [0, n])
    return bass.AP(ap.tensor, ap.offset, l)


def host_consts():
    c = {}
    inv = (np.float32(10000.0) ** (-np.arange(0, 64, 2, dtype=np.float32) / np.float32(64))).astype(np.float32)
    pos = np.arange(S_LEN, dtype=np.float32)
    ang = (pos[:, None] * inv[None, :]).astype(np.float32)
    cs = np.stack([np.cos(ang), np.sin(ang)], axis=1).astype(np.float32)
    csf = np.stack([np.concatenate([np.cos(ang), np.cos(ang)], axis=1),
                    np.concatenate([-np.sin(ang), np.sin(ang)], axis=1)], axis=1).astype(np.float32)
    c["rope_cs"] = np.ascontiguousarray(csf.reshape(NT, 128, 2, 64).transpose(1, 0, 2, 3))
    endp = (np.arange(127) * 16 + 31).astype(np.float32)
    angc = (endp[:, None] * inv[None, :]).astype(np.float32)
    rc = np.zeros((128, 2, 64), np.float32)
    rc[:127, 0] = np.concatenate([np.cos(angc), np.cos(angc)], axis=1)
    rc[:127, 1] = np.concatenate([-np.sin(angc), np.sin(angc)], axis=1)
    c["rope_c"] = rc
    t = np.arange(S_LEN)
    cm = np.where((np.arange(128)[None, :] * 16 + 31 <= t[:, None]) & (np.arange(128)[None, :] < 127), 0.0, NEG_BIG)
    c["cmask"] = np.ascontiguousarray(cm.astype(np.float32).reshape(NT, 128, 128).transpose(1, 0, 2))
    tb = (t // 64)[:, None]
    jj = np.arange(32)[None, :]
    forced = (jj == 0) | (jj == tb) | (jj == tb - 1)
    sb_ = np.where(jj <= tb, np.where(forced, 1e4, 0.0), NEG).astype(np.float32)
    c["selbias"] = np.ascontiguousarray(sb_.reshape(NT, 128, 32).transpose(1, 0, 2))
    E = np.zeros((32, 16, 128), np.float32)
    for kc in range(16):
        for k in range(128):
            E[2 * kc + k // 64, kc, k] = 1.0
    c["e_nsa"] = E
    p = np.arange(128)[:, None, None]
    d = np.arange(4)[None, :, None]
    j = np.arange(512)[None, None, :]
    c["caus"] = np.where(p + 128 * d <= j, 0.0, NEG_BIG).astype(np.float32)
    e = (np.arange(8) - 3)[None, :, None]
    dl = j - p + 128 * e
    c["wmask"] = np.where((dl >= 0) & (dl < 512), 0.0, NEG_BIG).astype(np.float32)
    blk = np.arange(8)[None, :]
    c["pastbias"] = np.ascontiguousarray(
        np.where(blk < (t // 256)[:, None], 0.0, NEG).astype(np.float32).reshape(NT, 128, 8).transpose(1, 0, 2))
    c["ownsel"] = np.ascontiguousarray(
        (blk == (t // 256)[:, None]).astype(np.float32).reshape(NT, 128, 8).transpose(1, 0, 2))
    ef = np.zeros((32, S_LEN), np.float32)
    ef[np.arange(S_LEN) // 64, np.arange(S_LEN)] = 1.0
    c["e_full"] = ef
    e8 = np.zeros((4, 32, S_LEN), np.float32)
    for j_ in range(4):
        e8[j_, 8 * j_ + np.arange(S_LEN) // 256, np.arange(S_LEN)] = 1.0
    c["e8"] = e8
    E2 = np.zeros((64, 64, 128), np.float32)
    for i in range(64):
        E2[i, i, :] = 1.0
    c["e_moba"] = E2
    return c


CONST_SHAPES = {
    "rope_cs": (128, 16, 2, 64), "rope_c": (128, 2, 64), "cmask": (128, 16, 128), "selbias": (128, 16, 32),
    "caus": (128, 4, 512), "wmask": (128, 8, 512), "pastbias": (128, 16, 8),
    "ownsel": (128, 16, 8), "e_full": (32, 2048), "e8": (4, 32, 2048),
}

WEIGHT_SHAPES = {
    "ffn1_norm": (2, 1024), "ffn1_wg": (2, 1024, 2816), "ffn1_wu": (2, 1024, 2816), "ffn1_wd": (2, 2816, 1024),
    "mix_norm": (2, 1024), "w_in": (2, 1024, 4888),
    "cmpk_pos": (2, 32, 64), "cmpk_w1": (2, 2048, 128), "cmpk_w2": (2, 128, 64),
    "cmpv_pos": (2, 32, 64), "cmpv_w1": (2, 2048, 128), "cmpv_w2": (2, 128, 64),
    "w_branch_nsa": (2, 512, 1024), "w_branch_moba": (2, 512, 1024), "w_out": (2, 1024, 1024),
    "ffn2_norm": (2, 1024), "ffn2_wg": (2, 1024, 2816), "ffn2_wu": (2, 1024, 2816), "ffn2_wd": (2, 2816, 1024),
    "final_norm": (1, 1024),
}


def build_program(n_layers=2, stages=("ffn1", "mix", "ffn2"), dbg=()):
    nc = bass.Bass("TRN2", target_bir_lowering=False)
    D = {}
    for name, shp in WEIGHT_SHAPES.items():
        D[name] = nc.dram_tensor(name, list(shp), F32, kind="ExternalInput").ap()
    for name, shp in CONST_SHAPES.items():
        D[name] = nc.dram_tensor(name, list(shp), F32, kind="ExternalInput").ap()
    x_d = nc.dram_tensor("x", [S_LEN, DM], F32, kind="ExternalInput").ap()
    out_d = nc.dram_tensor("out", [S_LEN, DM], F32, kind="ExternalOutput").ap()
    DBG = {}

    with ExitStack() as top:
        S = Sched(nc, top)
        cnt = [0]

        def sbuf(st, shape, dt, name=None):
            cnt[0] += 1
            return st.enter_context(nc.sbuf_tensor("%s_%d" % (name or "t", cnt[0]), list(shape), dt))

        def psum(st, shape, dt, name=None):
            cnt[0] += 1
            full = 512 if dt == F32 else 1024
            t = st.enter_context(nc.psum_tensor("%s_%d" % (name or "p", cnt[0]), [128, full], dt))
            shape = list(shape)
            n = 1
            for d_ in shape[1:]:
                n *= d_
            v = t[0:shape[0], 0:n]
            if len(shape) == 3:
                v = v.rearrange("p (a b) -> p a b", a=shape[1])
            elif len(shape) == 4:
                v = v.rearrange("p (a b c) -> p a b c", a=shape[1], b=shape[2])
            return v

        x_sb = sbuf(top, [128, NT, DM], F32, "x")
        xB = [Buf("x%d" % t) for t in range(NT)]
        ident = sbuf(top, [128, 128], BF16, "ident")
        identf = sbuf(top, [128, 128], F32, "identf")
        identB = Buf("ident")
        rope = sbuf(top, [128, NT, 2, 64], F32, "rope")
        ropeB = Buf("rope")

        for t in range(NT):
            S.dma(S.sp if t % 2 == 0 else S.pool, x_sb[:, t, :], x_d[t * 128:(t + 1) * 128, :], writes=[xB[t]])
        S.dma(S.sp, rope[:], D["rope_cs"], writes=[ropeB])
        S.op(S.pool, lambda e: e.memset(identf[:], 1.0), writes=[identB])
        S.op(S.pool, lambda e: e.affine_select(identf[:], identf[:], [[-1, 128]], ALU.is_equal, 0.0,
                                               base=0, channel_multiplier=1), reads=[identB], writes=[identB])
        S.op(S.dve, lambda e: e.tensor_copy(ident[:], identf[:]), reads=[identB], writes=[identB])

        def dbg_out(name, ap, reads):
            if name not in dbg:
                return
            shp = list(ap.shape)
            dt_ = ap.dtype
            d = nc.dram_tensor("dbg_" + name, shp, dt_, kind="ExternalOutput").ap()
            DBG[name] = d
            S.dma(S.sp, d, ap, reads=reads)

        def norm_T(gain_row, hT, hTB):
            with ExitStack() as st:
                gbc = sbuf(st, [128, DM], F32, "gbc")
                ss = sbuf(st, [128, NT], F32, "ss")
                sd = sbuf(st, [128, NT], F32, "sd")
                rstd = sbuf(st, [128, NT], F32, "rstd")
                junk = sbuf(st, [128, DM], BF16, "junk")
                hn = [sbuf(st, [128, DM], BF16, "hn") for _ in range(2)]
                tp = [psum(st, [128, 8, 128], BF16, "tp") for _ in range(2)]
                gB, ss0B, junkB = Buf(), Buf(), Buf()
                ssB = [Buf() for _ in range(NT)]
                hnB = [Buf(), Buf()]
                tpB = [PBuf(), PBuf()]
                S.dma(S.sp, gbc[:], gain_row.partition_broadcast(128), writes=[gB])
                S.op(S.dve, lambda e: e.memset(ss[:], 0.0), writes=ssB)

                def stats(t):
                    S.op(S.act, lambda e: e.activation(junk[:], x_sb[:, t, :], AF.Square, accum_out=ss[:, t:t + 1]),
                         reads=[xB[t]], writes=[junkB, ssB[t]])
                    S.op(S.act, lambda e: e.activation(sd[:, t:t + 1], ss[:, t:t + 1], AF.Sqrt, bias=EPS_AP[:, 0:1], scale=1.0 / DM),
                         reads=[ssB[t], epsB], writes=[ssB[t]])
                    S.op(S.dve, lambda e: e.reciprocal(rstd[:, t:t + 1], sd[:, t:t + 1]), reads=[ssB[t]], writes=[ssB[t]])
                stats(0)
                stats(1)
                for t in range(NT):
                    i = t % 2
                    if t + 2 < NT:
                        stats(t + 2)
                    S.op(S.dve, lambda e: e.scalar_tensor_tensor(hn[i][:], x_sb[:, t, :], rstd[:, t:t + 1], gbc[:],
                                                                 ALU.mult, ALU.mult),
                         reads=[xB[t], ssB[t], gB], writes=[hnB[i]])
                    for c in range(8):
                        S.op(S.pe, lambda e: e.transpose(tp[i][:, c, :], hn[i][:, c * 128:(c + 1) * 128], ident[:]),
                             reads=[hnB[i], identB], writes=[tpB[i]])
                    S.op(S.act, lambda e: e.activation(hT[:, :, t * 128:(t + 1) * 128], tp[i][:], AF.Copy),
                         reads=[tpB[i]], writes=[hTB[t]])
                S.barrier()

        def make_tile_norm(st, gain_row, shared=None):
            gbc = sbuf(st, [128, DM], F32, "gbc")
            hn = [sbuf(st, [128, DM], BF16, "hn") for _ in range(2)]
            hTt = [sbuf(st, [128, 8, 128], BF16, "hTt") for _ in range(2)]
            tph = psum(st, [128, 8, 128], BF16, "tph")
            gB, tphB = Buf(), PBuf()
            hnB = [Buf(), Buf()]
            hTtB = [Buf(), Buf()]
            S.dma(S.sp, gbc[:], gain_row.partition_broadcast(128), writes=[gB])
            have = shared is not None and shared.get("done")
            if shared is None:
                shared = {}
            if not have:
                if "rstd" not in shared:
                    shared["rstd"] = sbuf(st, [128, NT], F32, "rstd")
                    shared["B"] = [Buf() for _ in range(NT)]
                ss = sbuf(st, [128, NT], F32, "ss")
                sd = sbuf(st, [128, NT], F32, "sd")
                junk = sbuf(st, [128, DM], BF16, "junk")
                junkB = Buf()
                S.op(S.dve, lambda e: e.memset(ss[:], 0.0), writes=shared["B"])
            rstd = shared["rstd"]
            ssB = shared["B"]
            started = set()

            def stats(t):
                if have or t in started or t >= NT:
                    return
                started.add(t)
                S.op(S.act, lambda e: e.activation(junk[:], x_sb[:, t, :], AF.Square, accum_out=ss[:, t:t + 1]),
                     reads=[xB[t]], writes=[junkB, ssB[t]])
                S.op(S.act, lambda e: e.activation(sd[:, t:t + 1], ss[:, t:t + 1], AF.Sqrt, bias=EPS_AP[:, 0:1], scale=1.0 / DM),
                     reads=[ssB[t], epsB], writes=[ssB[t]])
                S.op(S.dve, lambda e: e.reciprocal(rstd[:, t:t + 1], sd[:, t:t + 1]), reads=[ssB[t]], writes=[ssB[t]])
            stats(0)
            stats(1)
            shared["done"] = True

            def tile(t):
                i = t % 2
                stats(t)
                stats(t + 1)
                stats(t + 2)
                S.op(S.dve, lambda e: e.scalar_tensor_tensor(hn[i][:], x_sb[:, t, :], rstd[:, t:t + 1], gbc[:],
                                                             ALU.mult, ALU.mult),
                     reads=[xB[t], ssB[t], gB], writes=[hnB[i]])
                for c in range(8):
                    S.op(S.pe, lambda e: e.transpose(tph[:, c, :], hn[i][:, c * 128:(c + 1) * 128], ident[:]),
                         reads=[hnB[i], identB], writes=[tphB])
                S.op(S.act, lambda e: e.activation(hTt[i][:], tph[:], AF.Copy), reads=[tphB], writes=[hTtB[i]])
                return hTt[i], hTtB[i]
            return tile

        def ffn(l, gain, wg, wu, wd, passes=((0, 11), (11, 22))):
            with ExitStack() as st:
                hT = sbuf(st, [128, 8, S_LEN], BF16, "hT")
                hTB = [Buf() for _ in range(NT)]
                wg_v = wg[l].rearrange("(c p) f -> p c f", p=128)
                wu_v = wu[l].rearrange("(c p) f -> p c f", p=128)
                wd_v = wd[l].rearrange("(c p) d -> p c d", p=128)
                wgs = [sbuf(st, [128, 8, 256], BF16, "wg") for _ in range(2)]
                wus = [sbuf(st, [128, 8, 256], BF16, "wu") for _ in range(2)]
                wgB = [Buf(), Buf()]
                wuB = [Buf(), Buf()]
                sg = [sbuf(st, [128, 512], F32, "sg") for _ in range(2)]
                sgB = [Buf(), Buf()]
                pg = [psum(st, [128, 512], F32, "pg") for _ in range(2)]
                pu = [psum(st, [128, 512], F32, "pu") for _ in range(2)]
                pd = [psum(st, [128, 512], F32, "pd") for _ in range(2)]
                pgB = [PBuf(), PBuf()]
                puB = [PBuf(), PBuf()]
                pdB = [PBuf(), PBuf()]
                npmax = max(b - a for a, b in passes)
                actT = sbuf(st, [128, npmax, S_LEN], BF16, "actT")
                wds = sbuf(st, [128, npmax, DM], BF16, "wd")
                actB = [[Buf() for _ in range(4)] for _ in range(npmax)]
                wdB = [Buf() for _ in range(npmax)]
                cnt_ = {"gi": 0, "ei": 0, "di": 0}

                def load_group(f0, fi, gsz):
                    b = cnt_["gi"] % 2
                    cnt_["gi"] += 1
                    c0 = (f0 + fi) * 128
                    S.dma(S.pool, wgs[b][:, :, 0:gsz * 128], wg_v[:, :, c0:c0 + gsz * 128], writes=[wgB[b]])
                    S.dma(S.pool, wus[b][:, :, 0:gsz * 128], wu_v[:, :, c0:c0 + gsz * 128], writes=[wuB[b]])
                    return b

                def gate_up(b, fidx, fo, tb):
                    k = cnt_["ei"] % 2
                    cnt_["ei"] += 1
                    hr = [hTB[4 * tb + q] for q in range(4)]
                    S.mm(pg[k][:], [(wgs[b][:, c, fo * 128:(fo + 1) * 128], hT[:, c, tb * 512:(tb + 1) * 512])
                                    for c in range(8)], reads=[wgB[b]] + hr, writes=[pgB[k]])
                    S.mm(pu[k][:], [(wus[b][:, c, fo * 128:(fo + 1) * 128], hT[:, c, tb * 512:(tb + 1) * 512])
                                    for c in range(8)], reads=[wuB[b]] + hr, writes=[puB[k]])
                    S.op(S.act, lambda e: e.activation(sg[k][:], pg[k][:], AF.Silu), reads=[pgB[k]], writes=[sgB[k]])
                    S.op(S.dve, lambda e: e.tensor_tensor(actT[:, fidx, tb * 512:(tb + 1) * 512], sg[k][:], pu[k][:], ALU.mult),
                         reads=[sgB[k], puB[k]], writes=[actB[fidx][tb]])

                f0_, f1_ = passes[0]
                g0sz = min(2, f1_ - f0_)
                b0 = load_group(f0_, 0, g0sz)
                gbc = sbuf(st, [128, DM], F32, "gbc")
                ss = sbuf(st, [128, NT], F32, "ss")
                sd = sbuf(st, [128, NT], F32, "sd")
                rstd = sbuf(st, [128, NT], F32, "rstd")
                junk = sbuf(st, [128, DM], BF16, "junk")
                hn = [sbuf(st, [128, DM], BF16, "hn") for _ in range(2)]
                tp = [psum(st, [128, 8, 128], BF16, "tp") for _ in range(2)]
                gB, junkB = Buf(), Buf()
                ssB = [Buf() for _ in range(NT)]
                hnB = [Buf(), Buf()]
                tpB = [PBuf(), PBuf()]
                S.dma(S.sp, gbc[:], gain[l:l + 1, :].partition_broadcast(128), writes=[gB])
                S.op(S.dve, lambda e: e.memset(ss[:], 0.0), writes=ssB)

                def stats(t):
                    S.op(S.act, lambda e: e.activation(junk[:], x_sb[:, t, :], AF.Square, accum_out=ss[:, t:t + 1]),
                         reads=[xB[t]], writes=[junkB, ssB[t]])
                    S.op(S.act, lambda e: e.activation(sd[:, t:t + 1], ss[:, t:t + 1], AF.Sqrt, bias=EPS_AP[:, 0:1], scale=1.0 / DM),
                         reads=[ssB[t], epsB], writes=[ssB[t]])
                    S.op(S.dve, lambda e: e.reciprocal(rstd[:, t:t + 1], sd[:, t:t + 1]), reads=[ssB[t]], writes=[ssB[t]])
                stats(0)
                stats(1)
                for t in range(NT):
                    i = t % 2
                    if t + 2 < NT:
                        stats(t + 2)
                    S.op(S.dve, lambda e: e.scalar_tensor_tensor(hn[i][:], x_sb[:, t, :], rstd[:, t:t + 1], gbc[:],
                                                                 ALU.mult, ALU.mult),
                         reads=[xB[t], ssB[t], gB], writes=[hnB[i]])
                    for c in range(8):
                        S.op(S.pe, lambda e: e.transpose(tp[i][:, c, :], hn[i][:, c * 128:(c + 1) * 128], ident[:]),
                             reads=[hnB[i], identB], writes=[tpB[i]])
                    S.op(S.act, lambda e: e.activation(hT[:, :, t * 128:(t + 1) * 128], tp[i][:], AF.Copy),
                         reads=[tpB[i]], writes=[hTB[t]])
                    if t % 4 == 3 and t >= 7:
                        for fo in range(g0sz):
                            gate_up(b0, fo, fo, t // 4 - 1)
                for fo in range(g0sz):
                    gate_up(b0, fo, fo, 3)

                first = True
                for (f0, f1) in passes:
                    nf = f1 - f0
                    wd_issued = False
                    fi = 0
                    while fi < nf:
                        gsz = min(2, nf - fi)
                        if first:
                            b = b0
                        else:
                            b = load_group(f0, fi, gsz)
                        if not wd_issued:
                            wd_issued = True
                            for fj in range(nf):
                                S.dma(S.pool, wds[:, fj, :], wd_v[:, f0 + fj, :], writes=[wdB[fj]])
                        if not first:
                            for fo in range(gsz):
                                for tb in range(4):
                                    gate_up(b, fi + fo, fo, tb)
                        first = False
                        fi += gsz
                    for t in range(NT):
                        for hf in range(2):
                            k = cnt_["di"] % 2
                            cnt_["di"] += 1
                            S.mm(pd[k][:], [(actT[:, fi, t * 128:(t + 1) * 128], wds[:, fi, hf * 512:(hf + 1) * 512])
                                            for fi in range(nf)],
                                 reads=[actB[fi][t // 4] for fi in range(nf)] + wdB[0:nf], writes=[pdB[k]])
                            S.op(S.dve, lambda e: e.scalar_tensor_tensor(
                                x_sb[:, t, hf * 512:(hf + 1) * 512], pd[k][:], 0.5,
                                x_sb[:, t, hf * 512:(hf + 1) * 512], ALU.mult, ALU.add),
                                reads=[pdB[k], xB[t]], writes=[xB[t]])
                S.barrier()

        EPS_AP = sbuf(top, [128, 1], F32, "eps")
        epsB = Buf("eps")
        S.op(S.dve, lambda e: e.memset(EPS_AP[:], EPS), writes=[epsB])

        from_mixer = {}
        for l in range(n_layers):
            if "ffn1" in stages:
                ffn(l, D["ffn1_norm"], D["ffn1_wg"], D["ffn1_wu"], D["ffn1_wd"])
            if "mix" in stages:
                mixer(nc, S, D, l, x_sb, xB, ident, identf, identB, rope, ropeB, make_tile_norm, sbuf, psum, dbg_out, EPS_AP, epsB)
            if "ffn2" in stages:
                ffn(l, D["ffn2_norm"], D["ffn2_wg"], D["ffn2_wu"], D["ffn2_wd"])

        S.budget = None
        with ExitStack() as st:
            gbc = sbuf(st, [128, DM], F32, "gbcf")
            ss = sbuf(st, [128, NT], F32, "ssf")
            sd = sbuf(st, [128, NT], F32, "sdf")
            rstd = sbuf(st, [128, NT], F32, "rstdf")
            junk = sbuf(st, [128, DM], BF16, "junkf")
            yo = [sbuf(st, [128, DM], F32, "yo") for _ in range(2)]
            gB, ssB, junkB = Buf(), Buf(), Buf()
            yB = [Buf(), Buf()]
            S.dma(S.sp, gbc[:], D["final_norm"][0:1, :].partition_broadcast(128), writes=[gB])
            S.op(S.dve, lambda e: e.memset(ss[:], 0.0), writes=[ssB])
            for t in range(NT):
                S.op(S.act, lambda e: e.activation(junk[:], x_sb[:, t, :], AF.Square, accum_out=ss[:, t:t + 1]),
                     reads=[xB[t]], writes=[junkB, ssB])
            S.op(S.act, lambda e: e.activation(sd[:], ss[:], AF.Sqrt, bias=EPS_AP[:, 0:1], scale=1.0 / DM),
                 reads=[ssB, epsB], writes=[ssB])
            S.op(S.dve, lambda e: e.reciprocal(rstd[:], sd[:]), reads=[ssB], writes=[ssB])
            for t in range(NT):
                i = t % 2
                S.op(S.dve, lambda e: e.scalar_tensor_tensor(yo[i][:], x_sb[:, t, :], rstd[:, t:t + 1], gbc[:],
                                                             ALU.mult, ALU.mult),
                     reads=[xB[t], ssB, gB], writes=[yB[i]])
                S.dma(S.sp, out_d[t * 128:(t + 1) * 128, :], yo[i][:], reads=[yB[i]])
            S.finish()
        print("program: n_inst=%d n_wait=%d" % (S.n_inst, S.n_wait))
    return nc, DBG


def rope_apply(S, st_tmp, src, H, cs, dst, reads, writes, dup=1):
    k = st_tmp["i"][0] % 2
    st_tmp["i"][0] += 1
    t1, t2 = st_tmp["t"][k]
    tB = st_tmp["B"][k]
    np_ = src.shape[0]
    a1 = t1[0:np_, 0:H, :]
    a2 = t2[0:np_, 0:H, :]
    S.op(S.dve, lambda e: e.tensor_tensor(a1, src, bcast(cs[0], 1, H), ALU.mult), reads=reads, writes=[tB[0]])
    S.op(S.dve, lambda e: e.tensor_tensor(a2[:, :, 0:32], src[:, :, 32:64], bcast(cs[1][:, 0:32], 1, H), ALU.mult),
         reads=reads, writes=[tB[1]])
    S.op(S.dve, lambda e: e.tensor_tensor(a2[:, :, 32:64], src[:, :, 0:32], bcast(cs[1][:, 32:64], 1, H), ALU.mult),
         reads=reads, writes=[tB[1]])
    if dup > 1:
        b1, b2 = bcast(a1, 2, dup), bcast(a2, 2, dup)
    else:
        b1, b2 = a1, a2
    S.op(S.pool, lambda e: e.tensor_tensor(dst, b1, b2, ALU.add), reads=[tB[0], tB[1]], writes=writes)


STOP = [None]


def run_pipeline(items, la=2, filler=None, per_item=0, drain=True, epi_delay=2):
    n = len(items)
    pending = []
    for i in range(n + la):
        if i < n:
            items[i][0]()
        j = i - la
        if j >= 0:
            items[j][1]()
            while pending and pending[0][0] <= j:
                pending.pop(0)[1]()
            items[j][2]()
            if items[j][3] is not None:
                items[j][3]()
                pending.append((j + epi_delay, items[j][4]))
            if filler is not None:
                for _ in range(per_item):
                    if next(filler, "done") == "done":
                        filler = None
                        break
    for _, fn in pending:
        fn()
    if filler is not None and drain:
        for _ in filler:
            pass


def mixer(nc, S, D, l, x_sb, xB, ident, identf, identB, rope, ropeB, make_tile_norm, sbuf, psum, dbg_out, EPS_AP, epsB):
    win_v = D["w_in"][l].rearrange("(c p) n -> p c n", p=128)
    with ExitStack() as ms:
        nsaT = sbuf(ms, [128, 4, S_LEN], BF16, "nsaT")
        nsaTB = [Buf() for _ in range(NT)]
        mobaTB = [Buf() for _ in range(NT)]
        nstats = {"rstd": sbuf(ms, [128, NT], F32, "rstd_mix"), "B": [Buf() for _ in range(NT)]}

        def make_rt(st_):
            return {"t": [(sbuf(st_, [128, 8, 64], F32, "rt1"), sbuf(st_, [128, 8, 64], F32, "rt2")) for _ in range(2)],
                    "B": [(Buf(), Buf()) for _ in range(2)], "i": [0]}

        with ExitStack() as ns:
            qaT = sbuf(ns, [128, 8, S_LEN], BF16, "qaT")
            kskwT = sbuf(ns, [128, 4, S_LEN], BF16, "kskwT")
            vsw = sbuf(ns, [128, NT, 4, 65], BF16, "vsw")
            ga = sbuf(ns, [128, NT, 24], F32, "ga")
            kcbT = sbuf(ns, [128, 2, 128], BF16, "kcbT")
            vcb = sbuf(ns, [128, 2, 64], BF16, "vcb")
            kcbTB, vcbB = Buf(), Buf()
            selRB = [[Buf() for _ in range(NT)] for _ in range(2)]
            eB_ = Buf()
            S.op(S.pool, lambda e: e.memset(qaT[64:96, :, :], 0.0), writes=[b_ for g_ in selRB for b_ in g_])
            S.op(S.pool, lambda e: e.memset(kskwT[64:96, 2:4, :], 0.0), writes=[eB_])
            S.op(S.pool, lambda e: e.memset(kcbT[:], 0.0), writes=[kcbTB])
            for g_ in range(2):
                S.dma(S.pool, kskwT[64:96, g_, :], D["e_full"], writes=[eB_])
            ks_scope = ns.enter_context(ExitStack())
            kcvcT = sbuf(ks_scope, [128, 2, S_LEN], BF16, "kcvcT")
            qaB = [Buf() for _ in range(NT)]
            kkB = [Buf() for _ in range(NT)]
            kcB = [Buf() for _ in range(NT)]
            vswB = [Buf() for _ in range(NT)]
            gaB = [Buf() for _ in range(NT)]
            S.op(S.pool, lambda e: e.memset(vsw[:], 1.0), writes=vswB)
            with ExitStack() as st:
                rt = make_rt(st)
                tnorm = make_tile_norm(st, D["mix_norm"][l:l + 1, :], nstats)
                wn = sbuf(st, [128, 8, 1304], BF16, "wn")
                wnB = [Buf() for _ in range(3)]
                segs = [(0, 0, 512, 0), (512, 536, 664, 1), (640, 664, 792, 1), (768, 792, 920, 1), (896, 1048, 1176, 1),
                        (1024, 920, 1048, 2), (1152, 1176, 1304, 2), (1280, 512, 536, 2)]
                for (dst, a, b, wb) in segs:
                    S.dma(S.pool, wn[:, :, dst:dst + (b - a)], win_v[:, :, a:b], reads=[], writes=[wnB[wb]])
                pA = [psum(st, [128, 512], F32, "pA") for _ in range(2)]
                pBk = [psum(st, [128, 512], F32, "pB")] * 2
                pC = psum(st, [128, 512], F32, "pC")
                tq = psum(st, [128, 8, 128], BF16, "tq")
                tkk = psum(st, [128, 6, 128], BF16, "tkk")
                tk = tkk[:, 0:4, :]
                tkv = tkk[:, 4:6, :]
                pAB = [PBuf(), PBuf()]
                pBB = [PBuf()] * 2
                pCB, tqB, tkB = PBuf(), PBuf(), PBuf()
                tkvB = tkB
                qr = [sbuf(st, [128, 8, 128], BF16, "qr") for _ in range(2)]
                kr = [sbuf(st, [128, 4, 128], BF16, "kr") for _ in range(2)]
                kv = [sbuf(st, [128, 256], BF16, "kv") for _ in range(2)]
                qrB = [Buf(), Buf()]
                krB = [Buf(), Buf()]
                kvB = [Buf(), Buf()]
                for i_ in range(2):
                    S.op(S.pool, lambda e: e.memset(qr[i_][:], 0.0), writes=[qrB[i_]])
                    S.op(S.pool, lambda e: e.memset(kr[i_][:], 0.0), writes=[krB[i_]])
                hts = {}

                def s_norm(t):
                    hts[t] = tnorm(t)

                def s_pa(t):
                    hTt, hTtB = hts[t]
                    S.mm(pA[t % 2][:], [(hTt[:, c, :], wn[:, c, 0:512]) for c in range(8)], reads=[hTtB, wnB[0]], writes=[pAB[t % 2]])

                def s_pbc(t):
                    hTt, hTtB = hts[t]
                    S.mm(pBk[0][:], [(hTt[:, c, :], wn[:, c, 512:1024]) for c in range(8)], reads=[hTtB, wnB[1]], writes=[pBB[0]])
                    S.mm(pC[:, 0:280], [(hTt[:, c, :], wn[:, c, 1024:1304]) for c in range(8)], reads=[hTtB, wnB[2]], writes=[pCB])

                def s_post(t):
                    i = t % 2
                    tsl = slice(t * 128, (t + 1) * 128)
                    cs = (rope[:, t, 0, :], rope[:, t, 1, :])
                    rope_apply(S, rt, pA[i][:].rearrange("p (h d) -> p h d", h=8), 8, cs, qr[i][:, :, 0:64],
                               reads=[pAB[i], ropeB], writes=[qrB[i]])
                    S.op(S.act, lambda e: e.activation(kv[i][:], pBk[0][:, 0:256], AF.Copy), reads=[pBB[0]], writes=[kvB[i]])
                    rope_apply(S, rt, pBk[0][:, 256:512].rearrange("p (h d) -> p h d", h=4), 4, cs,
                               kr[i][:, :, 0:64], reads=[pBB[0], ropeB], writes=[krB[i]])
                    S.op(S.act, lambda e: e.activation(vsw[:, t, :, 0:64], pC[:, 0:256].rearrange("p (h d) -> p h d", h=4), AF.Copy),
                         reads=[pCB], writes=[vswB[t]])
                    S.op(S.act, lambda e: e.activation(ga[:, t, :], pC[:, 256:280], AF.Sigmoid), reads=[pCB], writes=[gaB[t]])
                    for c in range(8):
                        S.op(S.pe, lambda e: e.transpose(tq[:, c, :], qr[i][:, c, :], ident[:]),
                             reads=[qrB[i], identB], writes=[tqB])
                    S.op(S.act, lambda e: e.activation(qaT[0:64, :, tsl], tq[0:64, :, :], AF.Copy), reads=[tqB], writes=[qaB[t]])
                    for c in range(2):
                        S.op(S.pe, lambda e: e.transpose(tkv[:, c, :], kv[i][:, c * 128:(c + 1) * 128], ident[:]),
                             reads=[kvB[i], identB], writes=[tkvB])
                    S.op(S.act, lambda e: e.activation(kcvcT[:, :, tsl], tkv, AF.Copy), reads=[tkvB], writes=[kcB[t]])
                    for c in range(4):
                        S.op(S.pe, lambda e: e.transpose(tk[:, c, :], kr[i][:, c, :], ident[:]),
                             reads=[krB[i], identB], writes=[tkB])
                    S.op(S.act, lambda e: e.activation(kskwT[0:64, :, tsl], tk[0:64, :, :], AF.Copy), reads=[tkB], writes=[kkB[t]])

                print('SBUF remaining in NSA proj:', nc.sbuf_bytes_remaining)
                s_norm(0)
                s_pa(0)
                s_pbc(0)
                s_norm(1)
                for t in range(NT):
                    if t + 2 < NT:
                        s_norm(t + 2)
                    if t + 1 < NT:
                        s_pa(t + 1)
                    s_post(t)
                    if t + 1 < NT:
                        s_pbc(t + 1)
                S.barrier()
            if STOP[0] == "nsaproj":
                return
            dbg_out("qaT", qaT[:], qaB)
            dbg_out("kskwT", kskwT[:], kkB)
            dbg_out("ga", ga[:], gaB)

            with ExitStack() as st:
                rt = make_rt(st)
                w1 = [sbuf(st, [128, 32, 128], BF16, "w1") for _ in range(2)]
                posT = [sbuf(st, [128, 32], BF16, "posT") for _ in range(2)]
                posj = [sbuf(st, [32, 64], BF16, "posj") for _ in range(2)]
                w2 = [sbuf(st, [128, 64], BF16, "w2") for _ in range(2)]
                ropec = sbuf(st, [128, 2, 64], F32, "ropec")
                wB = Buf()
                for kvi, nm in enumerate(("cmpk", "cmpv")):
                    w1v = D[nm + "_w1"][l].rearrange("(j d) h -> d j h", d=64)
                    for hlf in range(2):
                        S.dma(S.pool, w1[kvi][hlf * 64:(hlf + 1) * 64, :, :], w1v, writes=[wB])
                    S.dma(S.pool, posj[kvi][:], D[nm + "_pos"][l], writes=[wB])
                    S.dma(S.pool, w2[kvi][:], D[nm + "_w2"][l], writes=[wB])
                S.dma(S.sp, ropec[:], D["rope_c"], writes=[wB])
                ph = [psum(st, [128, 128], F32, "ph") for _ in range(2)]
                pb = psum(st, [128, 8], F32, "pb")
                pk = psum(st, [128, 64], F32, "pk")
                tpk = psum(st, [128, 128], BF16, "tpk")
                phB = [PBuf(), PBuf()]
                pbB, pkB, tpkB = PBuf(), PBuf(), PBuf()
                bias = sbuf(st, [128, 2], F32, "bias")
                biasB = Buf()
                hb = sbuf(st, [128, 128], F32, "hb")
                h2 = sbuf(st, [128, 128], F32, "h2")
                uu = sbuf(st, [128, 128], F32, "uu")
                gl = sbuf(st, [128, 128], BF16, "gl")
                kcr = sbuf(st, [128, 2, 64], BF16, "kcr")
                hbB, h2B, uuB, glB, kcrB = Buf(), Buf(), Buf(), Buf(), Buf()
                posTB = Buf()
                for kvi in range(2):
                    S.op(S.pe, lambda e: e.transpose(tpk[0:64, 0:32], posj[kvi][:], ident[0:32, 0:32]), reads=[wB, identB], writes=[tpkB])
                    S.op(S.act, lambda e: e.activation(posT[kvi][0:64, :], tpk[0:64, 0:32], AF.Copy), reads=[tpkB], writes=[posTB])
                for kvi in range(2):
                    S.mm(pb[:, kvi:kvi + 1], [(w1[kvi][0:64, j, :], posT[kvi][0:64, j:j + 1]) for j in range(32)],
                         reads=[wB, posTB], writes=[pbB])
                S.op(S.act, lambda e: e.activation(bias[:], pb[:, 0:2], AF.Copy), reads=[pbB], writes=[biasB])
                it = 0
                for kvi in range(2):
                    for g in range(2):
                        k = it % 2
                        it += 1
                        pairs = []
                        for j in range(32):
                            base = kcvcT[g * 64:(g + 1) * 64, kvi, j:j + 1]
                            rhs = bass.AP(base.tensor, base.offset, [list(base.ap[0]), [16, 127]])
                            pairs.append((w1[kvi][g * 64:(g + 1) * 64, j, :], rhs))
                        S.mm(ph[k][:, 0:127], pairs, reads=[wB] + kcB, writes=[phB[k]])
                        hv, h2v, uv = hb[:, 0:127], h2[:, 0:127], uu[:, 0:127]
                        S.op(S.act, lambda e: e.activation(hv, ph[k][:, 0:127], AF.Identity, bias=bias[:, kvi:kvi + 1]),
                             reads=[phB[k], biasB], writes=[hbB])
                        S.op(S.dve, lambda e: e.tensor_tensor(h2v, hv, hv, ALU.mult), reads=[hbB], writes=[h2B])
                        S.op(S.dve, lambda e: e.tensor_scalar(h2v, h2v, 0.044715, 1.0, ALU.mult, ALU.add), reads=[h2B], writes=[h2B])
                        S.op(S.dve, lambda e: e.tensor_tensor(uv, h2v, hv, ALU.mult), reads=[h2B, hbB], writes=[uuB])
                        S.op(S.act, lambda e: e.activation(uv, uv, AF.Exp, scale=-1.5957691216057308), reads=[uuB], writes=[uuB])
                        S.op(S.dve, lambda e: e.tensor_scalar(uv, uv, 1.0, None, ALU.add), reads=[uuB], writes=[uuB])
                        S.op(S.dve, lambda e: e.reciprocal(uv, uv), reads=[uuB], writes=[uuB])
                        S.op(S.dve, lambda e: e.tensor_tensor(gl[:, 0:127], hv, uv, ALU.mult), reads=[uuB, hbB], writes=[glB])
                        S.mm(pk[0:127, :], [(gl[:, 0:127], w2[kvi][:])], reads=[glB, wB], writes=[pkB])
                        if kvi == 0:
                            rope_apply(S, rt, pk[0:127, :].rearrange("p (h d) -> p h d", h=1), 1,
                                       (ropec[0:127, 0, :], ropec[0:127, 1, :]),
                                       kcr[0:127, :, :].rearrange("p (h u) d -> p h u d", h=1),
                                       reads=[pkB, wB], writes=[kcrB], dup=2)
                            S.op(S.pe, lambda e: e.transpose(tpk[:, 0:127], kcr[0:127, :, :].rearrange("p a d -> p (a d)"), ident[0:127, 0:127]),
                                 reads=[kcrB, identB], writes=[tpkB])
                            S.op(S.act, lambda e: e.activation(kcbT[0:64, g, 0:127], tpk[0:64, 0:127], AF.Copy), reads=[tpkB], writes=[kcbTB])
                        else:
                            S.op(S.act, lambda e: e.activation(vcb[0:127, g, :], pk[0:127, :], AF.Copy), reads=[pkB], writes=[vcbB])
                S.barrier()
            ks_scope.close()
            if STOP[0] == "compress":
                return
            dbg_out("kcbT", kcbT[:], [kcbTB])
            dbg_out("vcb", vcb[:], [vcbB])

            with ExitStack() as st:
                cmask = sbuf(st, [128, NT, 128], F32, "cmask")
                selbias = sbuf(st, [128, NT, 32], F32, "selbias")
                wmask = sbuf(st, [128, 8, 512], BF16, "wmask")
                cB = Buf()
                caus = sbuf(st, [128, 4, 512], BF16, "caus")
                causB = Buf()
                S.dma(S.pool, caus[:], D["caus"], writes=[causB])
                S.dma(S.sp, cmask[:], D["cmask"], writes=[cB])
                S.dma(S.sp, selbias[:], D["selbias"], writes=[cB])
                S.dma(S.pool, wmask[:], D["wmask"], writes=[cB])
                sc = psum(st, [128, 4, 128], F32, "sc")
                tpb = psum(st, [128, 5, 128], BF16, "tpb")
                ocotp = psum(st, [128, 512], F32, "ocotp")
                oc = ocotp[:, 0:256].rearrange("p (a b) -> p a b", a=4)
                otp = ocotp[:, 0:260].rearrange("p (a b) -> p a b", a=4)
                stp = [psum(st, [128, 512], F32, "stp") for _ in range(3)]
                oT = [psum(st, [128, 512], F32, "oT") for _ in range(2)]
                scB, tpbB, ocB = PBuf(), PBuf(), PBuf()
                otpB = ocB
                tpsB = tpbB
                stB = [PBuf(), PBuf(), PBuf()]
                oTB = [PBuf(), PBuf()]
                TL = []
                for g_ in range(2):
                    T = {}
                    T["sm"] = sbuf(st, [128, 4, 127], F32, "sm")
                    T["pb16"] = sbuf(st, [128, 4, 128], BF16, "pb16")
                    T["pT"] = sbuf(st, [128, 4, 128], BF16, "pT")
                    T["pp"] = sbuf(st, [128, 132], F32, "pp")
                    T["st8"] = sbuf(st, [128, 8], F32, "st8")
                    T["imp"] = sbuf(st, [128, 32], F32, "imp")
                    T["imp3"] = sbuf(st, [128, 32], F32, "imp3")
                    T["m8"] = sbuf(st, [128, 16], F32, "m8")
                    T["selb"] = sbuf(st, [128, 96], BF16, "selb")
                    T["etmp2"] = sbuf(st, [128, 4, 64], F32, "etmp2")
                    T["B"] = [Buf() for _ in range(9)]
                    TL.append(T)
                nacc2 = [sbuf(st, [128, 4, 8, 64], F32, "nacc") for _ in range(2)]
                nb16 = [sbuf(st, [128, 512], BF16, "nb16") for _ in range(2)]
                PT = [sbuf(st, [128, 512], BF16, "PT") for _ in range(3)]
                oTs = sbuf(st, [65, 512], F32, "oTs")
                ew = sbuf(st, [128, 8], F32, "ew")
                etmp = sbuf(st, [128, 4, 64], F32, "etmp")
                naccB2 = [[Buf() for _ in range(4)] for _ in range(2)]
                oTsB, ewB, etmpB = Buf(), Buf(), Buf()
                nb16B = [Buf(), Buf()]
                PTB = [Buf() for _ in range(3)]

                def writeout_gen(qb):
                    nacc = nacc2[qb % 2]
                    naccB = naccB2[qb % 2]
                    for tl in range(4):
                        t = 4 * qb + tl
                        i = tl % 2
                        S.op(S.dve, lambda e: e.tensor_copy(nb16[i][:], nacc[:, tl, :, :].rearrange("p h d -> p (h d)")),
                             reads=[naccB[tl]], writes=[nb16B[i]])
                        yield
                        for c in range(4):
                            S.op(S.pe, lambda e: e.transpose(tpb[:, c, :], nb16[i][:, c * 128:(c + 1) * 128], ident[:]),
                                 reads=[nb16B[i], identB], writes=[tpbB])
                        S.op(S.act, lambda e: e.activation(nsaT[:, :, t * 128:(t + 1) * 128], tpb[:, 0:4, :], AF.Copy),
                             reads=[tpbB], writes=[nsaTB[t]])
                        yield
                for T in TL:
                    S.op(S.dve, lambda e: e.memset(T["pp"][:], 0.0), writes=[T["B"][3]])
                    S.op(S.dve, lambda e: e.memset(T["pb16"][:], 0.0), writes=[T["B"][1]])
                    S.op(S.dve, lambda e: e.memset(T["selb"][:], 0.0), writes=[T["B"][7]])
                sti = [0]
                pti = [0]
                oti = [0]

                def attn_items(items, qb, h, kT_idx, v_idx, chunks, gate_col, acc_tile, accB, qT, qTB, kT, kTB, vt, vtB, mask_fn, extra_reads):
                    hs = slice(0, 96)
                    pair = h
                    ob = oti[0] % 2
                    oti[0] += 1
                    n = len(chunks)
                    for ci, (kc, j0, j1) in enumerate(chunks):
                        sb_ = sti[0] % 3
                        sti[0] += 1
                        pb_ = pti[0] % 3
                        pti[0] += 1
                        qsl = slice(qb * 512 + j0, qb * 512 + j1)

                        def qk(kc=kc, j0=j0, j1=j1, sb_=sb_, qsl=qsl):
                            pairs = [(kT[hs, kT_idx, kc * 128:(kc + 1) * 128], qT[hs, pair, qsl])] + mask_fn(kc, j0, j1)
                            S.mm(stp[sb_][:, j0:j1], pairs,
                                 reads=[kTB[kc]] + [qTB[4 * qb + q] for q in range(j0 // 128, (j1 + 127) // 128)] + extra_reads,
                                 writes=[stB[sb_]])

                        def ex(j0=j0, j1=j1, sb_=sb_, pb_=pb_):
                            S.op(S.act, lambda e: e.activation(PT[pb_][:, j0:j1], stp[sb_][:, j0:j1], AF.Exp, scale=0.125),
                                 reads=[stB[sb_]], writes=[PTB[pb_]])

                        def pv(kc=kc, j0=j0, j1=j1, pb_=pb_, ci=ci):
                            S.mm1(oT[ob][0:65, j0:j1], vt[:, kc, v_idx, 0:65], PT[pb_][:, j0:j1], start=(ci == 0), stop=(ci == n - 1),
                                  reads=[PTB[pb_], vtB[kc]], writes=[oTB[ob]])
                        epi_a = epi_b = None
                        if ci == n - 1:
                            def epi_a():
                                S.op(S.dve, lambda e: e.tensor_copy(oTs[:], oT[ob][0:65, :]), reads=[oTB[ob]], writes=[oTsB])

                            def epi_b():
                                for tl in range(4):
                                    S.op(S.pe, lambda e: e.transpose(otp[:, tl, :], oTs[:, tl * 128:(tl + 1) * 128], identf[0:65, 0:65]),
                                         reads=[oTsB, identB], writes=[otpB])
                                S.op(S.dve, lambda e: e.tensor_scalar(ew[:, 0:4], otp[:, :, 64], 1e-30, None, ALU.max), reads=[otpB], writes=[ewB])
                                S.op(S.dve, lambda e: e.reciprocal(ew[:, 0:4], ew[:, 0:4]), reads=[ewB], writes=[ewB])
                                S.op(S.dve, lambda e: e.tensor_tensor(ew[:, 0:4], ew[:, 0:4], ga[:, 4 * qb:4 * qb + 4, gate_col], ALU.mult),
                                     reads=[ewB] + gaB[4 * qb:4 * qb + 4], writes=[ewB])
                                S.op(S.dve, lambda e: e.tensor_tensor(etmp[:], otp[:, :, 0:64], bcast(ew[:, 0:4], 2, 64), ALU.mult),
                                     reads=[otpB, ewB], writes=[etmpB])
                                S.op(S.pool, lambda e: e.tensor_tensor(acc_tile[:, :, h, :], acc_tile[:, :, h, :], etmp[:], ALU.add),
                                     reads=[etmpB] + accB, writes=accB)
                        items.append((qk, ex, pv, epi_a, epi_b))

                if STOP[0] and STOP[0].startswith("budget:"):
                    S.budget = int(STOP[0].split(":")[1])
                def part_a_gen(qb):
                    nacc = nacc2[qb % 2]
                    naccB = naccB2[qb % 2]
                    S.op(S.pool, lambda e: e.memset(nacc[:], 0.0), writes=naccB)
                    yield
                    gens = [part_a_chain(qb, 0), part_a_chain(qb, 1)]
                    while gens:
                        for gn in list(gens):
                            if next(gn, "done") == "done":
                                gens.remove(gn)
                            else:
                                yield

                def part_a_chain(qb, g):
                    nacc = nacc2[qb % 2]
                    naccB = naccB2[qb % 2]
                    T = TL[g]
                    sm, pb16, pT, pp, st8, imp, imp3, m8, selb, etmp2 = (T[k_] for k_ in ("sm", "pb16", "pT", "pp", "st8", "imp", "imp3", "m8", "selb", "etmp2"))
                    smB, pb16B, pTB, ppB, st8B, impB, m8B, selbB, etmp2B = T["B"]
                    ppw = bass.AP(pp[:, 0:1].tensor, pp[:, 0:1].offset, [list(pp[:, 0:1].ap[0]), [4, 32], [1, 5]])
                    if True:
                        for tl in range(4):
                            t = 4 * qb + tl
                            tsl = slice(t * 128, (t + 1) * 128)
                            for r in range(4):
                                h = 4 * g + r
                                S.mm(sc[:, r, 0:127], [(qaT[0:96, h, tsl], kcbT[0:96, g, 0:127])],
                                     reads=[qaB[t], kcbTB], writes=[scB])
                            S.op(S.dve, lambda e: e.tensor_tensor(sm[:], sc[:, :, 0:127], bcast(cmask[:, t, 0:127], 1, 4), ALU.add),
                                 reads=[scB, cB], writes=[smB])
                            yield
                            S.op(S.act, lambda e: e.activation(sm[:], sm[:], AF.Exp, scale=0.125), reads=[smB], writes=[smB])
                            yield
                            S.op(S.dve, lambda e: e.tensor_reduce(st8[:, 0:4], sm[:], AX.X, ALU.add), reads=[smB], writes=[st8B])
                            S.op(S.dve, lambda e: e.tensor_scalar(st8[:, 0:4], st8[:, 0:4], 1e-30, None, ALU.max), reads=[st8B], writes=[st8B])
                            yield
                            S.op(S.dve, lambda e: e.reciprocal(st8[:, 0:4], st8[:, 0:4]), reads=[st8B], writes=[st8B])
                            S.op(S.dve, lambda e: e.tensor_tensor(sm[:], sm[:], bcast(st8[:, 0:4], 2, 127), ALU.mult),
                                 reads=[smB, st8B], writes=[smB])
                            yield
                            S.op(S.pool, lambda e: e.tensor_copy(pb16[:, :, 0:127], sm[:]), reads=[smB], writes=[pb16B])
                            S.op(S.dve, lambda e: e.tensor_reduce(pp[:, 1:128], sm[:].rearrange("p r c -> p c r"), AX.X, ALU.add),
                                 reads=[smB], writes=[ppB])
                            yield
                            S.op(S.dve, lambda e: e.tensor_reduce(imp[:], ppw, AX.X, ALU.add), reads=[ppB], writes=[impB])
                            S.op(S.dve, lambda e: e.tensor_tensor(imp[:], imp[:], selbias[:, t, :], ALU.add), reads=[impB, cB], writes=[impB])
                            yield
                            S.op(S.dve, lambda e: e.max(m8[:, 0:8], imp[:]), reads=[impB], writes=[m8B])
                            S.op(S.dve, lambda e: e.match_replace(imp3[:], m8[:, 0:8], imp[:], -3.0e38), reads=[impB, m8B], writes=[m8B])
                            yield
                            S.op(S.dve, lambda e: e.max(m8[:, 8:16], imp3[:]), reads=[m8B], writes=[m8B])
                            S.op(S.dve, lambda e: e.tensor_scalar(imp3[:], imp[:], m8[:, 15:16], None, ALU.is_ge), reads=[impB, m8B], writes=[m8B])
                            yield
                            S.op(S.dve, lambda e: e.tensor_scalar(selb[:, 64:96], imp3[:], -NEG_BIG, NEG_BIG, ALU.mult, ALU.add),
                                 reads=[m8B], writes=[selbB])
                            yield
                            S.op(S.pe, lambda e: e.transpose(tpb[0:96, 4, :], selb[:], ident[:]), reads=[selbB, identB], writes=[tpsB])
                            S.op(S.dve, lambda e: e.tensor_copy(qaT[64:96, 4 * g:4 * g + 4, tsl], bcast(tpb[64:96, 4, :], 1, 4)),
                                 reads=[tpsB], writes=[selRB[g][t]])
                            yield
                            for r in range(4):
                                S.op(S.pe, lambda e: e.transpose(tpb[0:127, r, :], pb16[:, r, 0:127], ident[:]),
                                     reads=[pb16B, identB], writes=[tpbB])
                            S.op(S.dve, lambda e: e.tensor_copy(pT[0:127, :, :], tpb[0:127, 0:4, :]), reads=[tpbB], writes=[pTB])
                            yield
                            for r in range(4):
                                S.mm(oc[:, r, :], [(pT[0:127, r, :], vcb[0:127, g, :])], reads=[pTB, vcbB], writes=[ocB])
                            S.op(S.dve, lambda e: e.tensor_tensor(etmp2[:], oc[:], bcast(ga[:, t, 12 * g:12 * g + 12:3], 2, 64), ALU.mult),
                                 reads=[ocB, gaB[t]], writes=[etmp2B])
                            S.op(S.pool, lambda e: e.tensor_tensor(nacc[:, tl, 4 * g:4 * g + 4, :], nacc[:, tl, 4 * g:4 * g + 4, :], etmp2[:], ALU.add),
                                 reads=[etmp2B, naccB[tl]], writes=[naccB[tl]])
                            yield

                print('SBUF remaining in NSA attn:', nc.sbuf_bytes_remaining)
                gen_a = part_a_gen(0)
                next(gen_a)
                wo_prev = None
                for qb in range(4):
                    nacc = nacc2[qb % 2]
                    naccB = naccB2[qb % 2]
                    win_items = []
                    sel_items = []
                    for h in range(8):
                        g = h // 4
                        order = [1, 0, 2, 3, 4, -1, -2, -3]
                        chunks = []
                        for e_ in order:
                            kc = 4 * qb - e_
                            if kc < 0 or kc > 4 * qb + 3:
                                continue
                            if e_ >= 1:
                                j0, j1 = 0, min(512, 640 - 128 * e_)
                            else:
                                j0, j1 = -128 * e_, 512
                            chunks.append((kc, j0, j1))

                        def win_mask(kc, j0, j1, qb=qb):
                            return [(ident[:], wmask[:, 4 * qb - kc + 3, j0:j1])]
                        attn_items(win_items, qb, h, 2 + g, 2 + g, chunks, 3 * h + 2, nacc, naccB, qaT, qaB, kskwT, kkB, vsw, vswB, win_mask,
                                   [cB, identB])
                    for h in range(8):
                        g = h // 4
                        chunks = []
                        for kc in range(4 * qb + 4):
                            d = kc - 4 * qb
                            chunks.append((kc, 128 * d if d > 0 else 0, 512))

                        def sel_mask(kc, j0, j1, g=g, qb=qb, h=h):
                            m = []
                            d = kc - 4 * qb
                            if d >= 0:
                                m.append((ident[:], caus[:, d, j0:j1]))
                            return m
                        attn_items(sel_items, qb, h, g, g, chunks, 3 * h + 1, nacc, naccB, qaT, qaB, kskwT, kkB, vsw, vswB, sel_mask,
                                   selRB[g][4 * qb:4 * qb + 4] + [eB_, cB, causB, identB])
                    fill_ = gen_a if wo_prev is None else itertools.chain(wo_prev, gen_a)
                    run_pipeline(win_items, filler=fill_, per_item=(8 * 14) // len(win_items) + 1)
                    gen_a = part_a_gen(qb + 1) if qb < 3 else None
                    if gen_a is not None:
                        next(gen_a)
                    run_pipeline(sel_items, filler=gen_a, per_item=1, drain=False)
                    wo_prev = writeout_gen(qb)
                for _ in wo_prev:
                    pass
                S.barrier()
        dbg_out("nsaT", nsaT[:], nsaTB)
        if STOP[0] == "nsaattn":
            return

        mobaT = sbuf(ms, [128, 4, S_LEN], BF16, "mobaT")
        mw = ms.enter_context(ExitStack())
        pastb = sbuf(mw, [128, NT, 8], F32, "pastb")
        ownsel = sbuf(mw, [128, NT, 8], F32, "ownsel")
        cB2 = Buf()
        S.dma(S.sp, pastb[:], D["pastbias"], writes=[cB2])
        S.dma(S.sp, ownsel[:], D["ownsel"], writes=[cB2])
        wm_all = [sbuf(mw, [128, 8, 768], BF16, "wm") for _ in range(2)]
        wmB_all = [[Buf() for _ in range(3)] for _ in range(2)]
        for hp_ in range(2):
            for i in range(3):
                c0 = 1304 + 512 * i + 256 * hp_
                S.dma(S.pool, wm_all[hp_][:, :, i * 256:(i + 1) * 256], win_v[:, :, c0:c0 + 256], writes=[wmB_all[hp_][i]])
        for hp in range(2):
            with ExitStack() as mo:
                qbT = sbuf(mo, [128, 4, S_LEN], BF16, "qbT")
                kbT = sbuf(mo, [128, 4, S_LEN], BF16, "kbT")
                vb = sbuf(mo, [128, NT, 4, 65], BF16, "vb")
                qbB = [Buf() for _ in range(NT)]
                kbB = [Buf() for _ in range(NT)]
                vbB = [Buf() for _ in range(NT)]
                mselB = [Buf() for _ in range(NT)]
                e8B = Buf()
                S.op(S.pool, lambda e: e.memset(vb[:], 1.0), writes=vbB)
                S.op(S.pool, lambda e: e.memset(qbT[64:96, :, :], 0.0), writes=mselB)
                for j in range(4):
                    S.dma(S.pool, kbT[64:96, j, :], D["e8"][j], writes=[e8B])
                with ExitStack() as st:
                    rt = make_rt(st)
                    tnorm = make_tile_norm(st, D["mix_norm"][l:l + 1, :], nstats)
                    wm = wm_all[hp]
                    wmB = wmB_all[hp]
                    pA = [psum(st, [128, 512], F32, "pA") for _ in range(2)]
                    pC = [psum(st, [128, 256], F32, "pC") for _ in range(2)]
                    tqk = psum(st, [128, 8, 128], BF16, "tqk")
                    pAB = [PBuf(), PBuf()]
                    pCB = [PBuf(), PBuf()]
                    tqkB = PBuf()
                    qkr = [sbuf(st, [128, 8, 128], BF16, "qkr") for _ in range(2)]
                    qkrB = [Buf(), Buf()]
                    for i_ in range(2):
                        S.op(S.pool, lambda e: e.memset(qkr[i_][:], 0.0), writes=[qkrB[i_]])
                    hts = {}

                    def m_norm(t):
                        hts[t] = tnorm(t)

                    def m_pa(t):
                        hTt, hTtB = hts[t]
                        i = t % 2
                        S.mm(pA[i][:], [(hTt[:, c, :], wm[:, c, 0:512]) for c in range(8)], reads=[hTtB, wmB[0], wmB[1]], writes=[pAB[i]])

                    def m_pc(t):
                        hTt, hTtB = hts[t]
                        i = t % 2
                        S.mm(pC[i][:], [(hTt[:, c, :], wm[:, c, 512:768]) for c in range(8)], reads=[hTtB, wmB[2]], writes=[pCB[i]])

                    def m_post(t):
                        i = t % 2
                        tsl = slice(t * 128, (t + 1) * 128)
                        cs = (rope[:, t, 0, :], rope[:, t, 1, :])
                        rope_apply(S, rt, pA[i][:].rearrange("p (h d) -> p h d", h=8), 8, cs, qkr[i][:, :, 0:64],
                                   reads=[pAB[i], ropeB], writes=[qkrB[i]])
                        S.op(S.act, lambda e: e.activation(vb[:, t, :, 0:64], pC[i][:].rearrange("p (h d) -> p h d", h=4), AF.Copy),
                             reads=[pCB[i]], writes=[vbB[t]])
                        for c in range(8):
                            S.op(S.pe, lambda e: e.transpose(tqk[:, c, :], qkr[i][:, c, :], ident[:]),
                                 reads=[qkrB[i], identB], writes=[tqkB])
                        S.op(S.act, lambda e: e.activation(qbT[0:64, :, tsl], tqk[0:64, 0:4, :], AF.Copy), reads=[tqkB], writes=[qbB[t]])
                        S.op(S.act, lambda e: e.activation(kbT[0:64, :, tsl], tqk[0:64, 4:8, :], AF.Copy), reads=[tqkB], writes=[kbB[t]])

                    print('SBUF remaining in MoBA proj:', nc.sbuf_bytes_remaining)
                    m_norm(0)
                    m_pa(0)
                    m_pc(0)
                    m_norm(1)
                    for t in range(NT):
                        if t + 2 < NT:
                            m_norm(t + 2)
                        if t + 1 < NT:
                            m_pa(t + 1)
                            m_pc(t + 1)
                        m_post(t)
                    S.barrier()
                with ExitStack() as st:
                    kmf = sbuf(st, [128, 4, 8], F32, "kmf")
                    kmb = sbuf(st, [128, 4, 8], BF16, "kmb")
                    kmB = Buf()
                    S.op(S.dve, lambda e: e.tensor_reduce(kmf[0:64], kbT[0:64].rearrange("p c (b k) -> p c b k", b=8), AX.X, ALU.add),
                         reads=kbB, writes=[kmB])
                    S.op(S.dve, lambda e: e.tensor_scalar(kmb[0:64], kmf[0:64], 1.0 / 256.0, None, ALU.mult), reads=[kmB], writes=[kmB])
                    gs2 = sbuf(st, [128, 4, 8], F32, "gs2")
                    m8a = sbuf(st, [128, 4, 8], F32, "m8a")
                    thr = sbuf(st, [128, 4], F32, "thr")
                    sel = sbuf(st, [128, 4, 8], F32, "sel")
                    mb = sbuf(st, [128, 128], BF16, "mb")
                    gs2B, m8aB, thrB, selB_, mbB = Buf(), Buf(), Buf(), Buf(), Buf()
                    S.op(S.pool, lambda e: e.memset(mb[:], 0.0), writes=[mbB])

                    def gate_gen(tiles):
                        for t in tiles:
                            tsl = slice(t * 128, (t + 1) * 128)
                            for j in range(4):
                                S.mm(gp[:, j, :], [(qbT[0:64, j, tsl], kmb[0:64, j, :])], reads=[qbB[t], kmB], writes=[gpB])
                            yield
                            S.op(S.dve, lambda e: e.tensor_tensor(gs2[:], gp[:, 0:4, :], bcast(pastb[:, t, :], 1, 4), ALU.add),
                                 reads=[gpB, cB2], writes=[gs2B])
                            yield
                            for j in range(4):
                                S.op(S.dve, lambda e: e.max(m8a[:, j, :], gs2[:, j, :]), reads=[gs2B], writes=[m8aB])
                                if j % 2 == 1:
                                    yield
                            S.op(S.dve, lambda e: e.tensor_scalar(thr[:], m8a[:, :, 2], -1e29, None, ALU.max), reads=[m8aB], writes=[thrB])
                            S.op(S.dve, lambda e: e.tensor_tensor(sel[:], gs2[:], bcast(thr[:], 2, 8), ALU.is_ge), reads=[gs2B, thrB], writes=[selB_])
                            yield
                            S.op(S.dve, lambda e: e.tensor_tensor(sel[:], sel[:], bcast(ownsel[:, t, :], 1, 4), ALU.max), reads=[selB_, cB2], writes=[selB_])
                            S.op(S.dve, lambda e: e.tensor_scalar(mb[:, 64:96], sel[:].rearrange("p h b -> p (h b)"), -NEG_BIG, NEG_BIG, ALU.mult, ALU.add),
                                 reads=[selB_], writes=[mbB])
                            yield
                            S.op(S.pe, lambda e: e.transpose(tm, mb[:], ident[:]), reads=[mbB, identB], writes=[tmB])
                            S.op(S.dve, lambda e: e.tensor_copy(qbT[64:96, :, tsl], bcast(tm[64:96, :], 1, 4)), reads=[tmB], writes=[mselB[t]])
                            yield

                    caus = sbuf(st, [128, 4, 512], BF16, "caus")
                    causB = Buf()
                    S.dma(S.pool, caus[:], D["caus"], writes=[causB])
                    stp = [psum(st, [128, 512], F32, "stp") for _ in range(4)]
                    oT = [psum(st, [128, 512], F32, "oT") for _ in range(2)]
                    otpg = psum(st, [128, 512], F32, "otpg")
                    otp = otpg[:, 0:260].rearrange("p (a b) -> p a b", a=4)
                    gp = otpg[:, 448:512].rearrange("p (a b) -> p a b", a=8)
                    tpbm = psum(st, [128, 8, 128], BF16, "tpbm")
                    tpb = tpbm[:, 0:4, :]
                    tm = tpbm[:, 4, :]
                    stB = [PBuf() for _ in range(4)]
                    oTB = [PBuf(), PBuf()]
                    otpB, tpbB = PBuf(), PBuf()
                    gpB = otpB
                    tmB = tpbB
                    print('SBUF remaining in MoBA attn (before macc etc):', nc.sbuf_bytes_remaining)
                    for _ in gate_gen(range(0, 4)):
                        pass
                    macc2 = [sbuf(st, [128, 4, 4, 64], F32, "macc") for _ in range(2)]
                    maccB2 = [[Buf() for _ in range(4)] for _ in range(2)]
                    nb16 = [sbuf(st, [128, 256], BF16, "nb16") for _ in range(2)]
                    PT = [sbuf(st, [128, 512], BF16, "PT") for _ in range(4)]
                    oTs = sbuf(st, [65, 512], F32, "oTs")
                    ew = sbuf(st, [128, 8], F32, "ew")
                    oTsB, ewB = Buf(), Buf()
                    nb16B = [Buf(), Buf()]
                    PTB = [Buf() for _ in range(4)]
                    sti = 0
                    pti = 0
                    oti = 0

                    def m_writeout_gen(qb):
                        macc = macc2[qb % 2]
                        maccB = maccB2[qb % 2]
                        for tl in range(4):
                            t = 4 * qb + tl
                            i = tl % 2
                            S.op(S.dve, lambda e: e.tensor_copy(nb16[i][:], macc[:, tl, :, :].rearrange("p h d -> p (h d)")),
                                 reads=[maccB[tl]], writes=[nb16B[i]])
                            yield
                            for c in range(2):
                                S.op(S.pe, lambda e: e.transpose(tpb[:, c, :], nb16[i][:, c * 128:(c + 1) * 128], ident[:]),
                                     reads=[nb16B[i], identB], writes=[tpbB])
                            S.op(S.act, lambda e: e.activation(mobaT[:, 2 * hp:2 * hp + 2, t * 128:(t + 1) * 128], tpb[:, 0:2, :], AF.Copy),
                                 reads=[tpbB], writes=[mobaTB[t]])
                            yield
                    m_wo_prev = None
                    for qb in range(4):
                        macc = macc2[qb % 2]
                        maccB = maccB2[qb % 2]
                        items = []
                        for j in range(4):
                            ob = oti % 2
                            oti += 1
                            nch = 4 * qb + 4
                            for kc in range(nch):
                                d = kc - 4 * qb
                                j0 = 128 * d if d > 0 else 0
                                sb_ = sti % 4
                                sti += 1
                                pb_ = pti % 4
                                pti += 1
                                qsl = slice(qb * 512 + j0, (qb + 1) * 512)

                                def qk(kc=kc, d=d, j0=j0, sb_=sb_, qsl=qsl, j=j, qb=qb):
                                    pairs = [(kbT[0:96, j, kc * 128:(kc + 1) * 128], qbT[0:96, j, qsl])]
                                    if d >= 0:
                                        pairs.append((ident[:], caus[:, d, j0:512]))
                                    S.mm(stp[sb_][:, j0:512], pairs,
                                         reads=[kbB[kc], e8B, causB, identB] + [qbB[4 * qb + q] for q in range(j0 // 128, 4)]
                                         + [mselB[4 * qb + q] for q in range(j0 // 128, 4)], writes=[stB[sb_]])

                                def ex(j0=j0, sb_=sb_, pb_=pb_):
                                    S.op(S.act, lambda e: e.activation(PT[pb_][:, j0:512], stp[sb_][:, j0:512], AF.Exp, scale=0.125),
                                         reads=[stB[sb_]], writes=[PTB[pb_]])

                                def pv(kc=kc, j0=j0, pb_=pb_, j=j, ob=ob, nch=nch):
                                    S.mm1(oT[ob][0:65, j0:512], vb[:, kc, j, 0:65], PT[pb_][:, j0:512], start=(kc == 0), stop=(kc == nch - 1),
                                          reads=[PTB[pb_], vbB[kc]], writes=[oTB[ob]])
                                epi_a = epi_b = None
                                if kc == nch - 1:
                                    def epi_a(ob=ob):
                                        S.op(S.dve, lambda e: e.tensor_copy(oTs[:], oT[ob][0:65, :]), reads=[oTB[ob]], writes=[oTsB])

                                    def epi_b(j=j, macc=macc, maccB=maccB):
                                        for tl in range(4):
                                            S.op(S.pe, lambda e: e.transpose(otp[:, tl, :], oTs[:, tl * 128:(tl + 1) * 128], identf[0:65, 0:65]),
                                                 reads=[oTsB, identB], writes=[otpB])
                                        S.op(S.dve, lambda e: e.tensor_scalar(ew[:, 0:4], otp[:, :, 64], 1e-30, None, ALU.max), reads=[otpB], writes=[ewB])
                                        S.op(S.dve, lambda e: e.reciprocal(ew[:, 0:4], ew[:, 0:4]), reads=[ewB], writes=[ewB])
                                        S.op(S.dve, lambda e: e.tensor_tensor(macc[:, :, j, :], otp[:, :, 0:64], bcast(ew[:, 0:4], 2, 64), ALU.mult),
                                             reads=[otpB, ewB], writes=maccB)
                                items.append((qk, ex, pv, epi_a, epi_b))
                        fl = []
                        if m_wo_prev is not None:
                            fl.append(m_wo_prev)
                        if qb < 3:
                            fl.append(gate_gen(range(4 * qb + 4, 4 * qb + 8)))
                        if fl:
                            run_pipeline(items, la=3, filler=itertools.chain(*fl), per_item=(4 * 8 + 8) // len(items) + 1)
                        else:
                            run_pipeline(items, la=3)
                        m_wo_prev = m_writeout_gen(qb)
                    for _ in m_wo_prev:
                        pass
                    S.barrier()
        mw.close()
        dbg_out("mobaT", mobaT[:], mobaTB)
        if STOP[0] == "mobaattn":
            return

        with ExitStack() as st:
            wgt = sbuf(st, [128, 8, 2048], BF16, "wgt")
            wbn = sbuf(st, [128, 4, DM], BF16, "wbn")
            wbm = sbuf(st, [128, 4, DM], BF16, "wbm")
            wo = sbuf(st, [128, 8, DM], BF16, "wo")
            wB = [Buf() for _ in range(4)]
            for i in range(4):
                S.dma(S.pool, wgt[:, :, i * 512:(i + 1) * 512], win_v[:, :, 2840 + i * 512:2840 + (i + 1) * 512], writes=[wB[0]])
            S.dma(S.pool, wbn[:], D["w_branch_nsa"][l].rearrange("(c p) n -> p c n", p=128), writes=[wB[1]])
            S.dma(S.pool, wbm[:], D["w_branch_moba"][l].rearrange("(c p) n -> p c n", p=128), writes=[wB[2]])
            for i in range(2):
                S.dma(S.pool, wo[:, :, i * 512:(i + 1) * 512], D["w_out"][l].rearrange("(c p) n -> p c n", p=128)[:, :, i * 512:(i + 1) * 512],
                      writes=[wB[3]])
            tnorm = make_tile_norm(st, D["mix_norm"][l:l + 1, :], nstats)
            sa = sbuf(st, [128, 512], F32, "sa")
            sbb = sbuf(st, [128, 512], F32, "sbb")
            ma = sbuf(st, [128, 512], F32, "ma")
            mg = sbuf(st, [128, DM], BF16, "mg")
            mgT = sbuf(st, [128, 8, 128], BF16, "mgT")
            saB, sbB, maB, mgB, mgTB = (Buf() for _ in range(5))
            tpm = psum(st, [128, 8, 128], BF16, "tpm")
            pga = psum(st, [128, 512], F32, "pga")
            pya = psum(st, [128, 512], F32, "pya")
            pgb = psum(st, [128, 512], F32, "pgb")
            pyb = psum(st, [128, 512], F32, "pyb")
            po = [psum(st, [128, 512], F32, "po") for _ in range(2)]
            tpmB, pgaB, pyaB, pgbB, pybB = (PBuf() for _ in range(5))
            poB = [PBuf(), PBuf()]
            mg2 = sbuf(st, [128, DM], BF16, "mg2")
            mgs = [mg, mg2]
            mgBs = [mgB, Buf()]
            hts = {}
            pi = [0]

            def g_norm(t):
                hts[t] = tnorm(t)

            def g_gate(t):
                tsl = slice(t * 128, (t + 1) * 128)
                hTt, hTtB = hts[t]
                mgt, mgtB = mgs[t % 2], mgBs[t % 2]
                for hf in range(2):
                    cs_ = slice(hf * 512, (hf + 1) * 512)
                    S.mm(pga[:], [(hTt[:, c, :], wgt[:, c, hf * 512:(hf + 1) * 512]) for c in range(8)], reads=[hTtB, wB[0]], writes=[pgaB])
                    S.mm(pya[:], [(nsaT[:, c, tsl], wbn[:, c, cs_]) for c in range(4)], reads=[nsaTB[t], wB[1]], writes=[pyaB])
                    S.mm(pgb[:], [(hTt[:, c, :], wgt[:, c, 1024 + hf * 512:1024 + (hf + 1) * 512]) for c in range(8)], reads=[hTtB, wB[0]], writes=[pgbB])
                    S.mm(pyb[:], [(mobaT[:, c, tsl], wbm[:, c, cs_]) for c in range(4)], reads=[mobaTB[t], wB[2]], writes=[pybB])
                    S.op(S.act, lambda e: e.activation(sa[:], pga[:], AF.Sigmoid), reads=[pgaB], writes=[saB])
                    S.op(S.act, lambda e: e.activation(sbb[:], pgb[:], AF.Sigmoid), reads=[pgbB], writes=[sbB])
                    S.op(S.dve, lambda e: e.tensor_tensor(ma[:], sa[:], pya[:], ALU.mult), reads=[saB, pyaB], writes=[maB])
                    S.op(S.dve, lambda e: e.tensor_tensor(sbb[:], sbb[:], pyb[:], ALU.mult), reads=[sbB, pybB], writes=[sbB])
                    S.op(S.dve, lambda e: e.tensor_tensor(mgt[:, cs_], ma[:], sbb[:], ALU.add), reads=[maB, sbB], writes=[mgtB])

            def g_out(t):
                mgt, mgtB = mgs[t % 2], mgBs[t % 2]
                for c in range(8):
                    S.op(S.pe, lambda e: e.transpose(tpm[:, c, :], mgt[:, c * 128:(c + 1) * 128], ident[:]), reads=[mgtB, identB], writes=[tpmB])
                S.op(S.act, lambda e: e.activation(mgT[:], tpm[:], AF.Copy), reads=[tpmB], writes=[mgTB])
                for hf in range(2):
                    k = pi[0] % 2
                    pi[0] += 1
                    cs_ = slice(hf * 512, (hf + 1) * 512)
                    S.mm(po[k][:], [(mgT[:, c, :], wo[:, c, cs_]) for c in range(8)], reads=[mgTB, wB[3]], writes=[poB[k]])
                    S.op(S.dve, lambda e: e.tensor_tensor(x_sb[:, t, cs_], x_sb[:, t, cs_], po[k][:], ALU.add),
                         reads=[poB[k], xB[t]], writes=[xB[t]])

            print('SBUF remaining in merge:', nc.sbuf_bytes_remaining)
            g_norm(0)
            g_norm(1)
            g_gate(0)
            for t in range(NT):
                if t + 2 < NT:
                    g_norm(t + 2)
                if t + 1 < NT:
                    g_gate(t + 1)
                g_out(t)
            S.barrier()


_CACHE = {}


def kernel(**inputs):
    x = np.asarray(inputs["x"], dtype=np.float32)
    B = x.shape[0]
    consts = host_consts()
    shared = {}
    for name, shp in WEIGHT_SHAPES.items():
        shared[name] = np.ascontiguousarray(np.asarray(inputs[name], dtype=np.float32).reshape(shp))
    for name in CONST_SHAPES:
        shared[name] = consts[name]
    if "nc" not in _CACHE:
        _CACHE["nc"] = build_program()[0]
    nc = _CACHE["nc"]
    in_maps = []
    for b in range(B):
        m = dict(shared)
        m["x"] = np.ascontiguousarray(x[b])
        in_maps.append(m)
    res = run_bass_kernel_spmd(nc, in_maps, core_ids=list(range(B)))
    out = np.stack([np.asarray(res.results[b]["out"], dtype=np.float32) for b in range(B)], axis=0)
    return out
```

```python
import itertools
import numpy as np
from contextlib import ExitStack
import concourse.bass as bass
import concourse.mybir as mybir
from concourse.bass_utils import run_bass_kernel_spmd

F32 = mybir.dt.float32
BF16 = mybir.dt.bfloat16
AF = mybir.ActivationFunctionType
ALU = mybir.AluOpType
AX = mybir.AxisListType

S_LEN = 2048
DM = 1024
NT = 16
DFF = 2816
NFC = 22
IN_COLS = 4888
NEG_BIG = -30000.0
NEG = -1e30
EPS = 1e-6


class Buf:
    __slots__ = ("name", "lw", "rd", "excl")

    def __init__(self, name="", excl=False):
        self.name = name
        self.lw = None
        self.rd = []
        self.excl = excl


def PBuf():
    return Buf("psum", True)


class Track:
    def __init__(self, sem, step, name):
        self.sem = sem
        self.step = step
        self.count = 0
        self.name = name


class Eng:
    def __init__(self, obj, track, name, inorder=False):
        self.obj = obj
        self.track = track
        self.name = name
        self.seen = {}
        self.inorder = inorder


class Sched:
    def __init__(self, nc, stack, n_dma_tracks=16):
        self.nc = nc

        def mk(obj, name, inorder=False):
            sem = stack.enter_context(nc.semaphore("s_" + name))
            return Eng(obj, Track(sem, 1, name), name, inorder)

        self.pe = mk(nc.tensor, "pe", True)
        self.act = mk(nc.scalar, "act")
        self.dve = mk(nc.vector, "dve")
        self.pool = mk(nc.gpsimd, "pool")
        self.sp = mk(nc.sync, "sp")
        self.engs = [self.pe, self.act, self.dve, self.pool, self.sp]
        self.dma_pools = {}
        for e_ in (self.sp, self.pool):
            self.dma_pools[e_.name] = [
                Track(stack.enter_context(nc.semaphore("s_dma_%s%d" % (e_.name, i))), 16, "dma_%s%d" % (e_.name, i))
                for i in range(n_dma_tracks // 2)
            ]
        self.dma_tracks = self.dma_pools["sp"] + self.dma_pools["pool"]
        self.dma_i = {"sp": 0, "pool": 0}
        self.n_inst = 0
        self.n_wait = 0
        self.budget = None

    def _skip(self):
        if self.budget is None:
            return False
        if self.budget <= 0:
            return True
        self.budget -= 1
        return False

    def _wait(self, eng, track, val):
        if eng.inorder and track is eng.track:
            return
        if eng.seen.get(track, 0) >= val:
            return
        eng.obj.wait_ge(track.sem, val)
        eng.seen[track] = val
        self.n_wait += 1

    @staticmethod
    def _split(reads, writes):
        ex = [b for b in reads if b.excl]
        if ex:
            reads = [b for b in reads if not b.excl]
            writes = list(writes) + ex
        return reads, writes

    def _deps(self, eng, reads, writes):
        for b in reads:
            if b.lw is not None:
                self._wait(eng, *b.lw)
        for b in writes:
            if b.lw is not None:
                self._wait(eng, *b.lw)
            for r in b.rd:
                self._wait(eng, *r)

    def _commit(self, track, val, reads, writes):
        for b in reads:
            b.rd.append((track, val))
        for b in writes:
            b.lw = (track, val)
            b.rd = []

    def op(self, eng, fn, reads=(), writes=()):
        if self._skip():
            return None
        reads, writes = self._split(reads, writes)
        self._deps(eng, reads, writes)
        inst = fn(eng.obj)
        t = eng.track
        t.count += 1
        inst.then_inc(t.sem, 1)
        self._commit(t, t.count, reads, writes)
        self.n_inst += 1
        return inst

    def mm(self, out_ap, pairs, reads=(), writes=()):
        if self._skip():
            return None
        eng = self.pe
        reads, writes = self._split(reads, writes)
        self._deps(eng, reads, writes)
        n = len(pairs)
        inst = None
        for i, (l, r) in enumerate(pairs):
            inst = eng.obj.matmul(out_ap, l, r, start=(i == 0), stop=(i == n - 1))
            self.n_inst += 1
        t = eng.track
        t.count += 1
        inst.then_inc(t.sem, 1)
        self._commit(t, t.count, reads, writes)

    def mm1(self, out_ap, l, r, start, stop, reads=(), writes=(), skip=False):
        if self._skip():
            return None
        eng = self.pe
        reads, writes = self._split(reads, writes)
        self._deps(eng, reads, writes)
        if skip:
            inst = eng.obj.matmul(out_ap, l, r, start=start, stop=stop, skip_group_check=True)
        else:
            inst = eng.obj.matmul(out_ap, l, r, start=start, stop=stop)
        t = eng.track
        t.count += 1
        inst.then_inc(t.sem, 1)
        self._commit(t, t.count, reads, writes)
        self.n_inst += 1

    def dma(self, eng, out, in_, reads=(), writes=(), **kw):
        if self._skip():
            return None
        pool_ = self.dma_pools[eng.name]
        tr = pool_[self.dma_i[eng.name] % len(pool_)]
        self.dma_i[eng.name] += 1
        if tr.count:
            self._wait(eng, tr, tr.count * 16)
        self._deps(eng, reads, writes)
        inst = eng.obj.dma_start(out=out, in_=in_, **kw)
        tr.count += 1
        inst.then_inc(tr.sem, 16)
        self._commit(tr, tr.count * 16, reads, writes)
        self.n_inst += 1
        return inst

    def pe_fence(self):
        t = self.pe.track
        if t.count and self.pe.seen.get(t, 0) < t.count:
            self.pe.obj.wait_ge(t.sem, t.count)
            self.pe.seen[t] = t.count
            self.n_wait += 1

    def barrier(self):
        for e in self.engs:
            for o in self.engs:
                if o.track.count:
                    self._wait(e, o.track, o.track.count)
            for tr in self.dma_tracks:
                if tr.count:
                    self._wait(e, tr, tr.count * 16)

    def finish(self, eng=None):
        eng = eng or self.sp
        for o in self.engs:
            if o.track.count and o is not eng:
                self._wait(eng, o.track, o.track.count)
        for tr in self.dma_tracks:
            if tr.count:
                self._wait(eng, tr, tr.count * 16)


def bcast(ap, pos, n):
    l = [list(d) for d in ap.ap]
    l.insert(pos, [0, n])
    return bass.AP(ap.tensor, ap.offset, l)


def host_consts():
    c = {}
    inv = (np.float32(10000.0) ** (-np.arange(0, 64, 2, dtype=np.float32) / np.float32(64))).astype(np.float32)
    pos = np.arange(S_LEN, dtype=np.float32)
    ang = (pos[:, None] * inv[None, :]).astype(np.float32)
    cs = np.stack([np.cos(ang), np.sin(ang)], axis=1).astype(np.float32)
    csf = np.stack([np.concatenate([np.cos(ang), np.cos(ang)], axis=1),
                    np.concatenate([-np.sin(ang), np.sin(ang)], axis=1)], axis=1).astype(np.float32)
    c["rope_cs"] = np.ascontiguousarray(csf.reshape(NT, 128, 2, 64).transpose(1, 0, 2, 3))
    endp = (np.arange(127) * 16 + 31).astype(np.float32)
    angc = (endp[:, None] * inv[None, :]).astype(np.float32)
    rc = np.zeros((128, 2, 64), np.float32)
    rc[:127, 0] = np.concatenate([np.cos(angc), np.cos(angc)], axis=1)
    rc[:127, 1] = np.concatenate([-np.sin(angc), np.sin(angc)], axis=1)
    c["rope_c"] = rc
    t = np.arange(S_LEN)
    cm = np.where((np.arange(128)[None, :] * 16 + 31 <= t[:, None]) & (np.arange(128)[None, :] < 127), 0.0, NEG_BIG)
    c["cmask"] = np.ascontiguousarray(cm.astype(np.float32).reshape(NT, 128, 128).transpose(1, 0, 2))
    tb = (t // 64)[:, None]
    jj = np.arange(32)[None, :]
    forced = (jj == 0) | (jj == tb) | (jj == tb - 1)
    sb_ = np.where(jj <= tb, np.where(forced, 1e4, 0.0), NEG).astype(np.float32)
    c["selbias"] = np.ascontiguousarray(sb_.reshape(NT, 128, 32).transpose(1, 0, 2))
    E = np.zeros((32, 16, 128), np.float32)
    for kc in range(16):
        for k in range(128):
            E[2 * kc + k // 64, kc, k] = 1.0
    c["e_nsa"] = E
    p = np.arange(128)[:, None, None]
    d = np.arange(4)[None, :, None]
    j = np.arange(512)[None, None, :]
    c["caus"] = np.where(p + 128 * d <= j, 0.0, NEG_BIG).astype(np.float32)
    e = (np.arange(8) - 3)[None, :, None]
    dl = j - p + 128 * e
    c["wmask"] = np.where((dl >= 0) & (dl < 512), 0.0, NEG_BIG).astype(np.float32)
    blk = np.arange(8)[None, :]
    c["pastbias"] = np.ascontiguousarray(
        np.where(blk < (t // 256)[:, None], 0.0, NEG).astype(np.float32).reshape(NT, 128, 8).transpose(1, 0, 2))
    c["ownsel"] = np.ascontiguousarray(
        (blk == (t // 256)[:, None]).astype(np.float32).reshape(NT, 128, 8).transpose(1, 0, 2))
    ef = np.zeros((32, S_LEN), np.float32)
    ef[np.arange(S_LEN) // 64, np.arange(S_LEN)] = 1.0
    c["e_full"] = ef
    e8 = np.zeros((4, 32, S_LEN), np.float32)
    for j_ in range(4):
        e8[j_, 8 * j_ + np.arange(S_LEN) // 256, np.arange(S_LEN)] = 1.0
    c["e8"] = e8
    E2 = np.zeros((64, 64, 128), np.float32)
    for i in range(64):
        E2[i, i, :] = 1.0
    c["e_moba"] = E2
    return c


CONST_SHAPES = {
    "rope_cs": (128, 16, 2, 64), "rope_c": (128, 2, 64), "cmask": (128, 16, 128), "selbias": (128, 16, 32),
    "caus": (128, 4, 512), "wmask": (128, 8, 512), "pastbias": (128, 16, 8),
    "ownsel": (128, 16, 8), "e_full": (32, 2048), "e8": (4, 32, 2048),
}

WEIGHT_SHAPES = {
    "ffn1_norm": (2, 1024), "ffn1_wg": (2, 1024, 2816), "ffn1_wu": (2, 1024, 2816), "ffn1_wd": (2, 2816, 1024),
    "mix_norm": (2, 1024), "w_in": (2, 1024, 4888),
    "cmpk_pos": (2, 32, 64), "cmpk_w1": (2, 2048, 128), "cmpk_w2": (2, 128, 64),
    "cmpv_pos": (2, 32, 64), "cmpv_w1": (2, 2048, 128), "cmpv_w2": (2, 128, 64),
    "w_branch_nsa": (2, 512, 1024), "w_branch_moba": (2, 512, 1024), "w_out": (2, 1024, 1024),
    "ffn2_norm": (2, 1024), "ffn2_wg": (2, 1024, 2816), "ffn2_wu": (2, 1024, 2816), "ffn2_wd": (2, 2816, 1024),
    "final_norm": (1, 1024),
}


def build_program(n_layers=2, stages=("ffn1", "mix", "ffn2"), dbg=()):
    nc = bass.Bass("TRN2", target_bir_lowering=False)
    D = {}
    for name, shp in WEIGHT_SHAPES.items():
        D[name] = nc.dram_tensor(name, list(shp), F32, kind="ExternalInput").ap()
    for name, shp in CONST_SHAPES.items():
        D[name] = nc.dram_tensor(name, list(shp), F32, kind="ExternalInput").ap()
    x_d = nc.dram_tensor("x", [S_LEN, DM], F32, kind="ExternalInput").ap()
    out_d = nc.dram_tensor("out", [S_LEN, DM], F32, kind="ExternalOutput").ap()
    DBG = {}

    with ExitStack() as top:
        S = Sched(nc, top)
        cnt = [0]

        def sbuf(st, shape, dt, name=None):
            cnt[0] += 1
            return st.enter_context(nc.sbuf_tensor("%s_%d" % (name or "t", cnt[0]), list(shape), dt))

        def psum(st, shape, dt, name=None):
            cnt[0] += 1
            full = 512 if dt == F32 else 1024
            t = st.enter_context(nc.psum_tensor("%s_%d" % (name or "p", cnt[0]), [128, full], dt))
            shape = list(shape)
            n = 1
            for d_ in shape[1:]:
                n *= d_
            v = t[0:shape[0], 0:n]
            if len(shape) == 3:
                v = v.rearrange("p (a b) -> p a b", a=shape[1])
            elif len(shape) == 4:
                v = v.rearrange("p (a b c) -> p a b c", a=shape[1], b=shape[2])
            return v

        x_sb = sbuf(top, [128, NT, DM], F32, "x")
        xB = [Buf("x%d" % t) for t in range(NT)]
        ident = sbuf(top, [128, 128], BF16, "ident")
        identf = sbuf(top, [128, 128], F32, "identf")
        identB = Buf("ident")
        rope = sbuf(top, [128, NT, 2, 64], F32, "rope")
        ropeB = Buf("rope")

        for t in range(NT):
            S.dma(S.sp if t % 2 == 0 else S.pool, x_sb[:, t, :], x_d[t * 128:(t + 1) * 128, :], writes=[xB[t]])
        S.dma(S.sp, rope[:], D["rope_cs"], writes=[ropeB])
        S.op(S.pool, lambda e: e.memset(identf[:], 1.0), writes=[identB])
        S.op(S.pool, lambda e: e.affine_select(identf[:], identf[:], [[-1, 128]], ALU.is_equal, 0.0,
                                               base=0, channel_multiplier=1), reads=[identB], writes=[identB])
        S.op(S.dve, lambda e: e.tensor_copy(ident[:], identf[:]), reads=[identB], writes=[identB])

        def dbg_out(name, ap, reads):
            if name not in dbg:
                return
            shp = list(ap.shape)
            dt_ = ap.dtype
            d = nc.dram_tensor("dbg_" + name, shp, dt_, kind="ExternalOutput").ap()
            DBG[name] = d
            S.dma(S.sp, d, ap, reads=reads)

        def norm_T(gain_row, hT, hTB):
            with ExitStack() as st:
                gbc = sbuf(st, [128, DM], F32, "gbc")
                ss = sbuf(st, [128, NT], F32, "ss")
                sd = sbuf(st, [128, NT], F32, "sd")
                rstd = sbuf(st, [128, NT], F32, "rstd")
                junk = sbuf(st, [128, DM], BF16, "junk")
                hn = [sbuf(st, [128, DM], BF16, "hn") for _ in range(2)]
                tp = [psum(st, [128, 8, 128], BF16, "tp") for _ in range(2)]
                gB, ss0B, junkB = Buf(), Buf(), Buf()
                ssB = [Buf() for _ in range(NT)]
                hnB = [Buf(), Buf()]
                tpB = [PBuf(), PBuf()]
                S.dma(S.sp, gbc[:], gain_row.partition_broadcast(128), writes=[gB])
                S.op(S.dve, lambda e: e.memset(ss[:], 0.0), writes=ssB)

                def stats(t):
                    S.op(S.act, lambda e: e.activation(junk[:], x_sb[:, t, :], AF.Square, accum_out=ss[:, t:t + 1]),
                         reads=[xB[t]], writes=[junkB, ssB[t]])
                    S.op(S.act, lambda e: e.activation(sd[:, t:t + 1], ss[:, t:t + 1], AF.Sqrt, bias=EPS_AP[:, 0:1], scale=1.0 / DM),
                         reads=[ssB[t], epsB], writes=[ssB[t]])
                    S.op(S.dve, lambda e: e.reciprocal(rstd[:, t:t + 1], sd[:, t:t + 1]), reads=[ssB[t]], writes=[ssB[t]])
                stats(0)
                stats(1)
                for t in range(NT):
                    i = t % 2
                    if t + 2 < NT:
                        stats(t + 2)
                    S.op(S.dve, lambda e: e.scalar_tensor_tensor(hn[i][:], x_sb[:, t, :], rstd[:, t:t + 1], gbc[:],
                                                                 ALU.mult, ALU.mult),
                         reads=[xB[t], ssB[t], gB], writes=[hnB[i]])
                    for c in range(8):
                        S.op(S.pe, lambda e: e.transpose(tp[i][:, c, :], hn[i][:, c * 128:(c + 1) * 128], ident[:]),
                             reads=[hnB[i], identB], writes=[tpB[i]])
                    S.op(S.act, lambda e: e.activation(hT[:, :, t * 128:(t + 1) * 128], tp[i][:], AF.Copy),
                         reads=[tpB[i]], writes=[hTB[t]])
                S.barrier()

        def make_tile_norm(st, gain_row, shared=None):
            gbc = sbuf(st, [128, DM], F32, "gbc")
            hn = [sbuf(st, [128, DM], BF16, "hn") for _ in range(2)]
            hTt = [sbuf(st, [128, 8, 128], BF16, "hTt") for _ in range(2)]
            tph = psum(st, [128, 8, 128], BF16, "tph")
            gB, tphB = Buf(), PBuf()
            hnB = [Buf(), Buf()]
            hTtB = [Buf(), Buf()]
            S.dma(S.sp, gbc[:], gain_row.partition_broadcast(128), writes=[gB])
            have = shared is not None and shared.get("done")
            if shared is None:
                shared = {}
            if not have:
                if "rstd" not in shared:
                    shared["rstd"] = sbuf(st, [128, NT], F32, "rstd")
                    shared["B"] = [Buf() for _ in range(NT)]
                ss = sbuf(st, [128, NT], F32, "ss")
                sd = sbuf(st, [128, NT], F32, "sd")
                junk = sbuf(st, [128, DM], BF16, "junk")
                junkB = Buf()
                S.op(S.dve, lambda e: e.memset(ss[:], 0.0), writes=shared["B"])
            rstd = shared["rstd"]
            ssB = shared["B"]
            started = set()

            def stats(t):
                g4 = t // 4
                if have or g4 in started or t >= NT:
                    return
                started.add(g4)
                tl_ = list(range(4 * g4, 4 * g4 + 4))
                for tt in tl_:
                    S.op(S.act, lambda e: e.activation(junk[:], x_sb[:, tt, :], AF.Square, accum_out=ss[:, tt:tt + 1]),
                         reads=[xB[tt]], writes=[junkB, ssB[tt]])
                gsl = slice(4 * g4, 4 * g4 + 4)
                S.op(S.act, lambda e: e.activation(sd[:, gsl], ss[:, gsl], AF.Sqrt, bias=EPS_AP[:, 0:1], scale=1.0 / DM),
                     reads=[ssB[tt] for tt in tl_] + [epsB], writes=[ssB[tt] for tt in tl_])
                S.op(S.dve, lambda e: e.reciprocal(rstd[:, gsl], sd[:, gsl]), reads=[ssB[tt] for tt in tl_], writes=[ssB[tt] for tt in tl_])
            stats(0)
            shared["done"] = True

            def tile(t):
                i = t % 2
                stats(t)
                stats(t + 3)
                S.op(S.dve, lambda e: e.scalar_tensor_tensor(hn[i][:], x_sb[:, t, :], rstd[:, t:t + 1], gbc[:],
                                                             ALU.mult, ALU.mult),
                     reads=[xB[t], ssB[t], gB], writes=[hnB[i]])
                for c in range(8):
                    S.op(S.pe, lambda e: e.transpose(tph[:, c, :], hn[i][:, c * 128:(c + 1) * 128], ident[:]),
                         reads=[hnB[i], identB], writes=[tphB])
                S.op(S.act, lambda e: e.activation(hTt[i][:], tph[:], AF.Copy), reads=[tphB], writes=[hTtB[i]])
                return hTt[i], hTtB[i]
            return tile

        def ffn(l, gain, wg, wu, wd, passes=((0, 11), (11, 22))):
            with ExitStack() as st:
                hT = sbuf(st, [128, 8, S_LEN], BF16, "hT")
                hTB = [Buf() for _ in range(NT)]
                wg_v = wg[l].rearrange("(c p) f -> p c f", p=128)
                wu_v = wu[l].rearrange("(c p) f -> p c f", p=128)
                wd_v = wd[l].rearrange("(c p) d -> p c d", p=128)
                wgs = [sbuf(st, [128, 8, 256], BF16, "wg") for _ in range(2)]
                wus = [sbuf(st, [128, 8, 256], BF16, "wu") for _ in range(2)]
                wgB = [Buf(), Buf()]
                wuB = [Buf(), Buf()]
                sg = [sbuf(st, [128, 512], F32, "sg") for _ in range(2)]
                sgB = [Buf(), Buf()]
                pg = [psum(st, [128, 512], F32, "pg") for _ in range(2)]
                pu = [psum(st, [128, 512], F32, "pu") for _ in range(2)]
                pd = [psum(st, [128, 512], F32, "pd") for _ in range(2)]
                pgB = [PBuf(), PBuf()]
                puB = [PBuf(), PBuf()]
                pdB = [PBuf(), PBuf()]
                npmax = max(b - a for a, b in passes)
                actT = sbuf(st, [128, npmax, S_LEN], BF16, "actT")
                wds = sbuf(st, [128, npmax, DM], BF16, "wd")
                actB = [[Buf() for _ in range(4)] for _ in range(npmax)]
                wdB = [Buf() for _ in range(npmax)]
                cnt_ = {"gi": 0, "ei": 0, "di": 0}

                def load_group(f0, fi, gsz):
                    b = cnt_["gi"] % 2
                    cnt_["gi"] += 1
                    c0 = (f0 + fi) * 128
                    S.dma(S.pool, wgs[b][:, :, 0:gsz * 128], wg_v[:, :, c0:c0 + gsz * 128], writes=[wgB[b]])
                    S.dma(S.pool, wus[b][:, :, 0:gsz * 128], wu_v[:, :, c0:c0 + gsz * 128], writes=[wuB[b]])
                    return b

                def gate_up(b, fidx, fo, tb):
                    k = cnt_["ei"] % 2
                    cnt_["ei"] += 1
                    hr = [hTB[4 * tb + q] for q in range(4)]
                    S.mm(pg[k][:], [(wgs[b][:, c, fo * 128:(fo + 1) * 128], hT[:, c, tb * 512:(tb + 1) * 512])
                                    for c in range(8)], reads=[wgB[b]] + hr, writes=[pgB[k]])
                    S.mm(pu[k][:], [(wus[b][:, c, fo * 128:(fo + 1) * 128], hT[:, c, tb * 512:(tb + 1) * 512])
                                    for c in range(8)], reads=[wuB[b]] + hr, writes=[puB[k]])
                    S.op(S.act, lambda e: e.activation(sg[k][:], pg[k][:], AF.Silu), reads=[pgB[k]], writes=[sgB[k]])
                    S.op(S.dve, lambda e: e.tensor_tensor(actT[:, fidx, tb * 512:(tb + 1) * 512], sg[k][:], pu[k][:], ALU.mult),
                         reads=[sgB[k], puB[k]], writes=[actB[fidx][tb]])

                f0_, f1_ = passes[0]
                g0sz = min(2, f1_ - f0_)
                b0 = load_group(f0_, 0, g0sz)
                gbc = sbuf(st, [128, DM], F32, "gbc")
                ss = sbuf(st, [128, NT], F32, "ss")
                sd = sbuf(st, [128, NT], F32, "sd")
                rstd = sbuf(st, [128, NT], F32, "rstd")
                junk = sbuf(st, [128, DM], BF16, "junk")
                hn = [sbuf(st, [128, DM], BF16, "hn") for _ in range(2)]
                tp = [psum(st, [128, 8, 128], BF16, "tp") for _ in range(2)]
                gB, junkB = Buf(), Buf()
                ssB = [Buf() for _ in range(NT)]
                hnB = [Buf(), Buf()]
                tpB = [PBuf(), PBuf()]
                S.dma(S.sp, gbc[:], gain[l:l + 1, :].partition_broadcast(128), writes=[gB])
                S.op(S.dve, lambda e: e.memset(ss[:], 0.0), writes=ssB)

                def stats4(g4):
                    tl_ = list(range(4 * g4, 4 * g4 + 4))
                    for tt in tl_:
                        S.op(S.act, lambda e: e.activation(junk[:], x_sb[:, tt, :], AF.Square, accum_out=ss[:, tt:tt + 1]),
                             reads=[xB[tt]], writes=[junkB, ssB[tt]])
                    gsl = slice(4 * g4, 4 * g4 + 4)
                    S.op(S.act, lambda e: e.activation(sd[:, gsl], ss[:, gsl], AF.Sqrt, bias=EPS_AP[:, 0:1], scale=1.0 / DM),
                         reads=[ssB[tt] for tt in tl_] + [epsB], writes=[ssB[tt] for tt in tl_])
                    S.op(S.dve, lambda e: e.reciprocal(rstd[:, gsl], sd[:, gsl]), reads=[ssB[tt] for tt in tl_], writes=[ssB[tt] for tt in tl_])
                stats4(0)
                for t in range(NT):
                    i = t % 2
                    if t % 4 == 0 and t + 4 < NT:
                        stats4(t // 4 + 1)
                    S.op(S.dve, lambda e: e.scalar_tensor_tensor(hn[i][:], x_sb[:, t, :], rstd[:, t:t + 1], gbc[:],
                                                                 ALU.mult, ALU.mult),
                         reads=[xB[t], ssB[t], gB], writes=[hnB[i]])
                    for c in range(8):
                        S.op(S.pe, lambda e: e.transpose(tp[i][:, c, :], hn[i][:, c * 128:(c + 1) * 128], ident[:]),
                             reads=[hnB[i], identB], writes=[tpB[i]])
                    S.op(S.act, lambda e: e.activation(hT[:, :, t * 128:(t + 1) * 128], tp[i][:], AF.Copy),
                         reads=[tpB[i]], writes=[hTB[t]])
                    if t % 4 == 3 and t >= 7:
                        for fo in range(g0sz):
                            gate_up(b0, fo, fo, t // 4 - 1)
                for fo in range(g0sz):
                    gate_up(b0, fo, fo, 3)

                first = True
                for (f0, f1) in passes:
                    nf = f1 - f0
                    wd_issued = False
                    fi = 0
                    while fi < nf:
                        gsz = min(2, nf - fi)
                        if first:
                            b = b0
                        else:
                            b = load_group(f0, fi, gsz)
                        if not wd_issued:
                            wd_issued = True
                            for fj in range(nf):
                                S.dma(S.pool, wds[:, fj, :], wd_v[:, f0 + fj, :], writes=[wdB[fj]])
                        if not first:
                            for fo in range(gsz):
                                for tb in range(4):
                                    gate_up(b, fi + fo, fo, tb)
                        first = False
                        fi += gsz
                    for t in range(NT):
                        for hf in range(2):
                            k = cnt_["di"] % 2
                            cnt_["di"] += 1
                            S.mm(pd[k][:], [(actT[:, fi, t * 128:(t + 1) * 128], wds[:, fi, hf * 512:(hf + 1) * 512])
                                            for fi in range(nf)],
                                 reads=[actB[fi][t // 4] for fi in range(nf)] + wdB[0:nf], writes=[pdB[k]])
                            S.op(S.dve, lambda e: e.scalar_tensor_tensor(
                                x_sb[:, t, hf * 512:(hf + 1) * 512], pd[k][:], 0.5,
                                x_sb[:, t, hf * 512:(hf + 1) * 512], ALU.mult, ALU.add),
                                reads=[pdB[k], xB[t]], writes=[xB[t]])
                S.barrier()

        EPS_AP = sbuf(top, [128, 1], F32, "eps")
        epsB = Buf("eps")
        S.op(S.dve, lambda e: e.memset(EPS_AP[:], EPS), writes=[epsB])

        from_mixer = {}
        for l in range(n_layers):
            if "ffn1" in stages:
                ffn(l, D["ffn1_norm"], D["ffn1_wg"], D["ffn1_wu"], D["ffn1_wd"])
            if "mix" in stages:
                mixer(nc, S, D, l, x_sb, xB, ident, identf, identB, rope, ropeB, make_tile_norm, sbuf, psum, dbg_out, EPS_AP, epsB)
            if "ffn2" in stages:
                ffn(l, D["ffn2_norm"], D["ffn2_wg"], D["ffn2_wu"], D["ffn2_wd"])

        S.budget = None
        with ExitStack() as st:
            gbc = sbuf(st, [128, DM], F32, "gbcf")
            ss = sbuf(st, [128, NT], F32, "ssf")
            sd = sbuf(st, [128, NT], F32, "sdf")
            rstd = sbuf(st, [128, NT], F32, "rstdf")
            junk = sbuf(st, [128, DM], BF16, "junkf")
            yo = [sbuf(st, [128, DM], F32, "yo") for _ in range(2)]
            gB, ssB, junkB = Buf(), Buf(), Buf()
            yB = [Buf(), Buf()]
            S.dma(S.sp, gbc[:], D["final_norm"][0:1, :].partition_broadcast(128), writes=[gB])
            S.op(S.dve, lambda e: e.memset(ss[:], 0.0), writes=[ssB])
            for t in range(NT):
                S.op(S.act, lambda e: e.activation(junk[:], x_sb[:, t, :], AF.Square, accum_out=ss[:, t:t + 1]),
                     reads=[xB[t]], writes=[junkB, ssB])
            S.op(S.act, lambda e: e.activation(sd[:], ss[:], AF.Sqrt, bias=EPS_AP[:, 0:1], scale=1.0 / DM),
                 reads=[ssB, epsB], writes=[ssB])
            S.op(S.dve, lambda e: e.reciprocal(rstd[:], sd[:]), reads=[ssB], writes=[ssB])
            for t in range(NT):
                i = t % 2
                S.op(S.dve, lambda e: e.scalar_tensor_tensor(yo[i][:], x_sb[:, t, :], rstd[:, t:t + 1], gbc[:],
                                                             ALU.mult, ALU.mult),
                     reads=[xB[t], ssB, gB], writes=[yB[i]])
                S.dma(S.sp, out_d[t * 128:(t + 1) * 128, :], yo[i][:], reads=[yB[i]])
            S.finish()
        print("program: n_inst=%d n_wait=%d" % (S.n_inst, S.n_wait))
    return nc, DBG


def rope_apply(S, st_tmp, src, H, cs, dst, reads, writes, dup=1):
    k = st_tmp["i"][0] % 2
    st_tmp["i"][0] += 1
    t1, t2 = st_tmp["t"][k]
    tB = st_tmp["B"][k]
    np_ = src.shape[0]
    a1 = t1[0:np_, 0:H, :]
    a2 = t2[0:np_, 0:H, :]
    S.op(S.dve, lambda e: e.tensor_tensor(a1, src, bcast(cs[0], 1, H), ALU.mult), reads=reads, writes=[tB[0]])
    S.op(S.dve, lambda e: e.tensor_tensor(a2[:, :, 0:32], src[:, :, 32:64], bcast(cs[1][:, 0:32], 1, H), ALU.mult),
         reads=reads, writes=[tB[1]])
    S.op(S.dve, lambda e: e.tensor_tensor(a2[:, :, 32:64], src[:, :, 0:32], bcast(cs[1][:, 32:64], 1, H), ALU.mult),
         reads=reads, writes=[tB[1]])
    if dup > 1:
        b1, b2 = bcast(a1, 2, dup), bcast(a2, 2, dup)
    else:
        b1, b2 = a1, a2
    S.op(S.pool, lambda e: e.tensor_tensor(dst, b1, b2, ALU.add), reads=[tB[0], tB[1]], writes=writes)


STOP = [None]


def run_pipeline(items, la=2, filler=None, per_item=0, drain=True, epi_delay=2):
    n = len(items)
    pending = []
    for i in range(n + la):
        if i < n:
            items[i][0]()
        j = i - la
        if j >= 0:
            items[j][1]()
            while pending and pending[0][0] <= j:
                pending.pop(0)[1]()
            items[j][2]()
            if items[j][3] is not None:
                items[j][3]()
                pending.append((j + epi_delay, items[j][4]))
            if filler is not None:
                for _ in range(per_item):
                    if next(filler, "done") == "done":
                        filler = None
                        break
    for _, fn in pending:
        fn()
    if filler is not None and drain:
        for _ in filler:
            pass


def mixer(nc, S, D, l, x_sb, xB, ident, identf, identB, rope, ropeB, make_tile_norm, sbuf, psum, dbg_out, EPS_AP, epsB):
    win_v = D["w_in"][l].rearrange("(c p) n -> p c n", p=128)
    with ExitStack() as ms:
        nsaT = sbuf(ms, [128, 4, S_LEN], BF16, "nsaT")
        nsaTB = [Buf() for _ in range(NT)]
        mobaTB = [Buf() for _ in range(NT)]
        nstats = {"rstd": sbuf(ms, [128, NT], F32, "rstd_mix"), "B": [Buf() for _ in range(NT)]}

        def make_rt(st_):
            return {"t": [(sbuf(st_, [128, 8, 64], F32, "rt1"), sbuf(st_, [128, 8, 64], F32, "rt2")) for _ in range(2)],
                    "B": [(Buf(), Buf()) for _ in range(2)], "i": [0]}

        with ExitStack() as ns:
            qaT = sbuf(ns, [128, 8, S_LEN], BF16, "qaT")
            kskwT = sbuf(ns, [128, 4, S_LEN], BF16, "kskwT")
            vsw = sbuf(ns, [128, NT, 4, 65], BF16, "vsw")
            ga = sbuf(ns, [128, NT, 24], F32, "ga")
            kcbT = sbuf(ns, [128, 2, 128], BF16, "kcbT")
            vcb = sbuf(ns, [128, 2, 64], BF16, "vcb")
            kcbTB, vcbB = Buf(), Buf()
            selRB = [[Buf() for _ in range(NT)] for _ in range(2)]
            eB_ = Buf()
            S.op(S.pool, lambda e: e.memset(qaT[64:96, :, :], 0.0), writes=[b_ for g_ in selRB for b_ in g_])
            S.op(S.pool, lambda e: e.memset(kskwT[64:96, 2:4, :], 0.0), writes=[eB_])
            S.op(S.pool, lambda e: e.memset(kcbT[:], 0.0), writes=[kcbTB])
            for g_ in range(2):
                S.dma(S.pool, kskwT[64:96, g_, :], D["e_full"], writes=[eB_])
            ks_scope = ns.enter_context(ExitStack())
            kcvcT = sbuf(ks_scope, [128, 2, S_LEN], BF16, "kcvcT")
            qaB = [Buf() for _ in range(NT)]
            kkB = [Buf() for _ in range(NT)]
            kcB = [Buf() for _ in range(NT)]
            vswB = [Buf() for _ in range(NT)]
            gaB = [Buf() for _ in range(NT)]
            S.op(S.pool, lambda e: e.memset(vsw[:], 1.0), writes=vswB)
            with ExitStack() as st:
                rt = make_rt(st)
                tnorm = make_tile_norm(st, D["mix_norm"][l:l + 1, :], nstats)
                wn = sbuf(st, [128, 8, 1304], BF16, "wn")
                wnB = [Buf() for _ in range(3)]
                segs = [(0, 0, 512, 0), (512, 536, 664, 1), (640, 664, 792, 1), (768, 792, 920, 1), (896, 1048, 1176, 1),
                        (1024, 920, 1048, 2), (1152, 1176, 1304, 2), (1280, 512, 536, 2)]
                for (dst, a, b, wb) in segs:
                    S.dma(S.pool, wn[:, :, dst:dst + (b - a)], win_v[:, :, a:b], reads=[], writes=[wnB[wb]])
                pA = [psum(st, [128, 512], F32, "pA") for _ in range(2)]
                pBk = [psum(st, [128, 512], F32, "pB")] * 2
                pC = psum(st, [128, 512], F32, "pC")
                tq = psum(st, [128, 8, 128], BF16, "tq")
                tkk = psum(st, [128, 6, 128], BF16, "tkk")
                tk = tkk[:, 0:4, :]
                tkv = tkk[:, 4:6, :]
                pAB = [PBuf(), PBuf()]
                pBB = [PBuf()] * 2
                pCB, tqB, tkB = PBuf(), PBuf(), PBuf()
                tkvB = tkB
                qr = [sbuf(st, [128, 8, 128], BF16, "qr") for _ in range(2)]
                kr = [sbuf(st, [128, 4, 128], BF16, "kr") for _ in range(2)]
                kv = [sbuf(st, [128, 256], BF16, "kv") for _ in range(2)]
                qrB = [Buf(), Buf()]
                krB = [Buf(), Buf()]
                kvB = [Buf(), Buf()]
                for i_ in range(2):
                    S.op(S.pool, lambda e: e.memset(qr[i_][:], 0.0), writes=[qrB[i_]])
                    S.op(S.pool, lambda e: e.memset(kr[i_][:], 0.0), writes=[krB[i_]])
                hts = {}

                def s_norm(t):
                    hts[t] = tnorm(t)

                def s_pa(t):
                    hTt, hTtB = hts[t]
                    S.mm(pA[t % 2][:], [(hTt[:, c, :], wn[:, c, 0:512]) for c in range(8)], reads=[hTtB, wnB[0]], writes=[pAB[t % 2]])

                def s_pbc(t):
                    hTt, hTtB = hts[t]
                    S.mm(pBk[0][:], [(hTt[:, c, :], wn[:, c, 512:1024]) for c in range(8)], reads=[hTtB, wnB[1]], writes=[pBB[0]])
                    S.mm(pC[:, 0:280], [(hTt[:, c, :], wn[:, c, 1024:1304]) for c in range(8)], reads=[hTtB, wnB[2]], writes=[pCB])

                def s_post(t):
                    i = t % 2
                    tsl = slice(t * 128, (t + 1) * 128)
                    cs = (rope[:, t, 0, :], rope[:, t, 1, :])
                    rope_apply(S, rt, pA[i][:].rearrange("p (h d) -> p h d", h=8), 8, cs, qr[i][:, :, 0:64],
                               reads=[pAB[i], ropeB], writes=[qrB[i]])
                    S.op(S.act, lambda e: e.activation(kv[i][:], pBk[0][:, 0:256], AF.Copy), reads=[pBB[0]], writes=[kvB[i]])
                    rope_apply(S, rt, pBk[0][:, 256:512].rearrange("p (h d) -> p h d", h=4), 4, cs,
                               kr[i][:, :, 0:64], reads=[pBB[0], ropeB], writes=[krB[i]])
                    S.op(S.act, lambda e: e.activation(vsw[:, t, :, 0:64], pC[:, 0:256].rearrange("p (h d) -> p h d", h=4), AF.Copy),
                         reads=[pCB], writes=[vswB[t]])
                    S.op(S.act, lambda e: e.activation(ga[:, t, :], pC[:, 256:280], AF.Copy), reads=[pCB], writes=[gaB[t]])
                    for c in range(8):
                        S.op(S.pe, lambda e: e.transpose(tq[:, c, :], qr[i][:, c, :], ident[:]),
                             reads=[qrB[i], identB], writes=[tqB])
                    S.op(S.act, lambda e: e.activation(qaT[0:64, :, tsl], tq[0:64, :, :], AF.Copy), reads=[tqB], writes=[qaB[t]])
                    for c in range(2):
                        S.op(S.pe, lambda e: e.transpose(tkv[:, c, :], kv[i][:, c * 128:(c + 1) * 128], ident[:]),
                             reads=[kvB[i], identB], writes=[tkvB])
                    S.op(S.act, lambda e: e.activation(kcvcT[:, :, tsl], tkv, AF.Copy), reads=[tkvB], writes=[kcB[t]])
                    for c in range(4):
                        S.op(S.pe, lambda e: e.transpose(tk[:, c, :], kr[i][:, c, :], ident[:]),
                             reads=[krB[i], identB], writes=[tkB])
                    S.op(S.act, lambda e: e.activation(kskwT[0:64, :, tsl], tk[0:64, :, :], AF.Copy), reads=[tkB], writes=[kkB[t]])

                print('SBUF remaining in NSA proj:', nc.sbuf_bytes_remaining)
                s_norm(0)
                s_pa(0)
                s_pbc(0)
                s_norm(1)
                for t in range(NT):
                    if t + 2 < NT:
                        s_norm(t + 2)
                    if t + 1 < NT:
                        s_pa(t + 1)
                    s_post(t)
                    if t + 1 < NT:
                        s_pbc(t + 1)
                S.op(S.act, lambda e: e.activation(ga[:], ga[:], AF.Sigmoid), reads=gaB, writes=gaB)
                S.barrier()
            if STOP[0] == "nsaproj":
                return
            dbg_out("qaT", qaT[:], qaB)
            dbg_out("kskwT", kskwT[:], kkB)
            dbg_out("ga", ga[:], gaB)

            with ExitStack() as st:
                rt = make_rt(st)
                w1 = [sbuf(st, [128, 32, 128], BF16, "w1") for _ in range(2)]
                posT = [sbuf(st, [128, 32], BF16, "posT") for _ in range(2)]
                posj = [sbuf(st, [32, 64], BF16, "posj") for _ in range(2)]
                w2 = [sbuf(st, [128, 64], BF16, "w2") for _ in range(2)]
                ropec = sbuf(st, [128, 2, 64], F32, "ropec")
                wB = Buf()
                for kvi, nm in enumerate(("cmpk", "cmpv")):
                    w1v = D[nm + "_w1"][l].rearrange("(j d) h -> d j h", d=64)
                    for hlf in range(2):
                        S.dma(S.pool, w1[kvi][hlf * 64:(hlf + 1) * 64, :, :], w1v, writes=[wB])
                    S.dma(S.pool, posj[kvi][:], D[nm + "_pos"][l], writes=[wB])
                    S.dma(S.pool, w2[kvi][:], D[nm + "_w2"][l], writes=[wB])
                S.dma(S.sp, ropec[:], D["rope_c"], writes=[wB])
                ph = [psum(st, [128, 128], F32, "ph") for _ in range(2)]
                pb = psum(st, [128, 8], F32, "pb")
                pk = psum(st, [128, 64], F32, "pk")
                tpk = psum(st, [128, 128], BF16, "tpk")
                phB = [PBuf(), PBuf()]
                pbB, pkB, tpkB = PBuf(), PBuf(), PBuf()
                bias = sbuf(st, [128, 2], F32, "bias")
                biasB = Buf()
                hb = sbuf(st, [128, 128], F32, "hb")
                h2 = sbuf(st, [128, 128], F32, "h2")
                uu = sbuf(st, [128, 128], F32, "uu")
                gl = sbuf(st, [128, 128], BF16, "gl")
                kcr = sbuf(st, [128, 2, 64], BF16, "kcr")
                hbB, h2B, uuB, glB, kcrB = Buf(), Buf(), Buf(), Buf(), Buf()
                posTB = Buf()
                for kvi in range(2):
                    S.op(S.pe, lambda e: e.transpose(tpk[0:64, 0:32], posj[kvi][:], ident[0:32, 0:32]), reads=[wB, identB], writes=[tpkB])
                    S.op(S.act, lambda e: e.activation(posT[kvi][0:64, :], tpk[0:64, 0:32], AF.Copy), reads=[tpkB], writes=[posTB])
                for kvi in range(2):
                    S.mm(pb[:, kvi:kvi + 1], [(w1[kvi][0:64, j, :], posT[kvi][0:64, j:j + 1]) for j in range(32)],
                         reads=[wB, posTB], writes=[pbB])
                S.op(S.act, lambda e: e.activation(bias[:], pb[:, 0:2], AF.Copy), reads=[pbB], writes=[biasB])
                it = 0
                for kvi in range(2):
                    for g in range(2):
                        k = it % 2
                        it += 1
                        pairs = []
                        for j in range(32):
                            base = kcvcT[g * 64:(g + 1) * 64, kvi, j:j + 1]
                            rhs = bass.AP(base.tensor, base.offset, [list(base.ap[0]), [16, 127]])
                            pairs.append((w1[kvi][g * 64:(g + 1) * 64, j, :], rhs))
                        S.mm(ph[k][:, 0:127], pairs, reads=[wB] + kcB, writes=[phB[k]])
                        hv, h2v, uv = hb[:, 0:127], h2[:, 0:127], uu[:, 0:127]
                        S.op(S.act, lambda e: e.activation(hv, ph[k][:, 0:127], AF.Identity, bias=bias[:, kvi:kvi + 1]),
                             reads=[phB[k], biasB], writes=[hbB])
                        S.op(S.dve, lambda e: e.tensor_tensor(h2v, hv, hv, ALU.mult), reads=[hbB], writes=[h2B])
                        S.op(S.dve, lambda e: e.tensor_scalar(h2v, h2v, 0.044715, 1.0, ALU.mult, ALU.add), reads=[h2B], writes=[h2B])
                        S.op(S.dve, lambda e: e.tensor_tensor(uv, h2v, hv, ALU.mult), reads=[h2B, hbB], writes=[uuB])
                        S.op(S.act, lambda e: e.activation(uv, uv, AF.Exp, scale=-1.5957691216057308), reads=[uuB], writes=[uuB])
                        S.op(S.dve, lambda e: e.tensor_scalar(uv, uv, 1.0, None, ALU.add), reads=[uuB], writes=[uuB])
                        S.op(S.dve, lambda e: e.reciprocal(uv, uv), reads=[uuB], writes=[uuB])
                        S.op(S.dve, lambda e: e.tensor_tensor(gl[:, 0:127], hv, uv, ALU.mult), reads=[uuB, hbB], writes=[glB])
                        S.mm(pk[0:127, :], [(gl[:, 0:127], w2[kvi][:])], reads=[glB, wB], writes=[pkB])
                        if kvi == 0:
                            rope_apply(S, rt, pk[0:127, :].rearrange("p (h d) -> p h d", h=1), 1,
                                       (ropec[0:127, 0, :], ropec[0:127, 1, :]),
                                       kcr[0:127, :, :].rearrange("p (h u) d -> p h u d", h=1),
                                       reads=[pkB, wB], writes=[kcrB], dup=2)
                            S.op(S.pe, lambda e: e.transpose(tpk[:, 0:127], kcr[0:127, :, :].rearrange("p a d -> p (a d)"), ident[0:127, 0:127]),
                                 reads=[kcrB, identB], writes=[tpkB])
                            S.op(S.act, lambda e: e.activation(kcbT[0:64, g, 0:127], tpk[0:64, 0:127], AF.Copy), reads=[tpkB], writes=[kcbTB])
                        else:
                            S.op(S.act, lambda e: e.activation(vcb[0:127, g, :], pk[0:127, :], AF.Copy), reads=[pkB], writes=[vcbB])
                S.barrier()
            ks_scope.close()
            if STOP[0] == "compress":
                return
            dbg_out("kcbT", kcbT[:], [kcbTB])
            dbg_out("vcb", vcb[:], [vcbB])

            with ExitStack() as st:
                cmask = sbuf(st, [128, NT, 128], F32, "cmask")
                selbias = sbuf(st, [128, NT, 32], F32, "selbias")
                wmask = sbuf(st, [128, 8, 512], BF16, "wmask")
                cB = Buf()
                caus = sbuf(st, [128, 4, 512], BF16, "caus")
                causB = Buf()
                S.dma(S.pool, caus[:], D["caus"], writes=[causB])
                S.dma(S.sp, cmask[:], D["cmask"], writes=[cB])
                S.dma(S.sp, selbias[:], D["selbias"], writes=[cB])
                S.dma(S.pool, wmask[:], D["wmask"], writes=[cB])
                sc = psum(st, [128, 4, 128], F32, "sc")
                tpb = psum(st, [128, 5, 128], BF16, "tpb")
                ocotp = psum(st, [128, 512], F32, "ocotp")
                oc = ocotp[:, 0:256].rearrange("p (a b) -> p a b", a=4)
                otp = ocotp[:, 0:260].rearrange("p (a b) -> p a b", a=4)
                stp = [psum(st, [128, 512], F32, "stp") for _ in range(3)]
                oT = [psum(st, [128, 512], F32, "oT") for _ in range(2)]
                scB, tpbB, ocB = PBuf(), PBuf(), PBuf()
                otpB = ocB
                tpsB = tpbB
                stB = [PBuf(), PBuf(), PBuf()]
                oTB = [PBuf(), PBuf()]
                TL = []
                for g_ in range(2):
                    T = {}
                    T["sm"] = sbuf(st, [128, 4, 127], F32, "sm")
                    T["pb16"] = sbuf(st, [128, 4, 128], BF16, "pb16")
                    T["pT"] = sbuf(st, [128, 4, 128], BF16, "pT")
                    T["pp"] = sbuf(st, [128, 132], F32, "pp")
                    T["st8"] = sbuf(st, [128, 8], F32, "st8")
                    T["imp"] = sbuf(st, [128, 32], F32, "imp")
                    T["imp3"] = sbuf(st, [128, 32], F32, "imp3")
                    T["m8"] = sbuf(st, [128, 16], F32, "m8")
                    T["selb"] = sbuf(st, [128, 96], BF16, "selb")
                    T["etmp2"] = sbuf(st, [128, 4, 64], F32, "etmp2")
                    T["B"] = [Buf() for _ in range(9)]
                    TL.append(T)
                nacc2 = [sbuf(st, [128, 4, 8, 64], F32, "nacc") for _ in range(2)]
                nb16 = [sbuf(st, [128, 512], BF16, "nb16") for _ in range(2)]
                PT = [sbuf(st, [128, 512], BF16, "PT") for _ in range(3)]
                oTs = sbuf(st, [65, 512], F32, "oTs")
                ew = sbuf(st, [128, 8], F32, "ew")
                etmp = sbuf(st, [128, 4, 64], F32, "etmp")
                naccB2 = [[Buf() for _ in range(4)] for _ in range(2)]
                oTsB, ewB, etmpB = Buf(), Buf(), Buf()
                nb16B = [Buf(), Buf()]
                PTB = [Buf() for _ in range(3)]

                def writeout_gen(qb):
                    nacc = nacc2[qb % 2]
                    naccB = naccB2[qb % 2]
                    for tl in range(4):
                        t = 4 * qb + tl
                        i = tl % 2
                        S.op(S.dve, lambda e: e.tensor_copy(nb16[i][:], nacc[:, tl, :, :].rearrange("p h d -> p (h d)")),
                             reads=[naccB[tl]], writes=[nb16B[i]])
                        yield
                        for c in range(4):
                            S.op(S.pe, lambda e: e.transpose(tpb[:, c, :], nb16[i][:, c * 128:(c + 1) * 128], ident[:]),
                                 reads=[nb16B[i], identB], writes=[tpbB])
                        S.op(S.act, lambda e: e.activation(nsaT[:, :, t * 128:(t + 1) * 128], tpb[:, 0:4, :], AF.Copy),
                             reads=[tpbB], writes=[nsaTB[t]])
                        yield
                for T in TL:
                    S.op(S.dve, lambda e: e.memset(T["pp"][:], 0.0), writes=[T["B"][3]])
                    S.op(S.dve, lambda e: e.memset(T["pb16"][:], 0.0), writes=[T["B"][1]])
                    S.op(S.dve, lambda e: e.memset(T["selb"][:], 0.0), writes=[T["B"][7]])
                sti = [0]
                pti = [0]
                oti = [0]

                def attn_items(items, qb, h, kT_idx, v_idx, chunks, gate_col, acc_tile, accB, qT, qTB, kT, kTB, vt, vtB, mask_fn, extra_reads):
                    hs = slice(0, 96)
                    pair = h
                    ob = oti[0] % 2
                    oti[0] += 1
                    n = len(chunks)
                    for ci, (kc, j0, j1) in enumerate(chunks):
                        sb_ = sti[0] % 3
                        sti[0] += 1
                        pb_ = pti[0] % 3
                        pti[0] += 1
                        qsl = slice(qb * 512 + j0, qb * 512 + j1)

                        def qk(kc=kc, j0=j0, j1=j1, sb_=sb_, qsl=qsl):
                            pairs = [(kT[hs, kT_idx, kc * 128:(kc + 1) * 128], qT[hs, pair, qsl])] + mask_fn(kc, j0, j1)
                            S.mm(stp[sb_][:, j0:j1], pairs,
                                 reads=[kTB[kc]] + [qTB[4 * qb + q] for q in range(j0 // 128, (j1 + 127) // 128)] + extra_reads,
                                 writes=[stB[sb_]])

                        def ex(j0=j0, j1=j1, sb_=sb_, pb_=pb_):
                            S.op(S.act, lambda e: e.activation(PT[pb_][:, j0:j1], stp[sb_][:, j0:j1], AF.Exp, scale=0.125),
                                 reads=[stB[sb_]], writes=[PTB[pb_]])

                        def pv(kc=kc, j0=j0, j1=j1, pb_=pb_, ci=ci):
                            S.mm1(oT[ob][0:65, j0:j1], vt[:, kc, v_idx, 0:65], PT[pb_][:, j0:j1], start=(ci == 0), stop=(ci == n - 1),
                                  reads=[PTB[pb_], vtB[kc]], writes=[oTB[ob]])
                        epi_a = epi_b = None
                        if ci == n - 1:
                            def epi_a():
                                S.op(S.dve, lambda e: e.tensor_copy(oTs[:], oT[ob][0:65, :]), reads=[oTB[ob]], writes=[oTsB])

                            def epi_b():
                                for tl in range(4):
                                    S.op(S.pe, lambda e: e.transpose(otp[:, tl, :], oTs[:, tl * 128:(tl + 1) * 128], identf[0:65, 0:65]),
                                         reads=[oTsB, identB], writes=[otpB])
                                S.op(S.dve, lambda e: e.tensor_scalar(ew[:, 0:4], otp[:, :, 64], 1e-30, None, ALU.max), reads=[otpB], writes=[ewB])
                                S.op(S.dve, lambda e: e.reciprocal(ew[:, 0:4], ew[:, 0:4]), reads=[ewB], writes=[ewB])
                                S.op(S.dve, lambda e: e.tensor_tensor(ew[:, 0:4], ew[:, 0:4], ga[:, 4 * qb:4 * qb + 4, gate_col], ALU.mult),
                                     reads=[ewB] + gaB[4 * qb:4 * qb + 4], writes=[ewB])
                                S.op(S.dve, lambda e: e.tensor_tensor(etmp[:], otp[:, :, 0:64], bcast(ew[:, 0:4], 2, 64), ALU.mult),
                                     reads=[otpB, ewB], writes=[etmpB])
                                S.op(S.pool, lambda e: e.tensor_tensor(acc_tile[:, :, h, :], acc_tile[:, :, h, :], etmp[:], ALU.add),
                                     reads=[etmpB] + accB, writes=accB)
                        items.append((qk, ex, pv, epi_a, epi_b))

                if STOP[0] and STOP[0].startswith("budget:"):
                    S.budget = int(STOP[0].split(":")[1])
                def part_a_gen(qb):
                    nacc = nacc2[qb % 2]
                    naccB = naccB2[qb % 2]
                    S.op(S.pool, lambda e: e.memset(nacc[:], 0.0), writes=naccB)
                    yield
                    gens = [part_a_chain(qb, 0), part_a_chain(qb, 1)]
                    while gens:
                        for gn in list(gens):
                            if next(gn, "done") == "done":
                                gens.remove(gn)
                            else:
                                yield

                def part_a_chain(qb, g):
                    nacc = nacc2[qb % 2]
                    naccB = naccB2[qb % 2]
                    T = TL[g]
                    sm, pb16, pT, pp, st8, imp, imp3, m8, selb, etmp2 = (T[k_] for k_ in ("sm", "pb16", "pT", "pp", "st8", "imp", "imp3", "m8", "selb", "etmp2"))
                    smB, pb16B, pTB, ppB, st8B, impB, m8B, selbB, etmp2B = T["B"]
                    ppw = bass.AP(pp[:, 0:1].tensor, pp[:, 0:1].offset, [list(pp[:, 0:1].ap[0]), [4, 32], [1, 5]])
                    if True:
                        for tl in range(4):
                            t = 4 * qb + tl
                            tsl = slice(t * 128, (t + 1) * 128)
                            for r in range(4):
                                h = 4 * g + r
                                S.mm(sc[:, r, 0:127], [(qaT[0:96, h, tsl], kcbT[0:96, g, 0:127])],
                                     reads=[qaB[t], kcbTB], writes=[scB])
                            S.op(S.dve, lambda e: e.tensor_tensor(sm[:], sc[:, :, 0:127], bcast(cmask[:, t, 0:127], 1, 4), ALU.add),
                                 reads=[scB, cB], writes=[smB])
                            yield
                            S.op(S.act, lambda e: e.activation(sm[:], sm[:], AF.Exp, scale=0.125), reads=[smB], writes=[smB])
                            yield
                            S.op(S.dve, lambda e: e.tensor_reduce(st8[:, 0:4], sm[:], AX.X, ALU.add), reads=[smB], writes=[st8B])
                            S.op(S.dve, lambda e: e.tensor_scalar(st8[:, 0:4], st8[:, 0:4], 1e-30, None, ALU.max), reads=[st8B], writes=[st8B])
                            yield
                            S.op(S.dve, lambda e: e.reciprocal(st8[:, 0:4], st8[:, 0:4]), reads=[st8B], writes=[st8B])
                            S.op(S.dve, lambda e: e.tensor_tensor(sm[:], sm[:], bcast(st8[:, 0:4], 2, 127), ALU.mult),
                                 reads=[smB, st8B], writes=[smB])
                            yield
                            S.op(S.dve, lambda e: e.tensor_copy(pb16[:, :, 0:127], sm[:]), reads=[smB], writes=[pb16B])
                            S.op(S.dve, lambda e: e.tensor_reduce(pp[:, 1:128], sm[:].rearrange("p r c -> p c r"), AX.X, ALU.add),
                                 reads=[smB], writes=[ppB])
                            yield
                            S.op(S.dve, lambda e: e.tensor_reduce(imp[:], ppw, AX.X, ALU.add), reads=[ppB], writes=[impB])
                            S.op(S.dve, lambda e: e.tensor_tensor(imp[:], imp[:], selbias[:, t, :], ALU.add), reads=[impB, cB], writes=[impB])
                            yield
                            S.op(S.dve, lambda e: e.max(m8[:, 0:8], imp[:]), reads=[impB], writes=[m8B])
                            S.op(S.dve, lambda e: e.match_replace(imp3[:], m8[:, 0:8], imp[:], -3.0e38), reads=[impB, m8B], writes=[m8B])
                            yield
                            S.op(S.dve, lambda e: e.max(m8[:, 8:16], imp3[:]), reads=[m8B], writes=[m8B])
                            S.op(S.dve, lambda e: e.tensor_scalar(imp3[:], imp[:], m8[:, 15:16], None, ALU.is_ge), reads=[impB, m8B], writes=[m8B])
                            yield
                            S.op(S.dve, lambda e: e.tensor_scalar(selb[:, 64:96], imp3[:], -NEG_BIG, NEG_BIG, ALU.mult, ALU.add),
                                 reads=[m8B], writes=[selbB])
                            yield
                            S.op(S.pe, lambda e: e.transpose(tpb[0:96, 4, :], selb[:], ident[:]), reads=[selbB, identB], writes=[tpsB])
                            S.op(S.dve, lambda e: e.tensor_copy(qaT[64:96, 4 * g:4 * g + 4, tsl], bcast(tpb[64:96, 4, :], 1, 4)),
                                 reads=[tpsB], writes=[selRB[g][t]])
                            yield
                            for r in range(4):
                                S.op(S.pe, lambda e: e.transpose(tpb[0:127, r, :], pb16[:, r, 0:127], ident[:]),
                                     reads=[pb16B, identB], writes=[tpbB])
                            S.op(S.dve, lambda e: e.tensor_copy(pT[0:127, :, :], tpb[0:127, 0:4, :]), reads=[tpbB], writes=[pTB])
                            yield
                            for r in range(4):
                                S.mm(oc[:, r, :], [(pT[0:127, r, :], vcb[0:127, g, :])], reads=[pTB, vcbB], writes=[ocB])
                            S.op(S.dve, lambda e: e.tensor_tensor(etmp2[:], oc[:], bcast(ga[:, t, 12 * g:12 * g + 12:3], 2, 64), ALU.mult),
                                 reads=[ocB, gaB[t]], writes=[etmp2B])
                            S.op(S.pool, lambda e: e.tensor_tensor(nacc[:, tl, 4 * g:4 * g + 4, :], nacc[:, tl, 4 * g:4 * g + 4, :], etmp2[:], ALU.add),
                                 reads=[etmp2B, naccB[tl]], writes=[naccB[tl]])
                            yield

                print('SBUF remaining in NSA attn:', nc.sbuf_bytes_remaining)
                gen_a = part_a_gen(0)
                next(gen_a)
                wo_prev = None
                for qb in range(4):
                    nacc = nacc2[qb % 2]
                    naccB = naccB2[qb % 2]
                    win_items = []
                    sel_items = []
                    for h in range(8):
                        g = h // 4
                        order = [1, 0, 2, 3, 4, -1, -2, -3]
                        chunks = []
                        for e_ in order:
                            kc = 4 * qb - e_
                            if kc < 0 or kc > 4 * qb + 3:
                                continue
                            if e_ >= 1:
                                j0, j1 = 0, min(512, 640 - 128 * e_)
                            else:
                                j0, j1 = -128 * e_, 512
                            chunks.append((kc, j0, j1))

                        def win_mask(kc, j0, j1, qb=qb):
                            return [(ident[:], wmask[:, 4 * qb - kc + 3, j0:j1])]
                        attn_items(win_items, qb, h, 2 + g, 2 + g, chunks, 3 * h + 2, nacc, naccB, qaT, qaB, kskwT, kkB, vsw, vswB, win_mask,
                                   [cB, identB])
                    for h in range(8):
                        g = h // 4
                        chunks = []
                        for kc in range(4 * qb + 4):
                            d = kc - 4 * qb
                            chunks.append((kc, 128 * d if d > 0 else 0, 512))

                        def sel_mask(kc, j0, j1, g=g, qb=qb, h=h):
                            m = []
                            d = kc - 4 * qb
                            if d >= 0:
                                m.append((ident[:], caus[:, d, j0:j1]))
                            return m
                        attn_items(sel_items, qb, h, g, g, chunks, 3 * h + 1, nacc, naccB, qaT, qaB, kskwT, kkB, vsw, vswB, sel_mask,
                                   selRB[g][4 * qb:4 * qb + 4] + [eB_, cB, causB, identB])
                    fill_ = gen_a if wo_prev is None else itertools.chain(wo_prev, gen_a)
                    run_pipeline(win_items, filler=fill_, per_item=(8 * 14) // len(win_items) + 1)
                    gen_a = part_a_gen(qb + 1) if qb < 3 else None
                    if gen_a is not None:
                        next(gen_a)
                    run_pipeline(sel_items, filler=gen_a, per_item=1, drain=False)
                    wo_prev = writeout_gen(qb)
                for _ in wo_prev:
                    pass
                S.barrier()
        dbg_out("nsaT", nsaT[:], nsaTB)
        if STOP[0] == "nsaattn":
            return

        mobaT = sbuf(ms, [128, 4, S_LEN], BF16, "mobaT")
        mw = ms.enter_context(ExitStack())
        pastb = sbuf(mw, [128, NT, 8], F32, "pastb")
        ownsel = sbuf(mw, [128, NT, 8], F32, "ownsel")
        cB2 = Buf()
        S.dma(S.sp, pastb[:], D["pastbias"], writes=[cB2])
        S.dma(S.sp, ownsel[:], D["ownsel"], writes=[cB2])
        wm_all = [sbuf(mw, [128, 8, 768], BF16, "wm") for _ in range(2)]
        wmB_all = [[Buf() for _ in range(3)] for _ in range(2)]
        for hp_ in range(2):
            for i in range(3):
                c0 = 1304 + 512 * i + 256 * hp_
                S.dma(S.pool, wm_all[hp_][:, :, i * 256:(i + 1) * 256], win_v[:, :, c0:c0 + 256], writes=[wmB_all[hp_][i]])
        for hp in range(2):
            with ExitStack() as mo:
                qbT = sbuf(mo, [128, 4, S_LEN], BF16, "qbT")
                kbT = sbuf(mo, [128, 4, S_LEN], BF16, "kbT")
                vb = sbuf(mo, [128, NT, 4, 65], BF16, "vb")
                qbB = [Buf() for _ in range(NT)]
                kbB = [Buf() for _ in range(NT)]
                vbB = [Buf() for _ in range(NT)]
                mselB = [Buf() for _ in range(NT)]
                e8B = Buf()
                S.op(S.pool, lambda e: e.memset(vb[:], 1.0), writes=vbB)
                S.op(S.pool, lambda e: e.memset(qbT[64:96, :, :], 0.0), writes=mselB)
                for j in range(4):
                    S.dma(S.pool, kbT[64:96, j, :], D["e8"][j], writes=[e8B])
                with ExitStack() as st:
                    rt = make_rt(st)
                    tnorm = make_tile_norm(st, D["mix_norm"][l:l + 1, :], nstats)
                    wm = wm_all[hp]
                    wmB = wmB_all[hp]
                    pA = [psum(st, [128, 512], F32, "pA") for _ in range(2)]
                    pC = [psum(st, [128, 256], F32, "pC") for _ in range(2)]
                    tqk = psum(st, [128, 8, 128], BF16, "tqk")
                    pAB = [PBuf(), PBuf()]
                    pCB = [PBuf(), PBuf()]
                    tqkB = PBuf()
                    qkr = [sbuf(st, [128, 8, 128], BF16, "qkr") for _ in range(2)]
                    qkrB = [Buf(), Buf()]
                    for i_ in range(2):
                        S.op(S.pool, lambda e: e.memset(qkr[i_][:], 0.0), writes=[qkrB[i_]])
                    hts = {}

                    def m_norm(t):
                        hts[t] = tnorm(t)

                    def m_pa(t):
                        hTt, hTtB = hts[t]
                        i = t % 2
                        S.mm(pA[i][:], [(hTt[:, c, :], wm[:, c, 0:512]) for c in range(8)], reads=[hTtB, wmB[0], wmB[1]], writes=[pAB[i]])

                    def m_pc(t):
                        hTt, hTtB = hts[t]
                        i = t % 2
                        S.mm(pC[i][:], [(hTt[:, c, :], wm[:, c, 512:768]) for c in range(8)], reads=[hTtB, wmB[2]], writes=[pCB[i]])

                    def m_post(t):
                        i = t % 2
                        tsl = slice(t * 128, (t + 1) * 128)
                        cs = (rope[:, t, 0, :], rope[:, t, 1, :])
                        rope_apply(S, rt, pA[i][:].rearrange("p (h d) -> p h d", h=8), 8, cs, qkr[i][:, :, 0:64],
                                   reads=[pAB[i], ropeB], writes=[qkrB[i]])
                        S.op(S.act, lambda e: e.activation(vb[:, t, :, 0:64], pC[i][:].rearrange("p (h d) -> p h d", h=4), AF.Copy),
                             reads=[pCB[i]], writes=[vbB[t]])
                        for c in range(8):
                            S.op(S.pe, lambda e: e.transpose(tqk[:, c, :], qkr[i][:, c, :], ident[:]),
                                 reads=[qkrB[i], identB], writes=[tqkB])
                        S.op(S.act, lambda e: e.activation(qbT[0:64, :, tsl], tqk[0:64, 0:4, :], AF.Copy), reads=[tqkB], writes=[qbB[t]])
                        S.op(S.act, lambda e: e.activation(kbT[0:64, :, tsl], tqk[0:64, 4:8, :], AF.Copy), reads=[tqkB], writes=[kbB[t]])

                    print('SBUF remaining in MoBA proj:', nc.sbuf_bytes_remaining)
                    m_norm(0)
                    m_pa(0)
                    m_pc(0)
                    m_norm(1)
                    for t in range(NT):
                        if t + 2 < NT:
                            m_norm(t + 2)
                        if t + 1 < NT:
                            m_pa(t + 1)
                            m_pc(t + 1)
                        m_post(t)
                    S.barrier()
                with ExitStack() as st:
                    kmf = sbuf(st, [128, 4, 8], F32, "kmf")
                    kmb = sbuf(st, [128, 4, 8], BF16, "kmb")
                    kmB = Buf()
                    S.op(S.dve, lambda e: e.tensor_reduce(kmf[0:64], kbT[0:64].rearrange("p c (b k) -> p c b k", b=8), AX.X, ALU.add),
                         reads=kbB, writes=[kmB])
                    S.op(S.dve, lambda e: e.tensor_scalar(kmb[0:64], kmf[0:64], 1.0 / 256.0, None, ALU.mult), reads=[kmB], writes=[kmB])
                    gs2 = sbuf(st, [128, 4, 8], F32, "gs2")
                    m8a = sbuf(st, [128, 4, 8], F32, "m8a")
                    thr = sbuf(st, [128, 4], F32, "thr")
                    sel = sbuf(st, [128, 4, 8], F32, "sel")
                    mb = sbuf(st, [128, 128], BF16, "mb")
                    gs2B, m8aB, thrB, selB_, mbB = Buf(), Buf(), Buf(), Buf(), Buf()
                    S.op(S.pool, lambda e: e.memset(mb[:], 0.0), writes=[mbB])

                    def gate_gen(tiles):
                        for t in tiles:
                            tsl = slice(t * 128, (t + 1) * 128)
                            for j in range(4):
                                S.mm(gp[:, j, :], [(qbT[0:64, j, tsl], kmb[0:64, j, :])], reads=[qbB[t], kmB], writes=[gpB])
                            yield
                            S.op(S.dve, lambda e: e.tensor_tensor(gs2[:], gp[:, 0:4, :], bcast(pastb[:, t, :], 1, 4), ALU.add),
                                 reads=[gpB, cB2], writes=[gs2B])
                            yield
                            for j in range(4):
                                S.op(S.dve, lambda e: e.max(m8a[:, j, :], gs2[:, j, :]), reads=[gs2B], writes=[m8aB])
                                if j % 2 == 1:
                                    yield
                            S.op(S.dve, lambda e: e.tensor_scalar(thr[:], m8a[:, :, 2], -1e29, None, ALU.max), reads=[m8aB], writes=[thrB])
                            S.op(S.dve, lambda e: e.tensor_tensor(sel[:], gs2[:], bcast(thr[:], 2, 8), ALU.is_ge), reads=[gs2B, thrB], writes=[selB_])
                            yield
                            S.op(S.dve, lambda e: e.tensor_tensor(sel[:], sel[:], bcast(ownsel[:, t, :], 1, 4), ALU.max), reads=[selB_, cB2], writes=[selB_])
                            S.op(S.dve, lambda e: e.tensor_scalar(mb[:, 64:96], sel[:].rearrange("p h b -> p (h b)"), -NEG_BIG, NEG_BIG, ALU.mult, ALU.add),
                                 reads=[selB_], writes=[mbB])
                            yield
                            S.op(S.pe, lambda e: e.transpose(tm, mb[:], ident[:]), reads=[mbB, identB], writes=[tmB])
                            S.op(S.dve, lambda e: e.tensor_copy(qbT[64:96, :, tsl], bcast(tm[64:96, :], 1, 4)), reads=[tmB], writes=[mselB[t]])
                            yield

                    caus = sbuf(st, [128, 4, 512], BF16, "caus")
                    causB = Buf()
                    S.dma(S.pool, caus[:], D["caus"], writes=[causB])
                    stp = [psum(st, [128, 512], F32, "stp") for _ in range(4)]
                    oT = [psum(st, [128, 512], F32, "oT") for _ in range(2)]
                    otpg = psum(st, [128, 512], F32, "otpg")
                    otp = otpg[:, 0:260].rearrange("p (a b) -> p a b", a=4)
                    gp = otpg[:, 448:512].rearrange("p (a b) -> p a b", a=8)
                    tpbm = psum(st, [128, 8, 128], BF16, "tpbm")
                    tpb = tpbm[:, 0:4, :]
                    tm = tpbm[:, 4, :]
                    stB = [PBuf() for _ in range(4)]
                    oTB = [PBuf(), PBuf()]
                    otpB, tpbB = PBuf(), PBuf()
                    gpB = otpB
                    tmB = tpbB
                    print('SBUF remaining in MoBA attn (before macc etc):', nc.sbuf_bytes_remaining)
                    for _ in gate_gen(range(0, 4)):
                        pass
                    macc2 = [sbuf(st, [128, 4, 4, 64], F32, "macc") for _ in range(2)]
                    maccB2 = [[Buf() for _ in range(4)] for _ in range(2)]
                    nb16 = [sbuf(st, [128, 256], BF16, "nb16") for _ in range(2)]
                    PT = [sbuf(st, [128, 512], BF16, "PT") for _ in range(4)]
                    oTs = sbuf(st, [65, 512], F32, "oTs")
                    ew = sbuf(st, [128, 8], F32, "ew")
                    oTsB, ewB = Buf(), Buf()
                    nb16B = [Buf(), Buf()]
                    PTB = [Buf() for _ in range(4)]
                    sti = 0
                    pti = 0
                    oti = 0

                    def m_writeout_gen(qb):
                        macc = macc2[qb % 2]
                        maccB = maccB2[qb % 2]
                        for tl in range(4):
                            t = 4 * qb + tl
                            i = tl % 2
                            S.op(S.dve, lambda e: e.tensor_copy(nb16[i][:], macc[:, tl, :, :].rearrange("p h d -> p (h d)")),
                                 reads=[maccB[tl]], writes=[nb16B[i]])
                            yield
                            for c in range(2):
                                S.op(S.pe, lambda e: e.transpose(tpb[:, c, :], nb16[i][:, c * 128:(c + 1) * 128], ident[:]),
                                     reads=[nb16B[i], identB], writes=[tpbB])
                            S.op(S.act, lambda e: e.activation(mobaT[:, 2 * hp:2 * hp + 2, t * 128:(t + 1) * 128], tpb[:, 0:2, :], AF.Copy),
                                 reads=[tpbB], writes=[mobaTB[t]])
                            yield
                    m_wo_prev = None
                    for qb in range(4):
                        macc = macc2[qb % 2]
                        maccB = maccB2[qb % 2]
                        items = []
                        for j in range(4):
                            ob = oti % 2
                            oti += 1
                            nch = 4 * qb + 4
                            for kc in range(nch):
                                d = kc - 4 * qb
                                j0 = 128 * d if d > 0 else 0
                                sb_ = sti % 4
                                sti += 1
                                pb_ = pti % 4
                                pti += 1
                                qsl = slice(qb * 512 + j0, (qb + 1) * 512)

                                def qk(kc=kc, d=d, j0=j0, sb_=sb_, qsl=qsl, j=j, qb=qb):
                                    pairs = [(kbT[0:96, j, kc * 128:(kc + 1) * 128], qbT[0:96, j, qsl])]
                                    if d >= 0:
                                        pairs.append((ident[:], caus[:, d, j0:512]))
                                    S.mm(stp[sb_][:, j0:512], pairs,
                                         reads=[kbB[kc], e8B, causB, identB] + [qbB[4 * qb + q] for q in range(j0 // 128, 4)]
                                         + [mselB[4 * qb + q] for q in range(j0 // 128, 4)], writes=[stB[sb_]])

                                def ex(j0=j0, sb_=sb_, pb_=pb_):
                                    S.op(S.act, lambda e: e.activation(PT[pb_][:, j0:512], stp[sb_][:, j0:512], AF.Exp, scale=0.125),
                                         reads=[stB[sb_]], writes=[PTB[pb_]])

                                def pv(kc=kc, j0=j0, pb_=pb_, j=j, ob=ob, nch=nch):
                                    S.mm1(oT[ob][0:65, j0:512], vb[:, kc, j, 0:65], PT[pb_][:, j0:512], start=(kc == 0), stop=(kc == nch - 1),
                                          reads=[PTB[pb_], vbB[kc]], writes=[oTB[ob]])
                                epi_a = epi_b = None
                                if kc == nch - 1:
                                    def epi_a(ob=ob):
                                        S.op(S.dve, lambda e: e.tensor_copy(oTs[:], oT[ob][0:65, :]), reads=[oTB[ob]], writes=[oTsB])

                                    def epi_b(j=j, macc=macc, maccB=maccB):
                                        for tl in range(4):
                                            S.op(S.pe, lambda e: e.transpose(otp[:, tl, :], oTs[:, tl * 128:(tl + 1) * 128], identf[0:65, 0:65]),
                                                 reads=[oTsB, identB], writes=[otpB])
                                        S.op(S.dve, lambda e: e.tensor_scalar(ew[:, 0:4], otp[:, :, 64], 1e-30, None, ALU.max), reads=[otpB], writes=[ewB])
                                        S.op(S.dve, lambda e: e.reciprocal(ew[:, 0:4], ew[:, 0:4]), reads=[ewB], writes=[ewB])
                                        S.op(S.dve, lambda e: e.tensor_tensor(macc[:, :, j, :], otp[:, :, 0:64], bcast(ew[:, 0:4], 2, 64), ALU.mult),
                                             reads=[otpB, ewB], writes=maccB)
                                items.append((qk, ex, pv, epi_a, epi_b))
                        fl = []
                        if m_wo_prev is not None:
                            fl.append(m_wo_prev)
                        if qb < 3:
                            fl.append(gate_gen(range(4 * qb + 4, 4 * qb + 8)))
                        if fl:
                            run_pipeline(items, la=3, filler=itertools.chain(*fl), per_item=(4 * 8 + 8) // len(items) + 1)
                        else:
                            run_pipeline(items, la=3)
                        m_wo_prev = m_writeout_gen(qb)
                    for _ in m_wo_prev:
                        pass
                    S.barrier()
        mw.close()
        dbg_out("mobaT", mobaT[:], mobaTB)
        if STOP[0] == "mobaattn":
            return

        with ExitStack() as st:
            wgt = sbuf(st, [128, 8, 2048], BF16, "wgt")
            wbn = sbuf(st, [128, 4, DM], BF16, "wbn")
            wbm = sbuf(st, [128, 4, DM], BF16, "wbm")
            wo = sbuf(st, [128, 8, DM], BF16, "wo")
            wB = [Buf() for _ in range(4)]
            for i in range(4):
                S.dma(S.pool, wgt[:, :, i * 512:(i + 1) * 512], win_v[:, :, 2840 + i * 512:2840 + (i + 1) * 512], writes=[wB[0]])
            S.dma(S.pool, wbn[:], D["w_branch_nsa"][l].rearrange("(c p) n -> p c n", p=128), writes=[wB[1]])
            S.dma(S.pool, wbm[:], D["w_branch_moba"][l].rearrange("(c p) n -> p c n", p=128), writes=[wB[2]])
            for i in range(2):
                S.dma(S.pool, wo[:, :, i * 512:(i + 1) * 512], D["w_out"][l].rearrange("(c p) n -> p c n", p=128)[:, :, i * 512:(i + 1) * 512],
                      writes=[wB[3]])
            tnorm = make_tile_norm(st, D["mix_norm"][l:l + 1, :], nstats)
            sa = sbuf(st, [128, 512], F32, "sa")
            sbb = sbuf(st, [128, 512], F32, "sbb")
            ma = sbuf(st, [128, 512], F32, "ma")
            mg = sbuf(st, [128, DM], BF16, "mg")
            mgT = sbuf(st, [128, 8, 128], BF16, "mgT")
            saB, sbB, maB, mgB, mgTB = (Buf() for _ in range(5))
            tpm = psum(st, [128, 8, 128], BF16, "tpm")
            pga = psum(st, [128, 512], F32, "pga")
            pya = psum(st, [128, 512], F32, "pya")
            pgb = psum(st, [128, 512], F32, "pgb")
            pyb = psum(st, [128, 512], F32, "pyb")
            po = [psum(st, [128, 512], F32, "po") for _ in range(2)]
            tpmB, pgaB, pyaB, pgbB, pybB = (PBuf() for _ in range(5))
            poB = [PBuf(), PBuf()]
            mg2 = sbuf(st, [128, DM], BF16, "mg2")
            mgs = [mg, mg2]
            mgBs = [mgB, Buf()]
            hts = {}
            pi = [0]

            def g_norm(t):
                hts[t] = tnorm(t)

            def g_gate(t):
                tsl = slice(t * 128, (t + 1) * 128)
                hTt, hTtB = hts[t]
                mgt, mgtB = mgs[t % 2], mgBs[t % 2]
                for hf in range(2):
                    cs_ = slice(hf * 512, (hf + 1) * 512)
                    S.mm(pga[:], [(hTt[:, c, :], wgt[:, c, hf * 512:(hf + 1) * 512]) for c in range(8)], reads=[hTtB, wB[0]], writes=[pgaB])
                    S.mm(pya[:], [(nsaT[:, c, tsl], wbn[:, c, cs_]) for c in range(4)], reads=[nsaTB[t], wB[1]], writes=[pyaB])
                    S.mm(pgb[:], [(hTt[:, c, :], wgt[:, c, 1024 + hf * 512:1024 + (hf + 1) * 512]) for c in range(8)], reads=[hTtB, wB[0]], writes=[pgbB])
                    S.mm(pyb[:], [(mobaT[:, c, tsl], wbm[:, c, cs_]) for c in range(4)], reads=[mobaTB[t], wB[2]], writes=[pybB])
                    S.op(S.act, lambda e: e.activation(sa[:], pga[:], AF.Sigmoid), reads=[pgaB], writes=[saB])
                    S.op(S.act, lambda e: e.activation(sbb[:], pgb[:], AF.Sigmoid), reads=[pgbB], writes=[sbB])
                    S.op(S.dve, lambda e: e.tensor_tensor(ma[:], sa[:], pya[:], ALU.mult), reads=[saB, pyaB], writes=[maB])
                    S.op(S.dve, lambda e: e.tensor_tensor(sbb[:], sbb[:], pyb[:], ALU.mult), reads=[sbB, pybB], writes=[sbB])
                    S.op(S.dve, lambda e: e.tensor_tensor(mgt[:, cs_], ma[:], sbb[:], ALU.add), reads=[maB, sbB], writes=[mgtB])

            def g_out(t):
                mgt, mgtB = mgs[t % 2], mgBs[t % 2]
                for c in range(8):
                    S.op(S.pe, lambda e: e.transpose(tpm[:, c, :], mgt[:, c * 128:(c + 1) * 128], ident[:]), reads=[mgtB, identB], writes=[tpmB])
                S.op(S.act, lambda e: e.activation(mgT[:], tpm[:], AF.Copy), reads=[tpmB], writes=[mgTB])
                for hf in range(2):
                    k = pi[0] % 2
                    pi[0] += 1
                    cs_ = slice(hf * 512, (hf + 1) * 512)
                    S.mm(po[k][:], [(mgT[:, c, :], wo[:, c, cs_]) for c in range(8)], reads=[mgTB, wB[3]], writes=[poB[k]])
                    S.op(S.dve, lambda e: e.tensor_tensor(x_sb[:, t, cs_], x_sb[:, t, cs_], po[k][:], ALU.add),
                         reads=[poB[k], xB[t]], writes=[xB[t]])

            print('SBUF remaining in merge:', nc.sbuf_bytes_remaining)
            g_norm(0)
            g_norm(1)
            g_gate(0)
            for t in range(NT):
                if t + 2 < NT:
                    g_norm(t + 2)
                if t + 1 < NT:
                    g_gate(t + 1)
                g_out(t)
            S.barrier()


_CACHE = {}


def kernel(**inputs):
    x = np.asarray(inputs["x"], dtype=np.float32)
    B = x.shape[0]
    consts = host_consts()
    shared = {}
    for name, shp in WEIGHT_SHAPES.items():
        shared[name] = np.ascontiguousarray(np.asarray(inputs[name], dtype=np.float32).reshape(shp))
    for name in CONST_SHAPES:
        shared[name] = consts[name]
    if "nc" not in _CACHE:
        _CACHE["nc"] = build_program()[0]
    nc = _CACHE["nc"]
    in_maps = []
    for b in range(B):
        m = dict(shared)
        m["x"] = np.ascontiguousarray(x[b])
        in_maps.append(m)
    res = run_bass_kernel_spmd(nc, in_maps, core_ids=list(range(B)))
    out = np.stack([np.asarray(res.results[b]["out"], dtype=np.float32) for b in range(B)], axis=0)
    return out
```

```python
import itertools
import numpy as np
from contextlib import ExitStack
import concourse.bass as bass
import concourse.mybir as mybir
from concourse.bass_utils import run_bass_kernel_spmd

F32 = mybir.dt.float32
BF16 = mybir.dt.bfloat16
AF = mybir.ActivationFunctionType
ALU = mybir.AluOpType
AX = mybir.AxisListType

S_LEN = 2048
DM = 1024
NT = 16
DFF = 2816
NFC = 22
IN_COLS = 4888
NEG_BIG = -30000.0
NEG = -1e30
EPS = 1e-6


class Buf:
    __slots__ = ("name", "lw", "rd", "excl")

    def __init__(self, name="", excl=False):
        self.name = name
        self.lw = None
        self.rd = []
        self.excl = excl


def PBuf():
    return Buf("psum", True)


class Track:
    def __init__(self, sem, step, name):
        self.sem = sem
        self.step = step
        self.count = 0
        self.name = name


class Eng:
    def __init__(self, obj, track, name, inorder=False):
        self.obj = obj
        self.track = track
        self.name = name
        self.seen = {}
        self.inorder = inorder


class Sched:
    def __init__(self, nc, stack, n_dma_tracks=16):
        self.nc = nc

        def mk(obj, name, inorder=False):
            sem = stack.enter_context(nc.semaphore("s_" + name))
            return Eng(obj, Track(sem, 1, name), name, inorder)

        self.pe = mk(nc.tensor, "pe", True)
        self.act = mk(nc.scalar, "act")
        self.dve = mk(nc.vector, "dve")
        self.pool = mk(nc.gpsimd, "pool")
        self.sp = mk(nc.sync, "sp")
        self.engs = [self.pe, self.act, self.dve, self.pool, self.sp]
        self.dma_pools = {}
        for e_ in (self.sp, self.pool):
            self.dma_pools[e_.name] = [
                Track(stack.enter_context(nc.semaphore("s_dma_%s%d" % (e_.name, i))), 16, "dma_%s%d" % (e_.name, i))
                for i in range(n_dma_tracks // 2)
            ]
        self.dma_tracks = self.dma_pools["sp"] + self.dma_pools["pool"]
        self.dma_i = {"sp": 0, "pool": 0}
        self.n_inst = 0
        self.n_wait = 0
        self.budget = None

    def _skip(self):
        if self.budget is None:
            return False
        if self.budget <= 0:
            return True
        self.budget -= 1
        return False

    def _wait(self, eng, track, val):
        if eng.inorder and track is eng.track:
            return
        if eng.seen.get(track, 0) >= val:
            return
        eng.obj.wait_ge(track.sem, val)
        eng.seen[track] = val
        self.n_wait += 1

    @staticmethod
    def _split(reads, writes):
        ex = [b for b in reads if b.excl]
        if ex:
            reads = [b for b in reads if not b.excl]
            writes = list(writes) + ex
        return reads, writes

    def _deps(self, eng, reads, writes):
        for b in reads:
            if b.lw is not None:
                self._wait(eng, *b.lw)
        for b in writes:
            if b.lw is not None:
                self._wait(eng, *b.lw)
            for r in b.rd:
                self._wait(eng, *r)

    def _commit(self, track, val, reads, writes):
        for b in reads:
            b.rd.append((track, val))
        for b in writes:
            b.lw = (track, val)
            b.rd = []

    def op(self, eng, fn, reads=(), writes=()):
        if self._skip():
            return None
        reads, writes = self._split(reads, writes)
        self._deps(eng, reads, writes)
        inst = fn(eng.obj)
        t = eng.track
        t.count += 1
        inst.then_inc(t.sem, 1)
        self._commit(t, t.count, reads, writes)
        self.n_inst += 1
        return inst

    def mm(self, out_ap, pairs, reads=(), writes=()):
        if self._skip():
            return None
        eng = self.pe
        reads, writes = self._split(reads, writes)
        self._deps(eng, reads, writes)
        n = len(pairs)
        inst = None
        for i, (l, r) in enumerate(pairs):
            inst = eng.obj.matmul(out_ap, l, r, start=(i == 0), stop=(i == n - 1))
            self.n_inst += 1
        t = eng.track
        t.count += 1
        inst.then_inc(t.sem, 1)
        self._commit(t, t.count, reads, writes)

    def mm1(self, out_ap, l, r, start, stop, reads=(), writes=(), skip=False):
        if self._skip():
            return None
        eng = self.pe
        reads, writes = self._split(reads, writes)
        self._deps(eng, reads, writes)
        if skip:
            inst = eng.obj.matmul(out_ap, l, r, start=start, stop=stop, skip_group_check=True)
        else:
            inst = eng.obj.matmul(out_ap, l, r, start=start, stop=stop)
        t = eng.track
        t.count += 1
        inst.then_inc(t.sem, 1)
        self._commit(t, t.count, reads, writes)
        self.n_inst += 1

    def dma(self, eng, out, in_, reads=(), writes=(), **kw):
        if self._skip():
            return None
        pool_ = self.dma_pools[eng.name]
        tr = pool_[self.dma_i[eng.name] % len(pool_)]
        self.dma_i[eng.name] += 1
        if tr.count:
            self._wait(eng, tr, tr.count * 16)
        self._deps(eng, reads, writes)
        inst = eng.obj.dma_start(out=out, in_=in_, **kw)
        tr.count += 1
        inst.then_inc(tr.sem, 16)
        self._commit(tr, tr.count * 16, reads, writes)
        self.n_inst += 1
        return inst

    def pe_fence(self):
        t = self.pe.track
        if t.count and self.pe.seen.get(t, 0) < t.count:
            self.pe.obj.wait_ge(t.sem, t.count)
            self.pe.seen[t] = t.count
            self.n_wait += 1

    def barrier(self):
        for e in self.engs:
            for o in self.engs:
                if o.track.count:
                    self._wait(e, o.track, o.track.count)
            for tr in self.dma_tracks:
                if tr.count:
                    self._wait(e, tr, tr.count * 16)

    def finish(self, eng=None):
        eng = eng or self.sp
        for o in self.engs:
            if o.track.count and o is not eng:
                self._wait(eng, o.track, o.track.count)
        for tr in self.dma_tracks:
            if tr.count:
                self._wait(eng, tr, tr.count * 16)


def bcast(ap, pos, n):
    l = [list(d) for d in ap.ap]
    l.insert(pos, [0, n])
    return bass.AP(ap.tensor, ap.offset, l)


def host_consts():
    c = {}
    inv = (np.float32(10000.0) ** (-np.arange(0, 64, 2, dtype=np.float32) / np.float32(64))).astype(np.float32)
    pos = np.arange(S_LEN, dtype=np.float32)
    ang = (pos[:, None] * inv[None, :]).astype(np.float32)
    cs = np.stack([np.cos(ang), np.sin(ang)], axis=1).astype(np.float32)
    csf = np.stack([np.concatenate([np.cos(ang), np.cos(ang)], axis=1),
                    np.concatenate([-np.sin(ang), np.sin(ang)], axis=1)], axis=1).astype(np.float32)
    c["rope_cs"] = np.ascontiguousarray(csf.reshape(NT, 128, 2, 64).transpose(1, 0, 2, 3))
    endp = (np.arange(127) * 16 + 31).astype(np.float32)
    angc = (endp[:, None] * inv[None, :]).astype(np.float32)
    rc = np.zeros((128, 2, 64), np.float32)
    rc[:127, 0] = np.concatenate([np.cos(angc), np.cos(angc)], axis=1)
    rc[:127, 1] = np.concatenate([-np.sin(angc), np.sin(angc)], axis=1)
    c["rope_c"] = rc
    t = np.arange(S_LEN)
    cm = np.where((np.arange(128)[None, :] * 16 + 31 <= t[:, None]) & (np.arange(128)[None, :] < 127), 0.0, NEG_BIG)
    c["cmask"] = np.ascontiguousarray(cm.astype(np.float32).reshape(NT, 128, 128).transpose(1, 0, 2))
    tb = (t // 64)[:, None]
    jj = np.arange(32)[None, :]
    forced = (jj == 0) | (jj == tb) | (jj == tb - 1)
    sb_ = np.where(jj <= tb, np.where(forced, 1e4, 0.0), NEG).astype(np.float32)
    c["selbias"] = np.ascontiguousarray(sb_.reshape(NT, 128, 32).transpose(1, 0, 2))
    E = np.zeros((32, 16, 128), np.float32)
    for kc in range(16):
        for k in range(128):
            E[2 * kc + k // 64, kc, k] = 1.0
    c["e_nsa"] = E
    p = np.arange(128)[:, None, None]
    d = np.arange(4)[None, :, None]
    j = np.arange(512)[None, None, :]
    c["caus"] = np.where(p + 128 * d <= j, 0.0, NEG_BIG).astype(np.float32)
    e = (np.arange(8) - 3)[None, :, None]
    dl = j - p + 128 * e
    c["wmask"] = np.where((dl >= 0) & (dl < 512), 0.0, NEG_BIG).astype(np.float32)
    blk = np.arange(8)[None, :]
    c["pastbias"] = np.ascontiguousarray(
        np.where(blk < (t // 256)[:, None], 0.0, NEG).astype(np.float32).reshape(NT, 128, 8).transpose(1, 0, 2))
    c["ownsel"] = np.ascontiguousarray(
        (blk == (t // 256)[:, None]).astype(np.float32).reshape(NT, 128, 8).transpose(1, 0, 2))
    ef = np.zeros((32, S_LEN), np.float32)
    ef[np.arange(S_LEN) // 64, np.arange(S_LEN)] = 1.0
    c["e_full"] = ef
    e8 = np.zeros((4, 32, S_LEN), np.float32)
    for j_ in range(4):
        e8[j_, 8 * j_ + np.arange(S_LEN) // 256, np.arange(S_LEN)] = 1.0
    c["e8"] = e8
    E2 = np.zeros((64, 64, 128), np.float32)
    for i in range(64):
        E2[i, i, :] = 1.0
    c["e_moba"] = E2
    return c


CONST_SHAPES = {
    "rope_cs": (128, 16, 2, 64), "rope_c": (128, 2, 64), "cmask": (128, 16, 128), "selbias": (128, 16, 32),
    "caus": (128, 4, 512), "wmask": (128, 8, 512), "pastbias": (128, 16, 8),
    "ownsel": (128, 16, 8), "e_full": (32, 2048), "e8": (4, 32, 2048),
}

WEIGHT_SHAPES = {
    "ffn1_norm": (2, 1024), "ffn1_wg": (2, 1024, 2816), "ffn1_wu": (2, 1024, 2816), "ffn1_wd": (2, 2816, 1024),
    "mix_norm": (2, 1024), "w_in": (2, 1024, 4888),
    "cmpk_pos": (2, 32, 64), "cmpk_w1": (2, 2048, 128), "cmpk_w2": (2, 128, 64),
    "cmpv_pos": (2, 32, 64), "cmpv_w1": (2, 2048, 128), "cmpv_w2": (2, 128, 64),
    "w_branch_nsa": (2, 512, 1024), "w_branch_moba": (2, 512, 1024), "w_out": (2, 1024, 1024),
    "ffn2_norm": (2, 1024), "ffn2_wg": (2, 1024, 2816), "ffn2_wu": (2, 1024, 2816), "ffn2_wd": (2, 2816, 1024),
    "final_norm": (1, 1024),
}


def build_program(n_layers=2, stages=("ffn1", "mix", "ffn2"), dbg=()):
    nc = bass.Bass("TRN2", target_bir_lowering=False)
    D = {}
    for name, shp in WEIGHT_SHAPES.items():
        D[name] = nc.dram_tensor(name, list(shp), F32, kind="ExternalInput").ap()
    for name, shp in CONST_SHAPES.items():
        D[name] = nc.dram_tensor(name, list(shp), F32, kind="ExternalInput").ap()
    x_d = nc.dram_tensor("x", [S_LEN, DM], F32, kind="ExternalInput").ap()
    out_d = nc.dram_tensor("out", [S_LEN, DM], F32, kind="ExternalOutput").ap()
    DBG = {}

    with ExitStack() as top:
        S = Sched(nc, top)
        cnt = [0]

        def sbuf(st, shape, dt, name=None):
            cnt[0] += 1
            return st.enter_context(nc.sbuf_tensor("%s_%d" % (name or "t", cnt[0]), list(shape), dt))

        def psum(st, shape, dt, name=None):
            cnt[0] += 1
            full = 512 if dt == F32 else 1024
            t = st.enter_context(nc.psum_tensor("%s_%d" % (name or "p", cnt[0]), [128, full], dt))
            shape = list(shape)
            n = 1
            for d_ in shape[1:]:
                n *= d_
            v = t[0:shape[0], 0:n]
            if len(shape) == 3:
                v = v.rearrange("p (a b) -> p a b", a=shape[1])
            elif len(shape) == 4:
                v = v.rearrange("p (a b c) -> p a b c", a=shape[1], b=shape[2])
            return v

        x_sb = sbuf(top, [128, NT, DM], F32, "x")
        xB = [Buf("x%d" % t) for t in range(NT)]
        ident = sbuf(top, [128, 128], BF16, "ident")
        identf = sbuf(top, [128, 128], F32, "identf")
        identB = Buf("ident")
        rope = sbuf(top, [128, NT, 2, 64], F32, "rope")
        ropeB = Buf("rope")

        for t in range(NT):
            S.dma(S.sp if t % 2 == 0 else S.pool, x_sb[:, t, :], x_d[t * 128:(t + 1) * 128, :], writes=[xB[t]])
        S.dma(S.sp, rope[:], D["rope_cs"], writes=[ropeB])
        S.op(S.pool, lambda e: e.memset(identf[:], 1.0), writes=[identB])
        S.op(S.pool, lambda e: e.affine_select(identf[:], identf[:], [[-1, 128]], ALU.is_equal, 0.0,
                                               base=0, channel_multiplier=1), reads=[identB], writes=[identB])
        S.op(S.dve, lambda e: e.tensor_copy(ident[:], identf[:]), reads=[identB], writes=[identB])

        def dbg_out(name, ap, reads):
            if name not in dbg:
                return
            shp = list(ap.shape)
            dt_ = ap.dtype
            d = nc.dram_tensor("dbg_" + name, shp, dt_, kind="ExternalOutput").ap()
            DBG[name] = d
            S.dma(S.sp, d, ap, reads=reads)

        def norm_T(gain_row, hT, hTB):
            with ExitStack() as st:
                gbc = sbuf(st, [128, DM], F32, "gbc")
                ss = sbuf(st, [128, NT], F32, "ss")
                sd = sbuf(st, [128, NT], F32, "sd")
                rstd = sbuf(st, [128, NT], F32, "rstd")
                junk = sbuf(st, [128, DM], BF16, "junk")
                hn = [sbuf(st, [128, DM], BF16, "hn") for _ in range(2)]
                tp = [psum(st, [128, 8, 128], BF16, "tp") for _ in range(2)]
                gB, ss0B, junkB = Buf(), Buf(), Buf()
                ssB = [Buf() for _ in range(NT)]
                hnB = [Buf(), Buf()]
                tpB = [PBuf(), PBuf()]
                S.dma(S.sp, gbc[:], gain_row.partition_broadcast(128), writes=[gB])
                S.op(S.dve, lambda e: e.memset(ss[:], 0.0), writes=ssB)

                def stats(t):
                    S.op(S.act, lambda e: e.activation(junk[:], x_sb[:, t, :], AF.Square, accum_out=ss[:, t:t + 1]),
                         reads=[xB[t]], writes=[junkB, ssB[t]])
                    S.op(S.act, lambda e: e.activation(sd[:, t:t + 1], ss[:, t:t + 1], AF.Sqrt, bias=EPS_AP[:, 0:1], scale=1.0 / DM),
                         reads=[ssB[t], epsB], writes=[ssB[t]])
                    S.op(S.dve, lambda e: e.reciprocal(rstd[:, t:t + 1], sd[:, t:t + 1]), reads=[ssB[t]], writes=[ssB[t]])
                stats(0)
                stats(1)
                for t in range(NT):
                    i = t % 2
                    if t + 2 < NT:
                        stats(t + 2)
                    S.op(S.dve, lambda e: e.scalar_tensor_tensor(hn[i][:], x_sb[:, t, :], rstd[:, t:t + 1], gbc[:],
                                                                 ALU.mult, ALU.mult),
                         reads=[xB[t], ssB[t], gB], writes=[hnB[i]])
                    for c in range(8):
                        S.op(S.pe, lambda e: e.transpose(tp[i][:, c, :], hn[i][:, c * 128:(c + 1) * 128], ident[:]),
                             reads=[hnB[i], identB], writes=[tpB[i]])
                    S.op(S.act, lambda e: e.activation(hT[:, :, t * 128:(t + 1) * 128], tp[i][:], AF.Copy),
                         reads=[tpB[i]], writes=[hTB[t]])
                S.barrier()

        def make_tile_norm(st, gain_row, shared=None):
            gbc = sbuf(st, [128, DM], F32, "gbc")
            hn = [sbuf(st, [128, DM], BF16, "hn") for _ in range(2)]
            hTt = [sbuf(st, [128, 8, 128], BF16, "hTt") for _ in range(2)]
            tph = psum(st, [128, 8, 128], BF16, "tph")
            gB, tphB = Buf(), PBuf()
            hnB = [Buf(), Buf()]
            hTtB = [Buf(), Buf()]
            S.dma(S.sp, gbc[:], gain_row.partition_broadcast(128), writes=[gB])
            have = shared is not None and shared.get("done")
            if shared is None:
                shared = {}
            if not have:
                if "rstd" not in shared:
                    shared["rstd"] = sbuf(st, [128, NT], F32, "rstd")
                    shared["B"] = [Buf() for _ in range(NT)]
                ss = sbuf(st, [128, NT], F32, "ss")
                sd = sbuf(st, [128, NT], F32, "sd")
                junk = sbuf(st, [128, DM], BF16, "junk")
                junkB = Buf()
                S.op(S.dve, lambda e: e.memset(ss[:], 0.0), writes=shared["B"])
            rstd = shared["rstd"]
            ssB = shared["B"]
            started = set()

            def stats(t):
                g4 = t // 4
                if have or g4 in started or t >= NT:
                    return
                started.add(g4)
                tl_ = list(range(4 * g4, 4 * g4 + 4))
                for tt in tl_:
                    S.op(S.act, lambda e: e.activation(junk[:], x_sb[:, tt, :], AF.Square, accum_out=ss[:, tt:tt + 1]),
                         reads=[xB[tt]], writes=[junkB, ssB[tt]])
                gsl = slice(4 * g4, 4 * g4 + 4)
                S.op(S.act, lambda e: e.activation(sd[:, gsl], ss[:, gsl], AF.Sqrt, bias=EPS_AP[:, 0:1], scale=1.0 / DM),
                     reads=[ssB[tt] for tt in tl_] + [epsB], writes=[ssB[tt] for tt in tl_])
                S.op(S.dve, lambda e: e.reciprocal(rstd[:, gsl], sd[:, gsl]), reads=[ssB[tt] for tt in tl_], writes=[ssB[tt] for tt in tl_])
            stats(0)
            shared["done"] = True

            def tile(t):
                i = t % 2
                stats(t)
                stats(t + 3)
                S.op(S.dve, lambda e: e.scalar_tensor_tensor(hn[i][:], x_sb[:, t, :], rstd[:, t:t + 1], gbc[:],
                                                             ALU.mult, ALU.mult),
                     reads=[xB[t], ssB[t], gB], writes=[hnB[i]])
                for c in range(8):
                    S.op(S.pe, lambda e: e.transpose(tph[:, c, :], hn[i][:, c * 128:(c + 1) * 128], ident[:]),
                         reads=[hnB[i], identB], writes=[tphB])
                S.op(S.act, lambda e: e.activation(hTt[i][:], tph[:], AF.Copy), reads=[tphB], writes=[hTtB[i]])
                return hTt[i], hTtB[i]
            return tile

        def ffn(l, gain, wg, wu, wd, passes=((0, 11), (11, 22))):
            with ExitStack() as st:
                hT = sbuf(st, [128, 8, S_LEN], BF16, "hT")
                hTB = [Buf() for _ in range(NT)]
                wg_v = wg[l].rearrange("(c p) f -> p c f", p=128)
                wu_v = wu[l].rearrange("(c p) f -> p c f", p=128)
                wd_v = wd[l].rearrange("(c p) d -> p c d", p=128)
                wgs = [sbuf(st, [128, 8, 256], BF16, "wg") for _ in range(2)]
                wus = [sbuf(st, [128, 8, 256], BF16, "wu") for _ in range(2)]
                wgB = [Buf(), Buf()]
                wuB = [Buf(), Buf()]
                sg = [sbuf(st, [128, 512], F32, "sg") for _ in range(2)]
                sgB = [Buf(), Buf()]
                pg = [psum(st, [128, 512], F32, "pg") for _ in range(2)]
                pu = [psum(st, [128, 512], F32, "pu") for _ in range(2)]
                pd = [psum(st, [128, 512], F32, "pd") for _ in range(2)]
                pgB = [PBuf(), PBuf()]
                puB = [PBuf(), PBuf()]
                pdB = [PBuf(), PBuf()]
                npmax = max(b - a for a, b in passes)
                actT = sbuf(st, [128, npmax, S_LEN], BF16, "actT")
                wds = sbuf(st, [128, npmax, DM], BF16, "wd")
                actB = [[Buf() for _ in range(4)] for _ in range(npmax)]
                wdB = [Buf() for _ in range(npmax)]
                cnt_ = {"gi": 0, "ei": 0, "di": 0}

                def load_group(f0, fi, gsz):
                    b = cnt_["gi"] % 2
                    cnt_["gi"] += 1
                    c0 = (f0 + fi) * 128
                    S.dma(S.pool, wgs[b][:, :, 0:gsz * 128], wg_v[:, :, c0:c0 + gsz * 128], writes=[wgB[b]])
                    S.dma(S.pool, wus[b][:, :, 0:gsz * 128], wu_v[:, :, c0:c0 + gsz * 128], writes=[wuB[b]])
                    return b

                def gate_up(b, fidx, fo, tb):
                    k = cnt_["ei"] % 2
                    cnt_["ei"] += 1
                    hr = [hTB[4 * tb + q] for q in range(4)]
                    S.mm(pg[k][:], [(wgs[b][:, c, fo * 128:(fo + 1) * 128], hT[:, c, tb * 512:(tb + 1) * 512])
                                    for c in range(8)], reads=[wgB[b]] + hr, writes=[pgB[k]])
                    S.mm(pu[k][:], [(wus[b][:, c, fo * 128:(fo + 1) * 128], hT[:, c, tb * 512:(tb + 1) * 512])
                                    for c in range(8)], reads=[wuB[b]] + hr, writes=[puB[k]])
                    S.op(S.act, lambda e: e.activation(sg[k][:], pg[k][:], AF.Silu), reads=[pgB[k]], writes=[sgB[k]])
                    S.op(S.dve, lambda e: e.tensor_tensor(actT[:, fidx, tb * 512:(tb + 1) * 512], sg[k][:], pu[k][:], ALU.mult),
                         reads=[sgB[k], puB[k]], writes=[actB[fidx][tb]])

                f0_, f1_ = passes[0]
                g0sz = min(2, f1_ - f0_)
                b0 = load_group(f0_, 0, g0sz)
                gbc = sbuf(st, [128, DM], F32, "gbc")
                ss = sbuf(st, [128, NT], F32, "ss")
                sd = sbuf(st, [128, NT], F32, "sd")
                rstd = sbuf(st, [128, NT], F32, "rstd")
                junk = sbuf(st, [128, DM], BF16, "junk")
                hn = [sbuf(st, [128, DM], BF16, "hn") for _ in range(2)]
                tp = [psum(st, [128, 8, 128], BF16, "tp") for _ in range(2)]
                gB, junkB = Buf(), Buf()
                ssB = [Buf() for _ in range(NT)]
                hnB = [Buf(), Buf()]
                tpB = [PBuf(), PBuf()]
                S.dma(S.sp, gbc[:], gain[l:l + 1, :].partition_broadcast(128), writes=[gB])
                S.op(S.dve, lambda e: e.memset(ss[:], 0.0), writes=ssB)

                def stats4(g4):
                    tl_ = list(range(4 * g4, 4 * g4 + 4))
                    for tt in tl_:
                        S.op(S.act, lambda e: e.activation(junk[:], x_sb[:, tt, :], AF.Square, accum_out=ss[:, tt:tt + 1]),
                             reads=[xB[tt]], writes=[junkB, ssB[tt]])
                    gsl = slice(4 * g4, 4 * g4 + 4)
                    S.op(S.act, lambda e: e.activation(sd[:, gsl], ss[:, gsl], AF.Sqrt, bias=EPS_AP[:, 0:1], scale=1.0 / DM),
                         reads=[ssB[tt] for tt in tl_] + [epsB], writes=[ssB[tt] for tt in tl_])
                    S.op(S.dve, lambda e: e.reciprocal(rstd[:, gsl], sd[:, gsl]), reads=[ssB[tt] for tt in tl_], writes=[ssB[tt] for tt in tl_])
                stats4(0)
                for t in range(NT):
                    i = t % 2
                    if t % 4 == 0 and t + 4 < NT:
                        stats4(t // 4 + 1)
                    S.op(S.dve, lambda e: e.scalar_tensor_tensor(hn[i][:], x_sb[:, t, :], rstd[:, t:t + 1], gbc[:],
                                                                 ALU.mult, ALU.mult),
                         reads=[xB[t], ssB[t], gB], writes=[hnB[i]])
                    for c in range(8):
                        S.op(S.pe, lambda e: e.transpose(tp[i][:, c, :], hn[i][:, c * 128:(c + 1) * 128], ident[:]),
                             reads=[hnB[i], identB], writes=[tpB[i]])
                    S.op(S.act, lambda e: e.activation(hT[:, :, t * 128:(t + 1) * 128], tp[i][:], AF.Copy),
                         reads=[tpB[i]], writes=[hTB[t]])
                    if t % 4 == 3 and t >= 7:
                        for fo in range(g0sz):
                            gate_up(b0, fo, fo, t // 4 - 1)
                for fo in range(g0sz):
                    gate_up(b0, fo, fo, 3)

                first = True
                for (f0, f1) in passes:
                    nf = f1 - f0
                    wd_issued = False
                    fi = 0
                    while fi < nf:
                        gsz = min(2, nf - fi)
                        if first:
                            b = b0
                        else:
                            b = load_group(f0, fi, gsz)
                        if not wd_issued:
                            wd_issued = True
                            for fj in range(nf):
                                S.dma(S.pool, wds[:, fj, :], wd_v[:, f0 + fj, :], writes=[wdB[fj]])
                        if not first:
                            for fo in range(gsz):
                                for tb in range(4):
                                    gate_up(b, fi + fo, fo, tb)
                        first = False
                        fi += gsz
                    for t in range(NT):
                        for hf in range(2):
                            k = cnt_["di"] % 2
                            cnt_["di"] += 1
                            S.mm(pd[k][:], [(actT[:, fi, t * 128:(t + 1) * 128], wds[:, fi, hf * 512:(hf + 1) * 512])
                                            for fi in range(nf)],
                                 reads=[actB[fi][t // 4] for fi in range(nf)] + wdB[0:nf], writes=[pdB[k]])
                            S.op(S.dve, lambda e: e.scalar_tensor_tensor(
                                x_sb[:, t, hf * 512:(hf + 1) * 512], pd[k][:], 0.5,
                                x_sb[:, t, hf * 512:(hf + 1) * 512], ALU.mult, ALU.add),
                                reads=[pdB[k], xB[t]], writes=[xB[t]])
                S.barrier()

        EPS_AP = sbuf(top, [128, 1], F32, "eps")
        epsB = Buf("eps")
        S.op(S.dve, lambda e: e.memset(EPS_AP[:], EPS), writes=[epsB])

        from_mixer = {}
        for l in range(n_layers):
            if "ffn1" in stages:
                ffn(l, D["ffn1_norm"], D["ffn1_wg"], D["ffn1_wu"], D["ffn1_wd"])
            if "mix" in stages:
                mixer(nc, S, D, l, x_sb, xB, ident, identf, identB, rope, ropeB, make_tile_norm, sbuf, psum, dbg_out, EPS_AP, epsB)
            if "ffn2" in stages:
                ffn(l, D["ffn2_norm"], D["ffn2_wg"], D["ffn2_wu"], D["ffn2_wd"])

        S.budget = None
        with ExitStack() as st:
            gbc = sbuf(st, [128, DM], F32, "gbcf")
            ss = sbuf(st, [128, NT], F32, "ssf")
            sd = sbuf(st, [128, NT], F32, "sdf")
            rstd = sbuf(st, [128, NT], F32, "rstdf")
            junk = sbuf(st, [128, DM], BF16, "junkf")
            yo = [sbuf(st, [128, DM], F32, "yo") for _ in range(2)]
            gB, ssB, junkB = Buf(), Buf(), Buf()
            yB = [Buf(), Buf()]
            S.dma(S.sp, gbc[:], D["final_norm"][0:1, :].partition_broadcast(128), writes=[gB])
            S.op(S.dve, lambda e: e.memset(ss[:], 0.0), writes=[ssB])
            for t in range(NT):
                S.op(S.act, lambda e: e.activation(junk[:], x_sb[:, t, :], AF.Square, accum_out=ss[:, t:t + 1]),
                     reads=[xB[t]], writes=[junkB, ssB])
            S.op(S.act, lambda e: e.activation(sd[:], ss[:], AF.Sqrt, bias=EPS_AP[:, 0:1], scale=1.0 / DM),
                 reads=[ssB, epsB], writes=[ssB])
            S.op(S.dve, lambda e: e.reciprocal(rstd[:], sd[:]), reads=[ssB], writes=[ssB])
            for t in range(NT):
                i = t % 2
                S.op(S.dve, lambda e: e.scalar_tensor_tensor(yo[i][:], x_sb[:, t, :], rstd[:, t:t + 1], gbc[:],
                                                             ALU.mult, ALU.mult),
                     reads=[xB[t], ssB, gB], writes=[yB[i]])
                S.dma(S.sp, out_d[t * 128:(t + 1) * 128, :], yo[i][:], reads=[yB[i]])
            S.finish()
        print("program: n_inst=%d n_wait=%d" % (S.n_inst, S.n_wait))
    return nc, DBG


def rope_apply(S, st_tmp, src, H, cs, dst, reads, writes, dup=1):
    k = st_tmp["i"][0] % 2
    st_tmp["i"][0] += 1
    t1, t2 = st_tmp["t"][k]
    tB = st_tmp["B"][k]
    np_ = src.shape[0]
    a1 = t1[0:np_, 0:H, :]
    a2 = t2[0:np_, 0:H, :]
    S.op(S.dve, lambda e: e.tensor_tensor(a1, src, bcast(cs[0], 1, H), ALU.mult), reads=reads, writes=[tB[0]])
    S.op(S.dve, lambda e: e.tensor_tensor(a2[:, :, 0:32], src[:, :, 32:64], bcast(cs[1][:, 0:32], 1, H), ALU.mult),
         reads=reads, writes=[tB[1]])
    S.op(S.dve, lambda e: e.tensor_tensor(a2[:, :, 32:64], src[:, :, 0:32], bcast(cs[1][:, 32:64], 1, H), ALU.mult),
         reads=reads, writes=[tB[1]])
    if dup > 1:
        b1, b2 = bcast(a1, 2, dup), bcast(a2, 2, dup)
    else:
        b1, b2 = a1, a2
    S.op(S.dve, lambda e: e.tensor_tensor(dst, b1, b2, ALU.add), reads=[tB[0], tB[1]], writes=writes)


STOP = [None]


def run_pipeline(items, la=2, filler=None, per_item=0, drain=True, epi_delay=2):
    n = len(items)
    pending = []
    for i in range(n + la):
        if i < n:
            items[i][0]()
        j = i - la
        if j >= 0:
            items[j][1]()
            while pending and pending[0][0] <= j:
                pending.pop(0)[1]()
            items[j][2]()
            if items[j][3] is not None:
                items[j][3]()
                pending.append((j + epi_delay, items[j][4]))
            if filler is not None:
                for _ in range(per_item):
                    if next(filler, "done") == "done":
                        filler = None
                        break
    for _, fn in pending:
        fn()
    if filler is not None and drain:
        for _ in filler:
            pass


def mixer(nc, S, D, l, x_sb, xB, ident, identf, identB, rope, ropeB, make_tile_norm, sbuf, psum, dbg_out, EPS_AP, epsB):
    win_v = D["w_in"][l].rearrange("(c p) n -> p c n", p=128)
    with ExitStack() as ms:
        nsaT = sbuf(ms, [128, 4, S_LEN], BF16, "nsaT")
        nsaTB = [Buf() for _ in range(NT)]
        mobaTB = [Buf() for _ in range(NT)]
        nstats = {"rstd": sbuf(ms, [128, NT], F32, "rstd_mix"), "B": [Buf() for _ in range(NT)]}

        def make_rt(st_):
            return {"t": [(sbuf(st_, [128, 8, 64], F32, "rt1"), sbuf(st_, [128, 8, 64], F32, "rt2")) for _ in range(2)],
                    "B": [(Buf(), Buf()) for _ in range(2)], "i": [0]}

        with ExitStack() as ns:
            qaT = sbuf(ns, [128, 8, S_LEN], BF16, "qaT")
            kskwT = sbuf(ns, [128, 4, S_LEN], BF16, "kskwT")
            vsw = sbuf(ns, [128, NT, 4, 65], BF16, "vsw")
            ga = sbuf(ns, [128, NT, 24], F32, "ga")
            kcbT = sbuf(ns, [128, 2, 128], BF16, "kcbT")
            vcb = sbuf(ns, [128, 2, 64], BF16, "vcb")
            kcbTB, vcbB = Buf(), Buf()
            selRB = [[Buf() for _ in range(NT)] for _ in range(2)]
            eB_ = Buf()
            S.op(S.pool, lambda e: e.memset(qaT[64:96, :, :], 0.0), writes=[b_ for g_ in selRB for b_ in g_])
            S.op(S.pool, lambda e: e.memset(kskwT[64:96, 2:4, :], 0.0), writes=[eB_])
            S.op(S.pool, lambda e: e.memset(kcbT[:], 0.0), writes=[kcbTB])
            for g_ in range(2):
                S.dma(S.pool, kskwT[64:96, g_, :], D["e_full"], writes=[eB_])
            ks_scope = ns.enter_context(ExitStack())
            kcvcT = sbuf(ks_scope, [128, 2, S_LEN], BF16, "kcvcT")
            qaB = [Buf() for _ in range(NT)]
            kkB = [Buf() for _ in range(NT)]
            kcB = [Buf() for _ in range(NT)]
            vswB = [Buf() for _ in range(NT)]
            gaB = [Buf() for _ in range(NT)]
            S.op(S.pool, lambda e: e.memset(vsw[:], 1.0), writes=vswB)
            with ExitStack() as st:
                rt = make_rt(st)
                tnorm = make_tile_norm(st, D["mix_norm"][l:l + 1, :], nstats)
                wn = sbuf(st, [128, 8, 1304], BF16, "wn")
                wnB = [Buf() for _ in range(3)]
                segs = [(0, 0, 512, 0), (512, 536, 664, 1), (640, 664, 792, 1), (768, 792, 920, 1), (896, 1048, 1176, 1),
                        (1024, 920, 1048, 2), (1152, 1176, 1304, 2), (1280, 512, 536, 2)]
                for (dst, a, b, wb) in segs:
                    S.dma(S.pool, wn[:, :, dst:dst + (b - a)], win_v[:, :, a:b], reads=[], writes=[wnB[wb]])
                pA = [psum(st, [128, 512], F32, "pA") for _ in range(2)]
                pBk = [psum(st, [128, 512], F32, "pB")] * 2
                pC = psum(st, [128, 512], F32, "pC")
                tq = psum(st, [128, 8, 128], BF16, "tq")
                tkk = psum(st, [128, 6, 128], BF16, "tkk")
                tk = tkk[:, 0:4, :]
                tkv = tkk[:, 4:6, :]
                pAB = [PBuf(), PBuf()]
                pBB = [PBuf()] * 2
                pCB, tqB, tkB = PBuf(), PBuf(), PBuf()
                tkvB = tkB
                qr = [sbuf(st, [128, 8, 128], BF16, "qr") for _ in range(2)]
                kr = [sbuf(st, [128, 4, 128], BF16, "kr") for _ in range(2)]
                kv = [sbuf(st, [128, 256], BF16, "kv") for _ in range(2)]
                qrB = [Buf(), Buf()]
                krB = [Buf(), Buf()]
                kvB = [Buf(), Buf()]
                for i_ in range(2):
                    S.op(S.pool, lambda e: e.memset(qr[i_][:], 0.0), writes=[qrB[i_]])
                    S.op(S.pool, lambda e: e.memset(kr[i_][:], 0.0), writes=[krB[i_]])
                hts = {}

                def s_norm(t):
                    hts[t] = tnorm(t)

                def s_pa(t):
                    hTt, hTtB = hts[t]
                    S.mm(pA[t % 2][:], [(hTt[:, c, :], wn[:, c, 0:512]) for c in range(8)], reads=[hTtB, wnB[0]], writes=[pAB[t % 2]])

                def s_pbc(t):
                    hTt, hTtB = hts[t]
                    S.mm(pBk[0][:], [(hTt[:, c, :], wn[:, c, 512:1024]) for c in range(8)], reads=[hTtB, wnB[1]], writes=[pBB[0]])
                    S.mm(pC[:, 0:280], [(hTt[:, c, :], wn[:, c, 1024:1304]) for c in range(8)], reads=[hTtB, wnB[2]], writes=[pCB])

                def s_post(t):
                    i = t % 2
                    tsl = slice(t * 128, (t + 1) * 128)
                    cs = (rope[:, t, 0, :], rope[:, t, 1, :])
                    rope_apply(S, rt, pA[i][:].rearrange("p (h d) -> p h d", h=8), 8, cs, qr[i][:, :, 0:64],
                               reads=[pAB[i], ropeB], writes=[qrB[i]])
                    S.op(S.act, lambda e: e.activation(kv[i][:], pBk[0][:, 0:256], AF.Copy), reads=[pBB[0]], writes=[kvB[i]])
                    rope_apply(S, rt, pBk[0][:, 256:512].rearrange("p (h d) -> p h d", h=4), 4, cs,
                               kr[i][:, :, 0:64], reads=[pBB[0], ropeB], writes=[krB[i]])
                    S.op(S.act, lambda e: e.activation(vsw[:, t, :, 0:64], pC[:, 0:256].rearrange("p (h d) -> p h d", h=4), AF.Copy),
                         reads=[pCB], writes=[vswB[t]])
                    S.op(S.act, lambda e: e.activation(ga[:, t, :], pC[:, 256:280], AF.Copy), reads=[pCB], writes=[gaB[t]])
                    for c in range(8):
                        S.op(S.pe, lambda e: e.transpose(tq[:, c, :], qr[i][:, c, :], ident[:]),
                             reads=[qrB[i], identB], writes=[tqB])
                    S.op(S.act, lambda e: e.activation(qaT[0:64, :, tsl], tq[0:64, :, :], AF.Copy), reads=[tqB], writes=[qaB[t]])
                    for c in range(2):
                        S.op(S.pe, lambda e: e.transpose(tkv[:, c, :], kv[i][:, c * 128:(c + 1) * 128], ident[:]),
                             reads=[kvB[i], identB], writes=[tkvB])
                    S.op(S.act, lambda e: e.activation(kcvcT[:, :, tsl], tkv, AF.Copy), reads=[tkvB], writes=[kcB[t]])
                    for c in range(4):
                        S.op(S.pe, lambda e: e.transpose(tk[:, c, :], kr[i][:, c, :], ident[:]),
                             reads=[krB[i], identB], writes=[tkB])
                    S.op(S.act, lambda e: e.activation(kskwT[0:64, :, tsl], tk[0:64, :, :], AF.Copy), reads=[tkB], writes=[kkB[t]])

                print('SBUF remaining in NSA proj:', nc.sbuf_bytes_remaining)
                s_norm(0)
                s_pa(0)
                s_pbc(0)
                s_norm(1)
                for t in range(NT):
                    if t + 2 < NT:
                        s_norm(t + 2)
                    if t + 1 < NT:
                        s_pa(t + 1)
                    s_post(t)
                    if t + 1 < NT:
                        s_pbc(t + 1)
                S.op(S.act, lambda e: e.activation(ga[:], ga[:], AF.Sigmoid), reads=gaB, writes=gaB)
                S.barrier()
            if STOP[0] == "nsaproj":
                return
            dbg_out("qaT", qaT[:], qaB)
            dbg_out("kskwT", kskwT[:], kkB)
            dbg_out("ga", ga[:], gaB)

            with ExitStack() as st:
                rt = make_rt(st)
                w1 = [sbuf(st, [128, 32, 128], BF16, "w1") for _ in range(2)]
                posT = [sbuf(st, [128, 32], BF16, "posT") for _ in range(2)]
                posj = [sbuf(st, [32, 64], BF16, "posj") for _ in range(2)]
                w2 = [sbuf(st, [128, 64], BF16, "w2") for _ in range(2)]
                ropec = sbuf(st, [128, 2, 64], F32, "ropec")
                wB = Buf()
                for kvi, nm in enumerate(("cmpk", "cmpv")):
                    w1v = D[nm + "_w1"][l].rearrange("(j d) h -> d j h", d=64)
                    for hlf in range(2):
                        S.dma(S.pool, w1[kvi][hlf * 64:(hlf + 1) * 64, :, :], w1v, writes=[wB])
                    S.dma(S.pool, posj[kvi][:], D[nm + "_pos"][l], writes=[wB])
                    S.dma(S.pool, w2[kvi][:], D[nm + "_w2"][l], writes=[wB])
                S.dma(S.sp, ropec[:], D["rope_c"], writes=[wB])
                ph = [psum(st, [128, 128], F32, "ph") for _ in range(2)]
                pb = psum(st, [128, 8], F32, "pb")
                pk = psum(st, [128, 64], F32, "pk")
                tpk = psum(st, [128, 128], BF16, "tpk")
                phB = [PBuf(), PBuf()]
                pbB, pkB, tpkB = PBuf(), PBuf(), PBuf()
                bias = sbuf(st, [128, 2], F32, "bias")
                biasB = Buf()
                hb = sbuf(st, [128, 128], F32, "hb")
                h2 = sbuf(st, [128, 128], F32, "h2")
                uu = sbuf(st, [128, 128], F32, "uu")
                gl = sbuf(st, [128, 128], BF16, "gl")
                kcr = sbuf(st, [128, 2, 64], BF16, "kcr")
                hbB, h2B, uuB, glB, kcrB = Buf(), Buf(), Buf(), Buf(), Buf()
                posTB = Buf()
                for kvi in range(2):
                    S.op(S.pe, lambda e: e.transpose(tpk[0:64, 0:32], posj[kvi][:], ident[0:32, 0:32]), reads=[wB, identB], writes=[tpkB])
                    S.op(S.act, lambda e: e.activation(posT[kvi][0:64, :], tpk[0:64, 0:32], AF.Copy), reads=[tpkB], writes=[posTB])
                for kvi in range(2):
                    S.mm(pb[:, kvi:kvi + 1], [(w1[kvi][0:64, j, :], posT[kvi][0:64, j:j + 1]) for j in range(32)],
                         reads=[wB, posTB], writes=[pbB])
                S.op(S.act, lambda e: e.activation(bias[:], pb[:, 0:2], AF.Copy), reads=[pbB], writes=[biasB])
                it = 0
                for kvi in range(2):
                    for g in range(2):
                        k = it % 2
                        it += 1
                        pairs = []
                        for j in range(32):
                            base = kcvcT[g * 64:(g + 1) * 64, kvi, j:j + 1]
                            rhs = bass.AP(base.tensor, base.offset, [list(base.ap[0]), [16, 127]])
                            pairs.append((w1[kvi][g * 64:(g + 1) * 64, j, :], rhs))
                        S.mm(ph[k][:, 0:127], pairs, reads=[wB] + kcB, writes=[phB[k]])
                        hv, h2v, uv = hb[:, 0:127], h2[:, 0:127], uu[:, 0:127]
                        S.op(S.act, lambda e: e.activation(hv, ph[k][:, 0:127], AF.Identity, bias=bias[:, kvi:kvi + 1]),
                             reads=[phB[k], biasB], writes=[hbB])
                        S.op(S.dve, lambda e: e.tensor_tensor(h2v, hv, hv, ALU.mult), reads=[hbB], writes=[h2B])
                        S.op(S.dve, lambda e: e.tensor_scalar(h2v, h2v, 0.044715, 1.0, ALU.mult, ALU.add), reads=[h2B], writes=[h2B])
                        S.op(S.dve, lambda e: e.tensor_tensor(uv, h2v, hv, ALU.mult), reads=[h2B, hbB], writes=[uuB])
                        S.op(S.act, lambda e: e.activation(uv, uv, AF.Exp, scale=-1.5957691216057308), reads=[uuB], writes=[uuB])
                        S.op(S.dve, lambda e: e.tensor_scalar(uv, uv, 1.0, None, ALU.add), reads=[uuB], writes=[uuB])
                        S.op(S.dve, lambda e: e.reciprocal(uv, uv), reads=[uuB], writes=[uuB])
                        S.op(S.dve, lambda e: e.tensor_tensor(gl[:, 0:127], hv, uv, ALU.mult), reads=[uuB, hbB], writes=[glB])
                        S.mm(pk[0:127, :], [(gl[:, 0:127], w2[kvi][:])], reads=[glB, wB], writes=[pkB])
                        if kvi == 0:
                            rope_apply(S, rt, pk[0:127, :].rearrange("p (h d) -> p h d", h=1), 1,
                                       (ropec[0:127, 0, :], ropec[0:127, 1, :]),
                                       kcr[0:127, :, :].rearrange("p (h u) d -> p h u d", h=1),
                                       reads=[pkB, wB], writes=[kcrB], dup=2)
                            S.op(S.pe, lambda e: e.transpose(tpk[:, 0:127], kcr[0:127, :, :].rearrange("p a d -> p (a d)"), ident[0:127, 0:127]),
                                 reads=[kcrB, identB], writes=[tpkB])
                            S.op(S.act, lambda e: e.activation(kcbT[0:64, g, 0:127], tpk[0:64, 0:127], AF.Copy), reads=[tpkB], writes=[kcbTB])
                        else:
                            S.op(S.act, lambda e: e.activation(vcb[0:127, g, :], pk[0:127, :], AF.Copy), reads=[pkB], writes=[vcbB])
                S.barrier()
            ks_scope.close()
            if STOP[0] == "compress":
                return
            dbg_out("kcbT", kcbT[:], [kcbTB])
            dbg_out("vcb", vcb[:], [vcbB])

            with ExitStack() as st:
                cmask = sbuf(st, [128, NT, 128], F32, "cmask")
                selbias = sbuf(st, [128, NT, 32], F32, "selbias")
                wmask = sbuf(st, [128, 8, 512], BF16, "wmask")
                cB = Buf()
                caus = sbuf(st, [128, 4, 512], BF16, "caus")
                causB = Buf()
                S.dma(S.pool, caus[:], D["caus"], writes=[causB])
                S.dma(S.sp, cmask[:], D["cmask"], writes=[cB])
                S.dma(S.sp, selbias[:], D["selbias"], writes=[cB])
                S.dma(S.pool, wmask[:], D["wmask"], writes=[cB])
                sc = psum(st, [128, 4, 128], F32, "sc")
                tpb = psum(st, [128, 5, 128], BF16, "tpb")
                ocotp = psum(st, [128, 512], F32, "ocotp")
                oc = ocotp[:, 0:256].rearrange("p (a b) -> p a b", a=4)
                otp = ocotp[:, 0:260].rearrange("p (a b) -> p a b", a=4)
                stp = [psum(st, [128, 512], F32, "stp") for _ in range(3)]
                oT = [psum(st, [128, 512], F32, "oT") for _ in range(2)]
                scB, tpbB, ocB = PBuf(), PBuf(), PBuf()
                otpB = ocB
                tpsB = tpbB
                stB = [PBuf(), PBuf(), PBuf()]
                oTB = [PBuf(), PBuf()]
                TL = []
                for g_ in range(2):
                    T = {}
                    T["sm"] = sbuf(st, [128, 4, 127], F32, "sm")
                    T["pb16"] = sbuf(st, [128, 4, 128], BF16, "pb16")
                    T["pT"] = sbuf(st, [128, 4, 128], BF16, "pT")
                    T["pp"] = sbuf(st, [128, 132], F32, "pp")
                    T["st8"] = sbuf(st, [128, 8], F32, "st8")
                    T["imp"] = sbuf(st, [128, 32], F32, "imp")
                    T["imp3"] = sbuf(st, [128, 32], F32, "imp3")
                    T["m8"] = sbuf(st, [128, 16], F32, "m8")
                    T["selb"] = sbuf(st, [128, 96], BF16, "selb")
                    T["etmp2"] = sbuf(st, [128, 4, 64], F32, "etmp2")
                    T["B"] = [Buf() for _ in range(9)]
                    TL.append(T)
                nacc2 = [sbuf(st, [128, 4, 8, 64], F32, "nacc") for _ in range(2)]
                nb16 = [sbuf(st, [128, 512], BF16, "nb16") for _ in range(2)]
                PT = [sbuf(st, [128, 512], BF16, "PT") for _ in range(3)]
                oTs = sbuf(st, [65, 512], F32, "oTs")
                ew = sbuf(st, [128, 8], F32, "ew")
                etmp = sbuf(st, [128, 4, 64], F32, "etmp")
                naccB2 = [[Buf() for _ in range(4)] for _ in range(2)]
                oTsB, ewB, etmpB = Buf(), Buf(), Buf()
                nb16B = [Buf(), Buf()]
                PTB = [Buf() for _ in range(3)]

                def writeout_gen(qb):
                    nacc = nacc2[qb % 2]
                    naccB = naccB2[qb % 2]
                    for tl in range(4):
                        t = 4 * qb + tl
                        i = tl % 2
                        S.op(S.dve, lambda e: e.tensor_copy(nb16[i][:], nacc[:, tl, :, :].rearrange("p h d -> p (h d)")),
                             reads=[naccB[tl]], writes=[nb16B[i]])
                        yield
                        for c in range(4):
                            S.op(S.pe, lambda e: e.transpose(tpb[:, c, :], nb16[i][:, c * 128:(c + 1) * 128], ident[:]),
                                 reads=[nb16B[i], identB], writes=[tpbB])
                        S.op(S.act, lambda e: e.activation(nsaT[:, :, t * 128:(t + 1) * 128], tpb[:, 0:4, :], AF.Copy),
                             reads=[tpbB], writes=[nsaTB[t]])
                        yield
                for T in TL:
                    S.op(S.dve, lambda e: e.memset(T["pp"][:], 0.0), writes=[T["B"][3]])
                    S.op(S.dve, lambda e: e.memset(T["pb16"][:], 0.0), writes=[T["B"][1]])
                    S.op(S.dve, lambda e: e.memset(T["selb"][:], 0.0), writes=[T["B"][7]])
                sti = [0]
                pti = [0]
                oti = [0]

                def attn_items(items, qb, h, kT_idx, v_idx, chunks, gate_col, acc_tile, accB, qT, qTB, kT, kTB, vt, vtB, mask_fn, extra_reads):
                    hs = slice(0, 96)
                    pair = h
                    ob = oti[0] % 2
                    oti[0] += 1
                    n = len(chunks)
                    for ci, (kc, j0, j1) in enumerate(chunks):
                        sb_ = sti[0] % 3
                        sti[0] += 1
                        pb_ = pti[0] % 3
                        pti[0] += 1
                        qsl = slice(qb * 512 + j0, qb * 512 + j1)

                        def qk(kc=kc, j0=j0, j1=j1, sb_=sb_, qsl=qsl):
                            pairs = [(kT[hs, kT_idx, kc * 128:(kc + 1) * 128], qT[hs, pair, qsl])] + mask_fn(kc, j0, j1)
                            S.mm(stp[sb_][:, j0:j1], pairs,
                                 reads=[kTB[kc]] + [qTB[4 * qb + q] for q in range(j0 // 128, (j1 + 127) // 128)] + extra_reads,
                                 writes=[stB[sb_]])

                        def ex(j0=j0, j1=j1, sb_=sb_, pb_=pb_):
                            S.op(S.act, lambda e: e.activation(PT[pb_][:, j0:j1], stp[sb_][:, j0:j1], AF.Exp, scale=0.125),
                                 reads=[stB[sb_]], writes=[PTB[pb_]])

                        def pv(kc=kc, j0=j0, j1=j1, pb_=pb_, ci=ci):
                            S.mm1(oT[ob][0:65, j0:j1], vt[:, kc, v_idx, 0:65], PT[pb_][:, j0:j1], start=(ci == 0), stop=(ci == n - 1),
                                  reads=[PTB[pb_], vtB[kc]], writes=[oTB[ob]])
                        epi_a = epi_b = None
                        if ci == n - 1:
                            def epi_a():
                                S.op(S.dve, lambda e: e.tensor_copy(oTs[:], oT[ob][0:65, :]), reads=[oTB[ob]], writes=[oTsB])

                            def epi_b():
                                for tl in range(4):
                                    S.op(S.pe, lambda e: e.transpose(otp[:, tl, :], oTs[:, tl * 128:(tl + 1) * 128], identf[0:65, 0:65]),
                                         reads=[oTsB, identB], writes=[otpB])
                                S.op(S.dve, lambda e: e.tensor_scalar(ew[:, 0:4], otp[:, :, 64], 1e-30, None, ALU.max), reads=[otpB], writes=[ewB])
                                S.op(S.dve, lambda e: e.reciprocal(ew[:, 0:4], ew[:, 0:4]), reads=[ewB], writes=[ewB])
                                S.op(S.dve, lambda e: e.tensor_tensor(ew[:, 0:4], ew[:, 0:4], ga[:, 4 * qb:4 * qb + 4, gate_col], ALU.mult),
                                     reads=[ewB] + gaB[4 * qb:4 * qb + 4], writes=[ewB])
                                S.op(S.dve, lambda e: e.tensor_tensor(etmp[:], otp[:, :, 0:64], bcast(ew[:, 0:4], 2, 64), ALU.mult),
                                     reads=[otpB, ewB], writes=[etmpB])
                                S.op(S.pool, lambda e: e.tensor_tensor(acc_tile[:, :, h, :], acc_tile[:, :, h, :], etmp[:], ALU.add),
                                     reads=[etmpB] + accB, writes=accB)
                        items.append((qk, ex, pv, epi_a, epi_b))

                if STOP[0] and STOP[0].startswith("budget:"):
                    S.budget = int(STOP[0].split(":")[1])
                def part_a_gen(qb):
                    nacc = nacc2[qb % 2]
                    naccB = naccB2[qb % 2]
                    S.op(S.pool, lambda e: e.memset(nacc[:], 0.0), writes=naccB)
                    yield
                    gens = [part_a_chain(qb, 0), part_a_chain(qb, 1)]
                    while gens:
                        for gn in list(gens):
                            if next(gn, "done") == "done":
                                gens.remove(gn)
                            else:
                                yield

                def part_a_chain(qb, g):
                    nacc = nacc2[qb % 2]
                    naccB = naccB2[qb % 2]
                    T = TL[g]
                    sm, pb16, pT, pp, st8, imp, imp3, m8, selb, etmp2 = (T[k_] for k_ in ("sm", "pb16", "pT", "pp", "st8", "imp", "imp3", "m8", "selb", "etmp2"))
                    smB, pb16B, pTB, ppB, st8B, impB, m8B, selbB, etmp2B = T["B"]
                    ppw = bass.AP(pp[:, 0:1].tensor, pp[:, 0:1].offset, [list(pp[:, 0:1].ap[0]), [4, 32], [1, 5]])
                    if True:
                        for tl in range(4):
                            t = 4 * qb + tl
                            tsl = slice(t * 128, (t + 1) * 128)
                            for r in range(4):
                                h = 4 * g + r
                                S.mm(sc[:, r, 0:127], [(qaT[0:96, h, tsl], kcbT[0:96, g, 0:127])],
                                     reads=[qaB[t], kcbTB], writes=[scB])
                            S.op(S.dve, lambda e: e.tensor_tensor(sm[:], sc[:, :, 0:127], bcast(cmask[:, t, 0:127], 1, 4), ALU.add),
                                 reads=[scB, cB], writes=[smB])
                            yield
                            S.op(S.act, lambda e: e.activation(sm[:], sm[:], AF.Exp, scale=0.125), reads=[smB], writes=[smB])
                            yield
                            S.op(S.dve, lambda e: e.tensor_reduce(st8[:, 0:4], sm[:], AX.X, ALU.add), reads=[smB], writes=[st8B])
                            S.op(S.dve, lambda e: e.tensor_scalar(st8[:, 0:4], st8[:, 0:4], 1e-30, None, ALU.max), reads=[st8B], writes=[st8B])
                            yield
                            S.op(S.dve, lambda e: e.reciprocal(st8[:, 0:4], st8[:, 0:4]), reads=[st8B], writes=[st8B])
                            S.op(S.dve, lambda e: e.tensor_tensor(sm[:], sm[:], bcast(st8[:, 0:4], 2, 127), ALU.mult),
                                 reads=[smB, st8B], writes=[smB])
                            yield
                            S.op(S.dve, lambda e: e.tensor_copy(pb16[:, :, 0:127], sm[:]), reads=[smB], writes=[pb16B])
                            S.op(S.dve, lambda e: e.tensor_reduce(pp[:, 1:128], sm[:].rearrange("p r c -> p c r"), AX.X, ALU.add),
                                 reads=[smB], writes=[ppB])
                            yield
                            S.op(S.dve, lambda e: e.tensor_reduce(imp[:], ppw, AX.X, ALU.add), reads=[ppB], writes=[impB])
                            S.op(S.dve, lambda e: e.tensor_tensor(imp[:], imp[:], selbias[:, t, :], ALU.add), reads=[impB, cB], writes=[impB])
                            yield
                            S.op(S.dve, lambda e: e.max(m8[:, 0:8], imp[:]), reads=[impB], writes=[m8B])
                            S.op(S.dve, lambda e: e.match_replace(imp3[:], m8[:, 0:8], imp[:], -3.0e38), reads=[impB, m8B], writes=[m8B])
                            yield
                            S.op(S.dve, lambda e: e.max(m8[:, 8:16], imp3[:]), reads=[m8B], writes=[m8B])
                            S.op(S.dve, lambda e: e.tensor_scalar(imp3[:], imp[:], m8[:, 15:16], None, ALU.is_ge), reads=[impB, m8B], writes=[m8B])
                            yield
                            S.op(S.dve, lambda e: e.tensor_scalar(selb[:, 64:96], imp3[:], -NEG_BIG, NEG_BIG, ALU.mult, ALU.add),
                                 reads=[m8B], writes=[selbB])
                            yield
                            S.op(S.pe, lambda e: e.transpose(tpb[0:96, 4, :], selb[:], ident[:]), reads=[selbB, identB], writes=[tpsB])
                            S.op(S.dve, lambda e: e.tensor_copy(qaT[64:96, 4 * g:4 * g + 4, tsl], bcast(tpb[64:96, 4, :], 1, 4)),
                                 reads=[tpsB], writes=[selRB[g][t]])
                            yield
                            for r in range(4):
                                S.op(S.pe, lambda e: e.transpose(tpb[0:127, r, :], pb16[:, r, 0:127], ident[:]),
                                     reads=[pb16B, identB], writes=[tpbB])
                            S.op(S.dve, lambda e: e.tensor_copy(pT[0:127, :, :], tpb[0:127, 0:4, :]), reads=[tpbB], writes=[pTB])
                            yield
                            for r in range(4):
                                S.mm(oc[:, r, :], [(pT[0:127, r, :], vcb[0:127, g, :])], reads=[pTB, vcbB], writes=[ocB])
                            S.op(S.dve, lambda e: e.tensor_tensor(etmp2[:], oc[:], bcast(ga[:, t, 12 * g:12 * g + 12:3], 2, 64), ALU.mult),
                                 reads=[ocB, gaB[t]], writes=[etmp2B])
                            S.op(S.pool, lambda e: e.tensor_tensor(nacc[:, tl, 4 * g:4 * g + 4, :], nacc[:, tl, 4 * g:4 * g + 4, :], etmp2[:], ALU.add),
                                 reads=[etmp2B, naccB[tl]], writes=[naccB[tl]])
                            yield

                print('SBUF remaining in NSA attn:', nc.sbuf_bytes_remaining)
                gen_a = part_a_gen(0)
                next(gen_a)
                wo_prev = None
                for qb in range(4):
                    nacc = nacc2[qb % 2]
                    naccB = naccB2[qb % 2]
                    win_items = []
                    sel_items = []
                    for h in range(8):
                        g = h // 4
                        order = [1, 0, 2, 3, 4, -1, -2, -3]
                        chunks = []
                        for e_ in order:
                            kc = 4 * qb - e_
                            if kc < 0 or kc > 4 * qb + 3:
                                continue
                            if e_ >= 1:
                                j0, j1 = 0, min(512, 640 - 128 * e_)
                            else:
                                j0, j1 = -128 * e_, 512
                            chunks.append((kc, j0, j1))

                        def win_mask(kc, j0, j1, qb=qb):
                            return [(ident[:], wmask[:, 4 * qb - kc + 3, j0:j1])]
                        attn_items(win_items, qb, h, 2 + g, 2 + g, chunks, 3 * h + 2, nacc, naccB, qaT, qaB, kskwT, kkB, vsw, vswB, win_mask,
                                   [cB, identB])
                    for h in range(8):
                        g = h // 4
                        chunks = []
                        for kc in range(4 * qb + 4):
                            d = kc - 4 * qb
                            chunks.append((kc, 128 * d if d > 0 else 0, 512))

                        def sel_mask(kc, j0, j1, g=g, qb=qb, h=h):
                            m = []
                            d = kc - 4 * qb
                            if d >= 0:
                                m.append((ident[:], caus[:, d, j0:j1]))
                            return m
                        attn_items(sel_items, qb, h, g, g, chunks, 3 * h + 1, nacc, naccB, qaT, qaB, kskwT, kkB, vsw, vswB, sel_mask,
                                   selRB[g][4 * qb:4 * qb + 4] + [eB_, cB, causB, identB])
                    fill_ = gen_a if wo_prev is None else itertools.chain(wo_prev, gen_a)
                    run_pipeline(win_items, filler=fill_, per_item=(8 * 14) // len(win_items) + 1)
                    gen_a = part_a_gen(qb + 1) if qb < 3 else None
                    if gen_a is not None:
                        next(gen_a)
                    run_pipeline(sel_items, filler=gen_a, per_item=1, drain=False)
                    wo_prev = writeout_gen(qb)
                for _ in wo_prev:
                    pass
                S.barrier()
        dbg_out("nsaT", nsaT[:], nsaTB)
        if STOP[0] == "nsaattn":
            return

        mobaT = sbuf(ms, [128, 4, S_LEN], BF16, "mobaT")
        mw = ms.enter_context(ExitStack())
        pastb = sbuf(mw, [128, NT, 8], F32, "pastb")
        ownsel = sbuf(mw, [128, NT, 8], F32, "ownsel")
        cB2 = Buf()
        S.dma(S.sp, pastb[:], D["pastbias"], writes=[cB2])
        S.dma(S.sp, ownsel[:], D["ownsel"], writes=[cB2])
        wm_all = [sbuf(mw, [128, 8, 768], BF16, "wm") for _ in range(2)]
        wmB_all = [[Buf() for _ in range(3)] for _ in range(2)]
        for hp_ in range(2):
            for i in range(3):
                c0 = 1304 + 512 * i + 256 * hp_
                S.dma(S.pool, wm_all[hp_][:, :, i * 256:(i + 1) * 256], win_v[:, :, c0:c0 + 256], writes=[wmB_all[hp_][i]])
        for hp in range(2):
            with ExitStack() as mo:
                qbT = sbuf(mo, [128, 4, S_LEN], BF16, "qbT")
                kbT = sbuf(mo, [128, 4, S_LEN], BF16, "kbT")
                vb = sbuf(mo, [128, NT, 4, 65], BF16, "vb")
                qbB = [Buf() for _ in range(NT)]
                kbB = [Buf() for _ in range(NT)]
                vbB = [Buf() for _ in range(NT)]
                mselB = [Buf() for _ in range(NT)]
                e8B = Buf()
                S.op(S.pool, lambda e: e.memset(vb[:], 1.0), writes=vbB)
                S.op(S.pool, lambda e: e.memset(qbT[64:96, :, :], 0.0), writes=mselB)
                for j in range(4):
                    S.dma(S.pool, kbT[64:96, j, :], D["e8"][j], writes=[e8B])
                with ExitStack() as st:
                    rt = make_rt(st)
                    tnorm = make_tile_norm(st, D["mix_norm"][l:l + 1, :], nstats)
                    wm = wm_all[hp]
                    wmB = wmB_all[hp]
                    pA = [psum(st, [128, 512], F32, "pA") for _ in range(2)]
                    pC = [psum(st, [128, 256], F32, "pC") for _ in range(2)]
                    tqk = psum(st, [128, 8, 128], BF16, "tqk")
                    pAB = [PBuf(), PBuf()]
                    pCB = [PBuf(), PBuf()]
                    tqkB = PBuf()
                    qkr = [sbuf(st, [128, 8, 128], BF16, "qkr") for _ in range(2)]
                    qkrB = [Buf(), Buf()]
                    for i_ in range(2):
                        S.op(S.pool, lambda e: e.memset(qkr[i_][:], 0.0), writes=[qkrB[i_]])
                    hts = {}

                    def m_norm(t):
                        hts[t] = tnorm(t)

                    def m_pa(t):
                        hTt, hTtB = hts[t]
                        i = t % 2
                        S.mm(pA[i][:], [(hTt[:, c, :], wm[:, c, 0:512]) for c in range(8)], reads=[hTtB, wmB[0], wmB[1]], writes=[pAB[i]])

                    def m_pc(t):
                        hTt, hTtB = hts[t]
                        i = t % 2
                        S.mm(pC[i][:], [(hTt[:, c, :], wm[:, c, 512:768]) for c in range(8)], reads=[hTtB, wmB[2]], writes=[pCB[i]])

                    def m_post(t):
                        i = t % 2
                        tsl = slice(t * 128, (t + 1) * 128)
                        cs = (rope[:, t, 0, :], rope[:, t, 1, :])
                        rope_apply(S, rt, pA[i][:].rearrange("p (h d) -> p h d", h=8), 8, cs, qkr[i][:, :, 0:64],
                                   reads=[pAB[i], ropeB], writes=[qkrB[i]])
                        S.op(S.act, lambda e: e.activation(vb[:, t, :, 0:64], pC[i][:].rearrange("p (h d) -> p h d", h=4), AF.Copy),
                             reads=[pCB[i]], writes=[vbB[t]])
                        for c in range(8):
                            S.op(S.pe, lambda e: e.transpose(tqk[:, c, :], qkr[i][:, c, :], ident[:]),
                                 reads=[qkrB[i], identB], writes=[tqkB])
                        S.op(S.act, lambda e: e.activation(qbT[0:64, :, tsl], tqk[0:64, 0:4, :], AF.Copy), reads=[tqkB], writes=[qbB[t]])
                        S.op(S.act, lambda e: e.activation(kbT[0:64, :, tsl], tqk[0:64, 4:8, :], AF.Copy), reads=[tqkB], writes=[kbB[t]])

                    print('SBUF remaining in MoBA proj:', nc.sbuf_bytes_remaining)
                    m_norm(0)
                    m_pa(0)
                    m_pc(0)
                    m_norm(1)
                    for t in range(NT):
                        if t + 2 < NT:
                            m_norm(t + 2)
                        if t + 1 < NT:
                            m_pa(t + 1)
                            m_pc(t + 1)
                        m_post(t)
                    S.barrier()
                with ExitStack() as st:
                    kmf = sbuf(st, [128, 4, 8], F32, "kmf")
                    kmb = sbuf(st, [128, 4, 8], BF16, "kmb")
                    kmB = Buf()
                    S.op(S.dve, lambda e: e.tensor_reduce(kmf[0:64], kbT[0:64].rearrange("p c (b k) -> p c b k", b=8), AX.X, ALU.add),
                         reads=kbB, writes=[kmB])
                    S.op(S.dve, lambda e: e.tensor_scalar(kmb[0:64], kmf[0:64], 1.0 / 256.0, None, ALU.mult), reads=[kmB], writes=[kmB])
                    gs2 = sbuf(st, [128, 4, 8], F32, "gs2")
                    m8a = sbuf(st, [128, 4, 8], F32, "m8a")
                    thr = sbuf(st, [128, 4], F32, "thr")
                    sel = sbuf(st, [128, 4, 8], F32, "sel")
                    mb = sbuf(st, [128, 128], BF16, "mb")
                    gs2B, m8aB, thrB, selB_, mbB = Buf(), Buf(), Buf(), Buf(), Buf()
                    S.op(S.pool, lambda e: e.memset(mb[:], 0.0), writes=[mbB])

                    def gate_gen(tiles):
                        for t in tiles:
                            tsl = slice(t * 128, (t + 1) * 128)
                            for j in range(4):
                                S.mm(gp[:, j, :], [(qbT[0:64, j, tsl], kmb[0:64, j, :])], reads=[qbB[t], kmB], writes=[gpB])
                            yield
                            S.op(S.dve, lambda e: e.tensor_tensor(gs2[:], gp[:, 0:4, :], bcast(pastb[:, t, :], 1, 4), ALU.add),
                                 reads=[gpB, cB2], writes=[gs2B])
                            yield
                            for j in range(4):
                                S.op(S.dve, lambda e: e.max(m8a[:, j, :], gs2[:, j, :]), reads=[gs2B], writes=[m8aB])
                                if j % 2 == 1:
                                    yield
                            S.op(S.dve, lambda e: e.tensor_scalar(thr[:], m8a[:, :, 2], -1e29, None, ALU.max), reads=[m8aB], writes=[thrB])
                            S.op(S.dve, lambda e: e.tensor_tensor(sel[:], gs2[:], bcast(thr[:], 2, 8), ALU.is_ge), reads=[gs2B, thrB], writes=[selB_])
                            yield
                            S.op(S.dve, lambda e: e.tensor_tensor(sel[:], sel[:], bcast(ownsel[:, t, :], 1, 4), ALU.max), reads=[selB_, cB2], writes=[selB_])
                            S.op(S.dve, lambda e: e.tensor_scalar(mb[:, 64:96], sel[:].rearrange("p h b -> p (h b)"), -NEG_BIG, NEG_BIG, ALU.mult, ALU.add),
                                 reads=[selB_], writes=[mbB])
                            yield
                            S.op(S.pe, lambda e: e.transpose(tm, mb[:], ident[:]), reads=[mbB, identB], writes=[tmB])
                            S.op(S.dve, lambda e: e.tensor_copy(qbT[64:96, :, tsl], bcast(tm[64:96, :], 1, 4)), reads=[tmB], writes=[mselB[t]])
                            yield

                    caus = sbuf(st, [128, 4, 512], BF16, "caus")
                    causB = Buf()
                    S.dma(S.pool, caus[:], D["caus"], writes=[causB])
                    stp = [psum(st, [128, 512], F32, "stp") for _ in range(4)]
                    oT = [psum(st, [128, 512], F32, "oT") for _ in range(2)]
                    otpg = psum(st, [128, 512], F32, "otpg")
                    otp = otpg[:, 0:260].rearrange("p (a b) -> p a b", a=4)
                    gp = otpg[:, 448:512].rearrange("p (a b) -> p a b", a=8)
                    tpbm = psum(st, [128, 8, 128], BF16, "tpbm")
                    tpb = tpbm[:, 0:4, :]
                    tm = tpbm[:, 4, :]
                    stB = [PBuf() for _ in range(4)]
                    oTB = [PBuf(), PBuf()]
                    otpB, tpbB = PBuf(), PBuf()
                    gpB = otpB
                    tmB = tpbB
                    print('SBUF remaining in MoBA attn (before macc etc):', nc.sbuf_bytes_remaining)
                    for _ in gate_gen(range(0, 4)):
                        pass
                    macc2 = [sbuf(st, [128, 4, 4, 64], F32, "macc") for _ in range(2)]
                    maccB2 = [[Buf() for _ in range(4)] for _ in range(2)]
                    nb16 = [sbuf(st, [128, 256], BF16, "nb16") for _ in range(2)]
                    PT = [sbuf(st, [128, 512], BF16, "PT") for _ in range(4)]
                    oTs = sbuf(st, [65, 512], F32, "oTs")
                    ew = sbuf(st, [128, 8], F32, "ew")
                    oTsB, ewB = Buf(), Buf()
                    nb16B = [Buf(), Buf()]
                    PTB = [Buf() for _ in range(4)]
                    sti = 0
                    pti = 0
                    oti = 0

                    def m_writeout_gen(qb):
                        macc = macc2[qb % 2]
                        maccB = maccB2[qb % 2]
                        for tl in range(4):
                            t = 4 * qb + tl
                            i = tl % 2
                            S.op(S.dve, lambda e: e.tensor_copy(nb16[i][:], macc[:, tl, :, :].rearrange("p h d -> p (h d)")),
                                 reads=[maccB[tl]], writes=[nb16B[i]])
                            yield
                            for c in range(2):
                                S.op(S.pe, lambda e: e.transpose(tpb[:, c, :], nb16[i][:, c * 128:(c + 1) * 128], ident[:]),
                                     reads=[nb16B[i], identB], writes=[tpbB])
                            S.op(S.act, lambda e: e.activation(mobaT[:, 2 * hp:2 * hp + 2, t * 128:(t + 1) * 128], tpb[:, 0:2, :], AF.Copy),
                                 reads=[tpbB], writes=[mobaTB[t]])
                            yield
                    m_wo_prev = None
                    for qb in range(4):
                        macc = macc2[qb % 2]
                        maccB = maccB2[qb % 2]
                        items = []
                        for j in range(4):
                            ob = oti % 2
                            oti += 1
                            nch = 4 * qb + 4
                            for kc in range(nch):
                                d = kc - 4 * qb
                                j0 = 128 * d if d > 0 else 0
                                sb_ = sti % 4
                                sti += 1
                                pb_ = pti % 4
                                pti += 1
                                qsl = slice(qb * 512 + j0, (qb + 1) * 512)

                                def qk(kc=kc, d=d, j0=j0, sb_=sb_, qsl=qsl, j=j, qb=qb):
                                    pairs = [(kbT[0:96, j, kc * 128:(kc + 1) * 128], qbT[0:96, j, qsl])]
                                    if d >= 0:
                                        pairs.append((ident[:], caus[:, d, j0:512]))
                                    S.mm(stp[sb_][:, j0:512], pairs,
                                         reads=[kbB[kc], e8B, causB, identB] + [qbB[4 * qb + q] for q in range(j0 // 128, 4)]
                                         + [mselB[4 * qb + q] for q in range(j0 // 128, 4)], writes=[stB[sb_]])

                                def ex(j0=j0, sb_=sb_, pb_=pb_):
                                    S.op(S.act, lambda e: e.activation(PT[pb_][:, j0:512], stp[sb_][:, j0:512], AF.Exp, scale=0.125),
                                         reads=[stB[sb_]], writes=[PTB[pb_]])

                                def pv(kc=kc, j0=j0, pb_=pb_, j=j, ob=ob, nch=nch):
                                    S.mm1(oT[ob][0:65, j0:512], vb[:, kc, j, 0:65], PT[pb_][:, j0:512], start=(kc == 0), stop=(kc == nch - 1),
                                          reads=[PTB[pb_], vbB[kc]], writes=[oTB[ob]])
                                epi_a = epi_b = None
                                if kc == nch - 1:
                                    def epi_a(ob=ob):
                                        S.op(S.dve, lambda e: e.tensor_copy(oTs[:], oT[ob][0:65, :]), reads=[oTB[ob]], writes=[oTsB])

                                    def epi_b(j=j, macc=macc, maccB=maccB):
                                        for tl in range(4):
                                            S.op(S.pe, lambda e: e.transpose(otp[:, tl, :], oTs[:, tl * 128:(tl + 1) * 128], identf[0:65, 0:65]),
                                                 reads=[oTsB, identB], writes=[otpB])
                                        S.op(S.dve, lambda e: e.tensor_scalar(ew[:, 0:4], otp[:, :, 64], 1e-30, None, ALU.max), reads=[otpB], writes=[ewB])
                                        S.op(S.dve, lambda e: e.reciprocal(ew[:, 0:4], ew[:, 0:4]), reads=[ewB], writes=[ewB])
                                        S.op(S.dve, lambda e: e.tensor_tensor(macc[:, :, j, :], otp[:, :, 0:64], bcast(ew[:, 0:4], 2, 64), ALU.mult),
                                             reads=[otpB, ewB], writes=maccB)
                                items.append((qk, ex, pv, epi_a, epi_b))
                        fl = []
                        if m_wo_prev is not None:
                            fl.append(m_wo_prev)
                        if qb < 3:
                            fl.append(gate_gen(range(4 * qb + 4, 4 * qb + 8)))
                        if fl:
                            run_pipeline(items, la=3, filler=itertools.chain(*fl), per_item=(4 * 8 + 8) // len(items) + 1)
                        else:
                            run_pipeline(items, la=3)
                        m_wo_prev = m_writeout_gen(qb)
                    for _ in m_wo_prev:
                        pass
                    S.barrier()
        mw.close()
        dbg_out("mobaT", mobaT[:], mobaTB)
        if STOP[0] == "mobaattn":
            return

        with ExitStack() as st:
            wgt = sbuf(st, [128, 8, 2048], BF16, "wgt")
            wbn = sbuf(st, [128, 4, DM], BF16, "wbn")
            wbm = sbuf(st, [128, 4, DM], BF16, "wbm")
            wo = sbuf(st, [128, 8, DM], BF16, "wo")
            wB = [Buf() for _ in range(4)]
            for i in range(4):
                S.dma(S.pool, wgt[:, :, i * 512:(i + 1) * 512], win_v[:, :, 2840 + i * 512:2840 + (i + 1) * 512], writes=[wB[0]])
            S.dma(S.pool, wbn[:], D["w_branch_nsa"][l].rearrange("(c p) n -> p c n", p=128), writes=[wB[1]])
            S.dma(S.pool, wbm[:], D["w_branch_moba"][l].rearrange("(c p) n -> p c n", p=128), writes=[wB[2]])
            for i in range(2):
                S.dma(S.pool, wo[:, :, i * 512:(i + 1) * 512], D["w_out"][l].rearrange("(c p) n -> p c n", p=128)[:, :, i * 512:(i + 1) * 512],
                      writes=[wB[3]])
            tnorm = make_tile_norm(st, D["mix_norm"][l:l + 1, :], nstats)
            sa = sbuf(st, [128, 512], F32, "sa")
            sbb = sbuf(st, [128, 512], F32, "sbb")
            ma = sbuf(st, [128, 512], F32, "ma")
            mg = sbuf(st, [128, DM], BF16, "mg")
            mgT = sbuf(st, [128, 8, 128], BF16, "mgT")
            saB, sbB, maB, mgB, mgTB = (Buf() for _ in range(5))
            tpm = psum(st, [128, 8, 128], BF16, "tpm")
            pga = psum(st, [128, 512], F32, "pga")
            pya = psum(st, [128, 512], F32, "pya")
            pgb = psum(st, [128, 512], F32, "pgb")
            pyb = psum(st, [128, 512], F32, "pyb")
            po = [psum(st, [128, 512], F32, "po") for _ in range(2)]
            tpmB, pgaB, pyaB, pgbB, pybB = (PBuf() for _ in range(5))
            poB = [PBuf(), PBuf()]
            mg2 = sbuf(st, [128, DM], BF16, "mg2")
            mgs = [mg, mg2]
            mgBs = [mgB, Buf()]
            hts = {}
            pi = [0]

            def g_norm(t):
                hts[t] = tnorm(t)

            def g_gate(t):
                tsl = slice(t * 128, (t + 1) * 128)
                hTt, hTtB = hts[t]
                mgt, mgtB = mgs[t % 2], mgBs[t % 2]
                for hf in range(2):
                    cs_ = slice(hf * 512, (hf + 1) * 512)
                    S.mm(pga[:], [(hTt[:, c, :], wgt[:, c, hf * 512:(hf + 1) * 512]) for c in range(8)], reads=[hTtB, wB[0]], writes=[pgaB])
                    S.mm(pya[:], [(nsaT[:, c, tsl], wbn[:, c, cs_]) for c in range(4)], reads=[nsaTB[t], wB[1]], writes=[pyaB])
                    S.mm(pgb[:], [(hTt[:, c, :], wgt[:, c, 1024 + hf * 512:1024 + (hf + 1) * 512]) for c in range(8)], reads=[hTtB, wB[0]], writes=[pgbB])
                    S.mm(pyb[:], [(mobaT[:, c, tsl], wbm[:, c, cs_]) for c in range(4)], reads=[mobaTB[t], wB[2]], writes=[pybB])
                    S.op(S.act, lambda e: e.activation(sa[:], pga[:], AF.Sigmoid), reads=[pgaB], writes=[saB])
                    S.op(S.act, lambda e: e.activation(sbb[:], pgb[:], AF.Sigmoid), reads=[pgbB], writes=[sbB])
                    S.op(S.dve, lambda e: e.tensor_tensor(ma[:], sa[:], pya[:], ALU.mult), reads=[saB, pyaB], writes=[maB])
                    S.op(S.dve, lambda e: e.tensor_tensor(sbb[:], sbb[:], pyb[:], ALU.mult), reads=[sbB, pybB], writes=[sbB])
                    S.op(S.dve, lambda e: e.tensor_tensor(mgt[:, cs_], ma[:], sbb[:], ALU.add), reads=[maB, sbB], writes=[mgtB])

            def g_out(t):
                mgt, mgtB = mgs[t % 2], mgBs[t % 2]
                for c in range(8):
                    S.op(S.pe, lambda e: e.transpose(tpm[:, c, :], mgt[:, c * 128:(c + 1) * 128], ident[:]), reads=[mgtB, identB], writes=[tpmB])
                S.op(S.act, lambda e: e.activation(mgT[:], tpm[:], AF.Copy), reads=[tpmB], writes=[mgTB])
                for hf in range(2):
                    k = pi[0] % 2
                    pi[0] += 1
                    cs_ = slice(hf * 512, (hf + 1) * 512)
                    S.mm(po[k][:], [(mgT[:, c, :], wo[:, c, cs_]) for c in range(8)], reads=[mgTB, wB[3]], writes=[poB[k]])
                    S.op(S.dve, lambda e: e.tensor_tensor(x_sb[:, t, cs_], x_sb[:, t, cs_], po[k][:], ALU.add),
                         reads=[poB[k], xB[t]], writes=[xB[t]])

            print('SBUF remaining in merge:', nc.sbuf_bytes_remaining)
            g_norm(0)
            g_norm(1)
            g_gate(0)
            for t in range(NT):
                if t + 2 < NT:
                    g_norm(t + 2)
                if t + 1 < NT:
                    g_gate(t + 1)
                g_out(t)
            S.barrier()


_CACHE = {}


def kernel(**inputs):
    x = np.asarray(inputs["x"], dtype=np.float32)
    B = x.shape[0]
    consts = host_consts()
    shared = {}
    for name, shp in WEIGHT_SHAPES.items():
        shared[name] = np.ascontiguousarray(np.asarray(inputs[name], dtype=np.float32).reshape(shp))
    for name in CONST_SHAPES:
        shared[name] = consts[name]
    if "nc" not in _CACHE:
        _CACHE["nc"] = build_program()[0]
    nc = _CACHE["nc"]
    in_maps = []
    for b in range(B):
        m = dict(shared)
        m["x"] = np.ascontiguousarray(x[b])
        in_maps.append(m)
    res = run_bass_kernel_spmd(nc, in_maps, core_ids=list(range(B)))
    out = np.stack([np.asarray(res.results[b]["out"], dtype=np.float32) for b in range(B)], axis=0)
    return out
```
